# Optimizing a Trainium2 kernel written in Bass

```python
import math
import jax, jax.numpy as jnp
from jax import lax
import numpy as np

D_MODEL = 1024
BATCH = 32
SEQ = 2048
DEPTH = 2

HEAD_DIM = 64
A_WIDTH = 3 * D_MODEL // 8
HY_WIDTH = D_MODEL // 4
C_WIDTH = D_MODEL - A_WIDTH - HY_WIDTH
MIX_WIDTH = A_WIDTH + HY_WIDTH + C_WIDTH
A_HEADS = A_WIDTH // HEAD_DIM
DILATED_PATTERNS = ((128, 1), (512, 4), (2048, 16))
C_Q_HEADS = C_WIDTH // HEAD_DIM
C_GROUP = 3
C_KV_HEADS = C_Q_HEADS // C_GROUP
C_KV_WIDTH = C_KV_HEADS * HEAD_DIM
Q_BLOCK = 128
GRID_W = 64
ROPE_THETA = 10000.0
HY_ORDER = 2
HY_BANDS = 16
HY_EMB = 1 + 2 * HY_BANDS
HY_HIDDEN = 64
D_FF = 11 * D_MODEL // 4
EPS = 1e-6

A_Q0 = 0
A_K0 = A_Q0 + A_WIDTH
A_V0 = A_K0 + A_WIDTH
HY_0 = A_V0 + A_WIDTH
C_Q0 = HY_0 + (HY_ORDER + 1) * HY_WIDTH
C_K0 = C_Q0 + C_WIDTH
C_V0 = C_K0 + C_KV_WIDTH
PROJ_WIDTH = C_V0 + C_KV_WIDTH

kernel_name = "hybrid_dilated_hyena_axial_gqa_encoder"

F32 = jnp.float32


def rms_norm(x, g):
    xf = x.astype(F32)
    y = xf * lax.rsqrt(jnp.mean(xf * xf, axis=-1, keepdims=True) + EPS)
    return (y * g.astype(F32)).astype(x.dtype)


def rope_angles(pos, dim):
    freqs = ROPE_THETA ** (-jnp.arange(0, dim, 2, dtype=F32) / dim)
    ang = pos.astype(F32)[:, None] * freqs[None, :]
    return jnp.cos(ang), jnp.sin(ang)


def apply_rope(x, cos, sin):
    xf = x.astype(F32)
    half = x.shape[-1] // 2
    x1, x2 = xf[..., :half], xf[..., half:]
    c, s = cos[None, :, None, :], sin[None, :, None, :]
    return jnp.concatenate([x1 * c - x2 * s, x2 * c + x1 * s], axis=-1).astype(x.dtype)


def axial_rope(x, rows, cols):
    half = x.shape[-1] // 2
    cr, sr = rope_angles(rows, half)
    cc, sc = rope_angles(cols, half)
    return jnp.concatenate([apply_rope(x[..., :half], cr, sr), apply_rope(x[..., half:], cc, sc)], axis=-1)


def dwconv3(x, w, b):
    xp = jnp.pad(x, ((0, 0), (1, 1), (0, 0)))
    return xp[:, :-2] * w[0] + xp[:, 1:-1] * w[1] + xp[:, 2:] * w[2] + b


def dilated_branch(q, k, v, dilation, n_side):
    B, S, H, E = q.shape
    Ls = S // dilation
    blk = n_side
    nb = -(-Ls // blk)
    Lp = nb * blk

    def regroup(a):
        return a.reshape(B, Ls, dilation, H, E).transpose(0, 2, 3, 1, 4)

    qs = jnp.pad(regroup(q), ((0, 0), (0, 0), (0, 0), (0, Lp - Ls), (0, 0))).reshape(B, dilation, H, nb, blk, E)

    def windows(a):
        ap = jnp.pad(regroup(a), ((0, 0), (0, 0), (0, 0), (blk, Lp - Ls + blk), (0, 0)))
        ap = ap.reshape(B, dilation, H, nb + 2, blk, E)
        return jnp.concatenate([ap[:, :, :, :-2], ap[:, :, :, 1:-1], ap[:, :, :, 2:]], axis=4)

    kw, vw = windows(k), windows(v)
    qi = jnp.arange(nb)[:, None, None] * blk + jnp.arange(blk)[None, :, None]
    ki = jnp.arange(nb)[:, None, None] * blk - blk + jnp.arange(3 * blk)[None, None, :]
    valid = (jnp.abs(qi - ki) <= n_side) & (ki >= 0) & (ki < Ls)

    s = jnp.einsum('bdhnqe,bdhnke->bdhnqk', qs, kw).astype(F32) * (E ** -0.5)
    s = jnp.where(valid, s, -1e30)
    m = jnp.max(s, axis=-1, keepdims=True)
    p = jnp.exp(s - m)
    l = jnp.sum(p, axis=-1, keepdims=True)
    o = jnp.einsum('bdhnqk,bdhnke->bdhnqe', (p / l).astype(v.dtype), vw)
    lse = (m + jnp.log(l))[..., 0]

    o = o.reshape(B, dilation, H, Lp, E)[:, :, :, :Ls].transpose(0, 3, 1, 2, 4).reshape(B, S, H, E)
    lse = lse.reshape(B, dilation, H, Lp)[:, :, :, :Ls].transpose(0, 3, 1, 2).reshape(B, S, H)
    return o, lse


def dilated_attention(q, k, v):
    outs, lses = [], []
    for window, dilation in DILATED_PATTERNS:
        o, lse = dilated_branch(q, k, v, dilation, (window // 2) // dilation)
        outs.append(o)
        lses.append(lse)
    w = jax.nn.softmax(jnp.stack(lses, axis=0), axis=0)
    o = jnp.sum(w[..., None] * jnp.stack(outs, axis=0).astype(F32), axis=0)
    return o.astype(q.dtype)


def hyena_filters(L, w1, b1, freq, w2, b2, w3, decay):
    t = jnp.linspace(0.0, 1.0, L, dtype=F32)[:, None]
    bands = jnp.linspace(1e-4, HY_BANDS - 1, HY_BANDS, dtype=F32)
    ang = 2.0 * math.pi * bands[None, :] * jnp.arange(L, dtype=F32)[:, None] / L
    z = jnp.concatenate([t, jnp.cos(ang), -jnp.sin(ang)], axis=-1)
    freq = freq.astype(F32)
    h = jnp.sin(freq[0] * (z @ w1.astype(F32) + b1.astype(F32)))
    h = jnp.sin(freq[1] * (h @ w2.astype(F32) + b2.astype(F32)))
    h = (h @ w3.astype(F32)).reshape(L, HY_ORDER, 2, HY_WIDTH)
    h = h * jnp.exp(-t[:, :, None, None] * decay.astype(F32)[None])
    hf, hb = h[:, :, 0], h[:, :, 1]
    kc = jnp.concatenate([hf, jnp.zeros((1, HY_ORDER, HY_WIDTH), F32), hb[1:][::-1]], axis=0)
    kc = kc / jnp.sum(jnp.abs(kc), axis=0, keepdims=True)
    return jnp.fft.rfft(kc, axis=0)


def fftconv(u, kf, dbias):
    L = u.shape[1]
    uf = u.astype(F32)
    U = jnp.fft.rfft(uf, n=2 * L, axis=1)
    y = jnp.fft.irfft(U * kf[None], n=2 * L, axis=1)[:, :L]
    return (y + uf * dbias.astype(F32)).astype(u.dtype)


def hyena_mixer(p, conv_w, conv_b, w1, b1, freq, w2, b2, w3, decay, dbias):
    S = p.shape[1]
    u = dwconv3(p, conv_w, conv_b)
    v = u[..., :HY_WIDTH]
    x1 = u[..., HY_WIDTH:2 * HY_WIDTH]
    x2 = u[..., 2 * HY_WIDTH:]
    kf = hyena_filters(S, w1, b1, freq, w2, b2, w3, decay)
    z = x1 * fftconv(v, kf[:, 0], dbias[0])
    return x2 * fftconv(z, kf[:, 1], dbias[1])


def axial_gqa(q, k, v, g_q, g_k):
    B, S, _, E = q.shape
    q = rms_norm(q, g_q)
    k = rms_norm(k, g_k)
    ROWS = S // GRID_W
    rows = jnp.repeat(jnp.arange(ROWS), GRID_W)
    cols = jnp.tile(jnp.arange(GRID_W), ROWS)
    q = axial_rope(q, rows, cols)
    k = axial_rope(k, rows, cols)
    nq = S // Q_BLOCK
    qb = q.reshape(B, nq, Q_BLOCK, C_KV_HEADS, C_GROUP, E).transpose(1, 0, 2, 3, 4, 5)
    scale = E ** -0.5

    def attend(qblk):
        s = jnp.einsum('bqgre,bkge->bgrqk', qblk, k).astype(F32) * scale
        pr = jax.nn.softmax(s, axis=-1)
        return jnp.einsum('bgrqk,bkge->bqgre', pr.astype(v.dtype), v)

    o = lax.map(attend, qb)
    return o.transpose(1, 0, 2, 3, 4, 5).reshape(B, S, C_WIDTH)


def conv_geglu(h, w_gate, w_up, conv_w, conv_b, w_down):
    gate = dwconv3(h @ w_gate, conv_w, conv_b)
    return (jax.nn.gelu(gate, approximate=True) * (h @ w_up)) @ w_down


def setup_inputs(seed: int = 0) -> dict:
    key = jax.random.key(seed)
    ks = jax.random.split(key, 32)

    def nrm(k, shape, scale):
        return jax.random.normal(k, shape, F32) * scale

    def gain(k, shape):
        return 1.0 + 0.05 * jax.random.normal(k, shape, F32)

    L_ = DEPTH
    base_decay = jnp.abs(jnp.linspace(math.log(1e-2) / 1.5, math.log(1e-2) / 0.3, HY_WIDTH, dtype=F32))
    hy_decay = base_decay[None, None, None, :] * (1.0 + 0.05 * jax.random.normal(ks[17], (L_, HY_ORDER, 2, HY_WIDTH), F32))
    return {
        "x": jax.random.normal(ks[0], (BATCH, SEQ, D_MODEL), F32),
        "g_mix_pre": gain(ks[1], (L_, D_MODEL)),
        "g_mix_post": gain(ks[2], (L_, D_MODEL)),
        "g_ffn_pre": gain(ks[3], (L_, D_MODEL)),
        "g_ffn_post": gain(ks[4], (L_, D_MODEL)),
        "w_in": nrm(ks[5], (L_, D_MODEL, PROJ_WIDTH), D_MODEL ** -0.5),
        "w_out": nrm(ks[6], (L_, MIX_WIDTH, D_MODEL), MIX_WIDTH ** -0.5),
        "g_q": gain(ks[7], (L_, HEAD_DIM)),
        "g_k": gain(ks[8], (L_, HEAD_DIM)),
        "hy_conv_w": nrm(ks[9], (L_, 3, (HY_ORDER + 1) * HY_WIDTH), 3 ** -0.5),
        "hy_conv_b": nrm(ks[10], (L_, (HY_ORDER + 1) * HY_WIDTH), 0.02),
        "hy_w1": nrm(ks[11], (L_, HY_EMB, HY_HIDDEN), HY_EMB ** -0.5),
        "hy_b1": nrm(ks[12], (L_, HY_HIDDEN), 0.02),
        "hy_freq": gain(ks[13], (L_, 2, HY_HIDDEN)),
        "hy_w2": nrm(ks[14], (L_, HY_HIDDEN, HY_HIDDEN), HY_HIDDEN ** -0.5),
        "hy_b2": nrm(ks[15], (L_, HY_HIDDEN), 0.02),
        "hy_w3": nrm(ks[16], (L_, HY_HIDDEN, HY_ORDER * 2 * HY_WIDTH), HY_HIDDEN ** -0.5),
        "hy_decay": hy_decay,
        "hy_d": nrm(ks[18], (L_, HY_ORDER, HY_WIDTH), 0.1),
        "ffn_w_gate": nrm(ks[19], (L_, D_MODEL, D_FF), D_MODEL ** -0.5),
        "ffn_w_up": nrm(ks[20], (L_, D_MODEL, D_FF), D_MODEL ** -0.5),
        "ffn_conv_w": nrm(ks[21], (L_, 3, D_FF), 3 ** -0.5),
        "ffn_conv_b": nrm(ks[22], (L_, D_FF), 0.02),
        "ffn_w_down": nrm(ks[23], (L_, D_FF, D_MODEL), D_FF ** -0.5),
    }


def reference(x, g_mix_pre, g_mix_post, g_ffn_pre, g_ffn_post, w_in, w_out, g_q, g_k,
              hy_conv_w, hy_conv_b, hy_w1, hy_b1, hy_freq, hy_w2, hy_b2, hy_w3, hy_decay, hy_d,
              ffn_w_gate, ffn_w_up, ffn_conv_w, ffn_conv_b, ffn_w_down):
    B, S, _ = x.shape
    cos1, sin1 = rope_angles(jnp.arange(S), HEAD_DIM)
    for i in range(DEPTH):
        h = rms_norm(x, g_mix_pre[i])
        p = h @ w_in[i]
        qa = apply_rope(p[..., A_Q0:A_K0].reshape(B, S, A_HEADS, HEAD_DIM), cos1, sin1)
        ka = apply_rope(p[..., A_K0:A_V0].reshape(B, S, A_HEADS, HEAD_DIM), cos1, sin1)
        va = p[..., A_V0:HY_0].reshape(B, S, A_HEADS, HEAD_DIM)
        out_a = dilated_attention(qa, ka, va).reshape(B, S, A_WIDTH)
        out_b = hyena_mixer(p[..., HY_0:C_Q0], hy_conv_w[i], hy_conv_b[i], hy_w1[i], hy_b1[i],
                            hy_freq[i], hy_w2[i], hy_b2[i], hy_w3[i], hy_decay[i], hy_d[i])
        qc = p[..., C_Q0:C_K0].reshape(B, S, C_Q_HEADS, HEAD_DIM)
        kc = p[..., C_K0:C_V0].reshape(B, S, C_KV_HEADS, HEAD_DIM)
        vc = p[..., C_V0:PROJ_WIDTH].reshape(B, S, C_KV_HEADS, HEAD_DIM)
        out_c = axial_gqa(qc, kc, vc, g_q[i], g_k[i])
        mix = jnp.concatenate([out_a, out_b, out_c], axis=-1) @ w_out[i]
        x = x + rms_norm(mix, g_mix_post[i])
        h = rms_norm(x, g_ffn_pre[i])
        f = conv_geglu(h, ffn_w_gate[i], ffn_w_up[i], ffn_conv_w[i], ffn_conv_b[i], ffn_w_down[i])
        x = x + rms_norm(f, g_ffn_post[i])
    return x
```

```python
import math
from contextlib import ExitStack

import numpy as np
import ml_dtypes

import concourse.bass as bass
import concourse.mybir as mybir
from concourse.bass_utils import run_bass_kernel_spmd

F32 = mybir.dt.float32
BF16 = mybir.dt.bfloat16
AF = mybir.ActivationFunctionType
ALU = mybir.AluOpType
AX = mybir.AxisListType

S_ = 2048
D_ = 1024
NT = 16
PW = 2560
DFF = 2816
NF = 22
EPS = 1e-6
NCORES = 8
ATT_NDUM = 0

COMPUTE = ('pe', 'act', 'dve', 'pool')
ENGS = ('pe', 'act', 'dve', 'pool', 'sp')


class _Op:
    __slots__ = ('eng', 'fn', 'deps', 'need', 'val', 'dsem', 'is_dma', 'bar')

    def __init__(self, eng, fn):
        self.eng = eng
        self.fn = fn
        self.deps = []
        self.need = False
        self.val = None
        self.dsem = None
        self.is_dma = False
        self.bar = None


class Sched:
    def __init__(self, nc):
        self.nc = nc
        self.eops = {e: [] for e in ENGS}
        self.last_w = {}
        self.readers = {}
        self.dma_cnt = {}
        self.dma_last = {}
        self.n_ops = 0

    def _track(self, op, reads, writes):
        deps = {}
        for k in reads:
            w = self.last_w.get(k)
            if w is not None:
                deps[id(w)] = w
        for k in writes:
            w = self.last_w.get(k)
            if w is not None:
                deps[id(w)] = w
            for r in self.readers.get(k, {}).values():
                if isinstance(r, list):
                    for rr in r:
                        deps[id(rr)] = rr
                else:
                    deps[id(r)] = r
        for k in reads:
            d = self.readers.setdefault(k, {})
            if op.is_dma:
                d.setdefault('dma', []).append(op)
            else:
                d[op.eng] = op
        for k in writes:
            self.last_w[k] = op
            self.readers[k] = {}
        for d in deps.values():
            if d is op:
                continue
            if (not d.is_dma) and d.eng == 'pe' and op.eng == 'pe' and not op.is_dma:
                continue
            d.need = True
            op.deps.append((d, self.dma_cnt[d.dsem] if d.is_dma else None))

    def op(self, eng, fn, reads=(), writes=()):
        o = _Op(eng, fn)
        self._track(o, reads, writes)
        self.eops[eng].append(o)
        self.n_ops += 1
        return o

    def dma(self, out, in_, sem, reads=(), writes=(), eng='sp', **kw):
        fn = lambda e, o=out, i=in_: e.dma_start(out=o, in_=i, **kw)
        o = _Op(eng, fn)
        o.is_dma = True
        o.dsem = sem
        self.dma_cnt.setdefault(sem, 0)
        self._track(o, reads, writes)
        self.dma_cnt[sem] += 16
        o.val = self.dma_cnt[sem]
        o.need = True
        self.eops[eng].append(o)
        self.dma_last[sem] = o
        self.n_ops += 1
        return o

    def barrier(self):
        lasts = []
        for e in COMPUTE:
            for o in reversed(self.eops[e]):
                if not o.is_dma and o.bar is None:
                    o.need = True
                    lasts.append(o)
                    break
        dl = list(self.dma_last.values())
        for e in ENGS:
            b = _Op(e, None)
            b.bar = True
            b.deps = [(o, None) for o in lasts] + [(o, None) for o in dl]
            self.eops[e].append(b)
        self.last_w = {}
        self.readers = {}

    def emit(self):
        nc = self.nc
        for e in COMPUTE:
            c = 0
            for o in self.eops[e]:
                if o.is_dma or o.bar:
                    continue
                if o.need:
                    c += 1
                    o.val = c
        with ExitStack() as es:
            sems = {}
            for e in COMPUTE:
                sems[e] = es.enter_context(nc.semaphore("s_" + e))
            for k in self.dma_cnt:
                sems['d_' + k] = es.enter_context(nc.semaphore("d_" + k))
            block = es.enter_context(nc.Block())

            def semof(o):
                return sems['d_' + o.dsem] if o.is_dma else sems[o.eng]

            def run(engname, eh):
                waited = {}
                for o in self.eops[engname]:
                    need = {}
                    for d, ov in o.deps:
                        s = semof(d)
                        dv = ov if ov is not None else d.val
                        if dv > need.get(s.num, (0, None))[0]:
                            need[s.num] = (dv, s)
                    for key, (v, s) in need.items():
                        if waited.get(key, 0) >= v:
                            continue
                        eh.wait_ge(s, v)
                        waited[key] = v
                    if o.bar:
                        continue
                    ins = o.fn(eh)
                    if o.is_dma:
                        ins.then_inc(sems['d_' + o.dsem], 16)
                    elif o.need:
                        ins.then_inc(sems[o.eng], 1)

            @block.tensor
            def _(eh):
                run('pe', eh)

            @block.scalar
            def _(eh):
                run('act', eh)

            @block.vector
            def _(eh):
                run('dve', eh)

            @block.gpsimd
            def _(eh):
                run('pool', eh)

            @block.sync
            def _(eh):
                run('sp', eh)


def _tok_tiled(a):
    return np.ascontiguousarray(a.reshape(NT, 128, -1).transpose(1, 0, 2))


_CONST_CACHE = {}


def host_consts():
    if _CONST_CACHE:
        return _CONST_CACHE
    c = {}
    bf = ml_dtypes.bfloat16
    c['ident'] = np.eye(128, dtype=np.float32).astype(bf)
    pos = np.arange(S_, dtype=np.float32)
    fr = (10000.0 ** (-np.arange(0, 64, 2, dtype=np.float32) / 64)).astype(np.float32)
    ang = pos[:, None] * fr[None, :]
    co, si = np.cos(ang), np.sin(ang)
    c['ropeA_c'] = _tok_tiled(np.concatenate([co, co], 1).astype(np.float32))
    c['ropeA_s'] = _tok_tiled(np.concatenate([-si, si], 1).astype(np.float32))
    rows = (np.arange(S_) // 64).astype(np.float32)
    cols = (np.arange(S_) % 64).astype(np.float32)
    fr2 = (10000.0 ** (-np.arange(0, 32, 2, dtype=np.float32) / 32)).astype(np.float32)
    ar, ac = rows[:, None] * fr2[None, :], cols[:, None] * fr2[None, :]
    c['ropeC_c'] = _tok_tiled(np.concatenate([np.cos(ar), np.cos(ar), np.cos(ac), np.cos(ac)], 1).astype(np.float32))
    c['ropeC_s'] = _tok_tiled(np.concatenate([-np.sin(ar), np.sin(ar), -np.sin(ac), np.sin(ac)], 1).astype(np.float32))
    d = np.arange(128)[:, None] - np.arange(3968)[None, :] + 1920
    ad = np.abs(d)
    m = (ad <= 64).astype(np.float32) + ((d % 4 == 0) & (ad <= 256)) + ((d % 16 == 0) & (ad <= 1024))
    c['maskW'] = m.astype(bf)
    L = S_
    t = np.linspace(0.0, 1.0, L, dtype=np.float32)
    bands = np.linspace(1e-4, 15, 16, dtype=np.float32)
    angz = (2.0 * math.pi * bands[None, :] * np.arange(L, dtype=np.float32)[:, None] / L).astype(np.float32)
    z = np.concatenate([t[:, None], np.cos(angz), -np.sin(angz)], -1).astype(np.float32)
    c['zT'] = np.ascontiguousarray(z.T)
    c['tneg'] = np.ascontiguousarray((-t).reshape(NT, 128).T.astype(np.float32))
    N = 2 * L
    tt = np.arange(L, dtype=np.float64)[:, None]
    kk = (np.arange(L, dtype=np.float64)[None, :] + 0.5)
    w = 2.0 * math.pi * tt * kk / N
    Fw = np.concatenate([np.cos(w), -np.sin(w)], 1)
    c['fw'] = np.ascontiguousarray(Fw.reshape(NT, 128, 32, 128).transpose(2, 1, 0, 3)).astype(np.float32).astype(bf)
    c['inv'] = np.ascontiguousarray(((2.0 / N) * Fw).reshape(NT, 128, 32, 128).transpose(0, 3, 2, 1)).astype(np.float32).astype(bf)
    c['ones_bf'] = np.ones((128, 128), dtype=np.float32).astype(bf)
    c['ones_f'] = np.ones((128, 128), dtype=np.float32)
    _CONST_CACHE.update(c)
    return c


def build_program(NB=4, NL=2, stages="WPACFHOM", dbg=()):
    nc = bass.Bass("TRN2", target_bir_lowering=False)
    S = Sched(nc)
    _cnt = [0]

    def SB(name, shape, dt):
        _cnt[0] += 1
        return nc.sbuf_tensor("%s_u%d" % (name, _cnt[0]), shape, dt)

    def PS(name, shape, dt):
        _cnt[0] += 1
        return nc.psum_tensor("%s_u%d" % (name, _cnt[0]), shape, dt)

    def din(name, shape, dt=F32):
        return nc.dram_tensor(name, list(shape), dt, kind="ExternalInput").ap()

    def dscr(name, shape, dt=F32):
        kind = "ExternalOutput" if name in dbg else "Internal"
        return nc.dram_tensor(name, list(shape), dt, kind=kind).ap()

    x_in = din("x", [NB, S_, D_])
    w_in = din("w_in", [NL, D_, PW])
    w_out = din("w_out", [NL, D_, D_])
    w_gate = din("ffn_w_gate", [NL, D_, DFF])
    w_up = din("ffn_w_up", [NL, D_, DFF])
    w_down = din("ffn_w_down", [NL, DFF, D_])
    g_pre_col = din("g_mix_pre_col", [NL, 128, 8])
    g_fpre_col = din("g_ffn_pre_col", [NL, 128, 8])
    g_post = din("g_mix_post", [NL, D_])
    g_fpost = din("g_ffn_post", [NL, D_])
    g_qk = din("g_qk", [NL, 8 * 64])
    hy_cw = din("hy_cw", [NL, 128, 6, 3])
    hy_cb = din("hy_cb", [NL, 128, 6])
    hy_w1 = din("hy_w1", [NL, 33, 64])
    hy_b1 = din("hy_b1c", [NL, 64, 1])
    hy_fr = din("hy_freqc", [NL, 64, 2])
    hy_w2 = din("hy_w2", [NL, 64, 64])
    hy_b2 = din("hy_b2c", [NL, 64, 1])
    hy_w3 = din("hy_w3", [NL, 64, 1024])
    hy_dec = din("hy_decay", [NL, 1024])
    hy_d = din("hy_d", [NL, 512])
    f_cw = din("ffn_cw", [NL, 128, NF, 3])
    f_cb = din("ffn_cb", [NL, 128, NF])
    c_ident = din("ident", [128, 128], BF16)
    c_rAc = din("ropeA_c", [128, NT, 64])
    c_rAs = din("ropeA_s", [128, NT, 64])
    c_rCc = din("ropeC_c", [128, NT, 64])
    c_rCs = din("ropeC_s", [128, NT, 64])
    c_mask = din("maskW", [128, 3968], BF16)
    c_zT = din("zT", [33, S_])
    c_tneg = din("tneg", [128, NT])
    c_fw = din("fw", [32, 128, NT, 128], BF16)
    c_inv = din("inv", [NT, 128, 32, 128], BF16)
    c_ones_bf = din("ones_bf", [128, 128], BF16)
    c_ones_f = din("ones_f", [128, 128])

    out = nc.dram_tensor("out", [NB, S_, D_], F32, kind="ExternalOutput").ap()

    wbf_in = dscr("wbf_in", [NL, D_, PW], BF16)
    wbf_out = dscr("wbf_out", [NL, D_, D_], BF16)
    wbf_gate = dscr("wbf_gate", [NL, D_, DFF], BF16)
    wbf_up = dscr("wbf_up", [NL, D_, DFF], BF16)
    wbf_down = dscr("wbf_down", [NL, DFF, D_], BF16)
    qTA = dscr("qTA", [NB, 3, 128, S_], BF16)
    kTA = dscr("kTA", [NB, 3, 128, S_], BF16)
    vA = dscr("vA", [NB, 128, NT, 6 * 65], BF16)
    qTC = dscr("qTC", [NB, 3, 128, S_], BF16)
    kTC = dscr("kTC", [NB, 128, S_], BF16)
    vC = dscr("vC", [NB, 128, NT, 2 * 65], BF16)
    hyraw = dscr("hyraw", [NB, 6, 128, S_], BF16)
    oT = dscr("oT", [NB, D_, S_], BF16)
    x1tm = dscr("x1tm", [NT, 128, NB * 256], BF16)
    x2tm = dscr("x2tm", [NT, 128, NB * 256], BF16)
    kfd = dscr("kfd", [32, 128, 512], F32)
    xmid = dscr("xmid", [NB, S_, D_], F32)
    hTd = dscr("hTd", [NB, D_, S_], BF16)
    xl = [x_in] + [dscr("xl%d" % i, [NB, S_, D_], F32) for i in range(1, NL)] + [out]
    xl[NL] = out

    with ExitStack() as top:
        T = top.enter_context
        ident = T(SB("sb_ident", [128, 128], BF16))
        ones_bf = T(SB("sb_ones_bf", [128, 128], BF16))
        ones_f = T(SB("sb_ones_f", [128, 128], F32))
        S.dma(ident[:], c_ident[:, :], 'c0', writes=['ident'])
        S.dma(ones_bf[:], c_ones_bf[:, :], 'c0', writes=['ones_bf'])
        S.dma(ones_f[:], c_ones_f[:, :], 'c0', writes=['ones_f'])
        S.barrier()

        if 'W' in stages:
            for l in range(NL):
                for i, (dst, src, rows) in enumerate(((wbf_in, w_in, D_), (wbf_out, w_out, D_), (wbf_gate, w_gate, D_),
                                                      (wbf_up, w_up, D_), (wbf_down, w_down, DFF))):
                    nchunk = 8 if rows == D_ else 11
                    rc = rows // nchunk
                    for ch in range(nchunk):
                        S.dma(dst[l, ch * rc:(ch + 1) * rc, :], src[l, ch * rc:(ch + 1) * rc, :], 'wc%d' % (i % 2), eng='pool',
                              writes=[])
            S.barrier()

        def norm_transpose(ctx, xsrc_tile, gcol, hT_dst, key_hT, slot, pT, pfx, kpT=None):
            xt, sq, ssq, xn = ctx['xt'][slot], ctx['sq'], ctx['ssq'][slot], ctx['xn'][slot]
            kx, kn = pfx + 'xt%d' % slot, pfx + 'xn%d' % slot
            kpT = kpT or (pfx + 'pT')
            S.dma(xt[:], xsrc_tile, pfx + 'x%d' % slot, writes=[kx])
            S.op('pool', lambda e: e.memset(ssq[:], 0.0), writes=[pfx + 'ssq%d' % slot])
            S.op('act', lambda e: e.activation(sq[:], xt[:], AF.Square, accum_out=ssq[:]), reads=[kx], writes=[pfx + 'sq', pfx + 'ssq%d' % slot])
            S.op('dve', lambda e: e.tensor_scalar(ssq[:], ssq[:], 1.0 / D_, EPS, ALU.mult, ALU.add), reads=[pfx + 'ssq%d' % slot], writes=[pfx + 'ssq%d' % slot])
            S.op('act', lambda e: e.activation(ssq[:], ssq[:], AF.Sqrt), reads=[pfx + 'ssq%d' % slot], writes=[pfx + 'ssq%d' % slot])
            S.op('dve', lambda e: e.reciprocal(ssq[:], ssq[:]), reads=[pfx + 'ssq%d' % slot], writes=[pfx + 'ssq%d' % slot])
            S.op('act', lambda e: e.activation(xn[:], xt[:], AF.Copy, scale=ssq[:]), reads=[kx, pfx + 'ssq%d' % slot], writes=[kn])
            for c in range(8):
                S.op('pe', lambda e, c=c: e.transpose(pT[:, c, :], xn[:, c * 128:(c + 1) * 128], ident[:]), reads=[kn, 'ident'], writes=[kpT])
            S.op('dve', lambda e: e.tensor_tensor(hT_dst, pT[:, :, :], gcol[:, :].unsqueeze(2).broadcast_to([128, 8, 128]), ALU.mult),
                 reads=[kpT, 'gcol'], writes=[key_hT])

        def norm_partA(ctx, jobs, pfx):
            for (src, slot) in jobs:
                xt, ssq = ctx['xt'][slot], ctx['ssq'][slot]
                S.dma(xt[:], src, pfx + 'x%d' % slot, writes=[pfx + 'xt%d' % slot])
            for (src, slot) in jobs:
                xt, ssq, xn = ctx['xt'][slot], ctx['ssq'][slot], ctx['xn'][slot]
                S.op('act', lambda e, xt=xt, ssq=ssq, xn=xn: e.activation(xn[:], xt[:], AF.Square, accum_out=ssq[:]),
                     reads=[pfx + 'xt%d' % slot], writes=[pfx + 'xn%d' % slot, pfx + 'ssq%d' % slot])
            for (src, slot) in jobs:
                ssq = ctx['ssq'][slot]
                S.op('dve', lambda e, ssq=ssq: e.tensor_scalar(ssq[:], ssq[:], 1.0 / D_, EPS, ALU.mult, ALU.add), reads=[pfx + 'ssq%d' % slot], writes=[pfx + 'ssq%d' % slot])
            for (src, slot) in jobs:
                ssq = ctx['ssq'][slot]
                S.op('act', lambda e, ssq=ssq: e.activation(ssq[:], ssq[:], AF.Sqrt), reads=[pfx + 'ssq%d' % slot], writes=[pfx + 'ssq%d' % slot])
            for (src, slot) in jobs:
                ssq = ctx['ssq'][slot]
                S.op('dve', lambda e, ssq=ssq: e.reciprocal(ssq[:], ssq[:]), reads=[pfx + 'ssq%d' % slot], writes=[pfx + 'ssq%d' % slot])
            for (src, slot) in jobs:
                xt, ssq, xn = ctx['xt'][slot], ctx['ssq'][slot], ctx['xn'][slot]
                S.op('act', lambda e, xt=xt, ssq=ssq, xn=xn: e.activation(xn[:], xt[:], AF.Copy, scale=ssq[:]),
                     reads=[pfx + 'xt%d' % slot, pfx + 'ssq%d' % slot], writes=[pfx + 'xn%d' % slot])

        def norm_partB(ctx, slot, gcol, hT_dst, key_hT, pT, kpT, pfx):
            xn = ctx['xn'][slot]
            for c in range(8):
                S.op('pe', lambda e, c=c: e.transpose(pT[:, c, :], xn[:, c * 128:(c + 1) * 128], ident[:]), reads=[pfx + 'xn%d' % slot, 'ident'], writes=kpT)
            S.op('dve', lambda e: e.tensor_tensor(hT_dst, pT[:, :, :], gcol[:, :].unsqueeze(2).broadcast_to([128, 8, 128]), ALU.mult),
                 reads=list(kpT) + ['gcol'], writes=[key_hT])

        def post_load(ctx, xsrc_tile, slot, pfx):
            S.dma(ctx['xt'][slot][:], xsrc_tile, pfx + 'x%d' % slot, writes=[pfx + 'xt%d' % slot])

        def post_part1(ctx, ps, kps, xsrc_tile, slot, pfx):
            xt, ssq, yn = ctx['xt'][slot], ctx['ssq'][slot], ctx['yn'][slot]
            kx, ks, ky = pfx + 'xt%d' % slot, pfx + 'ssq%d' % slot, pfx + 'yn%d' % slot
            if xsrc_tile is not None:
                S.dma(xt[:], xsrc_tile, pfx + 'x%d' % slot, writes=[kx])
            S.op('act', lambda e: e.activation(yn[:], ps, AF.Square, accum_out=ssq[:]), reads=list(kps), writes=[ky, ks])
            S.op('dve', lambda e: e.tensor_scalar(ssq[:], ssq[:], 1.0 / D_, EPS, ALU.mult, ALU.add), reads=[ks], writes=[ks])
            S.op('act', lambda e: e.activation(ssq[:], ssq[:], AF.Sqrt), reads=[ks], writes=[ks])
            S.op('dve', lambda e: e.reciprocal(ssq[:], ssq[:]), reads=[ks], writes=[ks])

        def post_part2(ctx, ps, kps, gB, dst_tile, slot, pfx):
            xt, ssq, yn = ctx['xt'][slot], ctx['ssq'][slot], ctx['yn'][slot]
            kx, ks, ky = pfx + 'xt%d' % slot, pfx + 'ssq%d' % slot, pfx + 'yn%d' % slot
            S.op('dve', lambda e: e.scalar_tensor_tensor(yn[:], ps, ssq[:], gB[:], ALU.mult, ALU.mult), reads=list(kps) + [ks, 'gB'], writes=[ky])
            S.op('pool', lambda e: e.tensor_tensor(yn[:], yn[:], xt[:], ALU.add), reads=[ky, kx], writes=[ky])
            S.dma(dst_tile, yn[:], pfx + 'st%d' % slot, reads=[ky], writes=[])

        def post_norm_residual(ctx, ps, kps, gB, xsrc_tile, dst_tile, slot, pfx):
            xt, sq, ssq, yn = ctx['xt'][slot], ctx['sq'], ctx['ssq'][slot], ctx['yn'][slot]
            kx = pfx + 'rx%d' % slot
            ks = pfx + 'rs%d' % slot
            ky = pfx + 'ry%d' % slot
            S.dma(xt[:], xsrc_tile, pfx + 'rx%d' % slot, reads=[pfx + 'st%d' % slot], writes=[kx])
            S.op('pool', lambda e: e.memset(ssq[:], 0.0), writes=[ks])
            S.op('act', lambda e: e.activation(sq[:], ps, AF.Square, accum_out=ssq[:]), reads=[kps], writes=[pfx + 'sq', ks])
            S.op('dve', lambda e: e.tensor_scalar(ssq[:], ssq[:], 1.0 / D_, EPS, ALU.mult, ALU.add), reads=[ks], writes=[ks])
            S.op('act', lambda e: e.activation(ssq[:], ssq[:], AF.Sqrt), reads=[ks], writes=[ks])
            S.op('dve', lambda e: e.reciprocal(ssq[:], ssq[:]), reads=[ks], writes=[ks])
            S.op('dve', lambda e: e.scalar_tensor_tensor(yn[:], ps, ssq[:], gB[:], ALU.mult, ALU.mult), reads=[kps, ks, 'gB'], writes=[ky])
            S.op('pool', lambda e: e.tensor_tensor(yn[:], yn[:], xt[:], ALU.add), reads=[ky, kx], writes=[ky])
            S.dma(dst_tile, yn[:], pfx + 'st%d' % slot, reads=[ky], writes=[pfx + 'st%d' % slot])

        for l in range(NL):
            xcur = xl[l]
            xnext = xl[l + 1]
            def stage_P(l=l, xcur=xcur, xnext=xnext):
                with ExitStack() as st:
                    A = st.enter_context
                    wsb = A(SB("p_w", [128, 8, PW], BF16))
                    gcol = A(SB("p_gcol", [128, 8], F32))
                    rAc = A(SB("p_rAc", [128, NT, 64], F32))
                    rAs = A(SB("p_rAs", [128, NT, 64], F32))
                    rCc = A(SB("p_rCc", [128, NT, 64], F32))
                    rCs = A(SB("p_rCs", [128, NT, 64], F32))
                    g8 = A(SB("p_g8", [128, 8, 64], F32))
                    ctx = dict(xt=[A(SB("p_xt%d" % i, [128, D_], F32)) for i in range(8)],
                               ssq=[A(SB("p_ssq%d" % i, [128, 1], F32)) for i in range(8)],
                               xn=[A(SB("p_xn%d" % i, [128, D_], BF16)) for i in range(8)])
                    hT2 = [A(SB("p_hT%d" % i, [128, 8, 512], BF16)) for i in range(2)]
                    tsets = [[(A(SB("p_t1_%d" % i, [128, 8, 64], F32)), A(SB("p_t2_%d" % i, [128, 8, 64], F32)), A(SB("p_t3_%d" % i, [128, 8, 64], F32)))
                              for i in range(6)]][0]
                    ss8 = A(SB("p_ss8", [128, 8], F32))
                    qka2 = [A(SB("p_qka%d" % i, [128, 12, 64], BF16)) for i in range(2)]
                    qkc2 = [A(SB("p_qkc%d" % i, [128, 8, 64], BF16)) for i in range(2)]
                    vAs = [A(SB("p_vA%d" % i, [128, 6, 65], BF16)) for i in range(2)]
                    vCs = [A(SB("p_vC%d" % i, [128, 2, 65], BF16)) for i in range(2)]
                    qTs2 = [A(SB("p_qTs%d" % i, [128, 10, 512], BF16)) for i in range(2)]
                    hys = [A(SB("p_hys%d" % i, [128, 512], BF16)) for i in range(2)]
                    pT = A(PS("p_pT", [128, 8, 128], BF16))
                    pj = [A(PS("p_pj%d" % i, [128, 512], F32)) for i in range(3)]
                    pq = [A(PS("p_pq%d" % i, [128, 8, 128], BF16)) for i in range(2)]
                    ph = [A(PS("p_ph%d" % i, [128, 512], F32)) for i in range(2)]

                    S.dma(wsb[:], wbf_in[l].rearrange("(c p) n -> p c n", p=128), 'pw', writes=['wsb'])
                    S.dma(gcol[:], g_pre_col[l], 'pc', writes=['gcol'])
                    S.dma(rAc[:], c_rAc[:, :, :], 'pc', writes=['rAc'])
                    S.dma(rAs[:], c_rAs[:, :, :], 'pc', writes=['rAs'])
                    S.dma(rCc[:], c_rCc[:, :, :], 'pc', writes=['rCc'])
                    S.dma(rCs[:], c_rCs[:, :, :], 'pc', writes=['rCs'])
                    S.dma(g8[:].rearrange("p h e -> p (h e)"), g_qk[l:l + 1, :].broadcast_to([128, 512]), 'pc', writes=['g8'])
                    for i in range(2):
                        S.op('pool', lambda e, i=i: e.memset(vAs[i][:], 1.0), writes=['vAs%d' % i])
                        S.op('pool', lambda e, i=i: e.memset(vCs[i][:], 1.0), writes=['vCs%d' % i])

                    groups = [
                        [(0, 0, 384)],
                        [(0, 384, 384)],
                        [(0, 768, 384), (384, 2432, 128)],
                        [(0, 1920, 512)],
                    ]
                    pjn = 0
                    phn = 0
                    GL = [(b, tg) for b in range(NB) for tg in range(4)]
                    pend_tr = []

                    def grp_jobs(k):
                        b, tg = GL[k]
                        return [(xcur[b, (tg * 4 + ti) * 128:(tg * 4 + ti + 1) * 128, :], (k % 2) * 4 + ti) for ti in range(4)]

                    def partB(k, ti):
                        norm_partB(ctx, (k % 2) * 4 + ti, gcol, hT2[k % 2][:, :, ti * 128:(ti + 1) * 128], 'hT%d_%d' % (k % 2, ti), pT, ['p_pT'], 'p_')

                    norm_partA(ctx, grp_jobs(0), 'p_')
                    for ti in range(4):
                        partB(0, ti)
                    for k, (b, tg) in enumerate(GL):
                            hT = hT2[k % 2]
                            qTs = qTs2[k % 2]
                            kq_ = 'qTs%d' % (k % 2)
                            hk = ['hT%d_%d' % (k % 2, i_) for i_ in range(4)]
                            if k + 1 < len(GL):
                                norm_partA(ctx, grp_jobs(k + 1), 'p_')
                            for ti in range(4):
                                tt = tg * 4 + ti
                                slot = tt % 2
                                lhs = lambda c, ti=ti, hT=hT: hT[:, c, ti * 128:(ti + 1) * 128]
                                qka, qkc = qka2[ti % 2], qkc2[ti % 2]
                                kqa, kqc = 'qka%d' % (ti % 2), 'qkc%d' % (ti % 2)
                                pss = []
                                for gi, grp in enumerate(groups):
                                    tsi = (ti % 2) * 3 + (gi if gi < 2 else 2)
                                    t1, t2, t3 = tsets[tsi]
                                    k1_, k2_, k3_ = 't1_%d' % tsi, 't2_%d' % tsi, 't3_%d' % tsi
                                    ps = pj[pjn % 3]
                                    kp = 'pj%d' % (pjn % 3)
                                    pjn += 1
                                    for (po, wo, wd) in grp:
                                        for c in range(8):
                                            S.op('pe', lambda e, ps=ps, po=po, wo=wo, wd=wd, c=c, lhs=lhs: e.matmul(
                                                ps[:, po:po + wd], lhs(c), wsb[:, c, wo:wo + wd], start=(c == 0), stop=(c == 7)),
                                                reads=[hk[ti], 'wsb'], writes=[kp])
                                    if gi in (0, 1):
                                        x3 = ps[:, 0:384].rearrange("p (h e) -> p h e", e=64)
                                        cA = rAc[:, tt, :].unsqueeze(1).broadcast_to([128, 6, 64])
                                        sAlo = rAs[:, tt, 0:32].unsqueeze(1).broadcast_to([128, 6, 32])
                                        sAhi = rAs[:, tt, 32:64].unsqueeze(1).broadcast_to([128, 6, 32])
                                        S.op('dve', lambda e, x3=x3, cA=cA, t1=t1: e.tensor_tensor(t1[:, 0:6, :], x3, cA, ALU.mult), reads=[kp, 'rAc'], writes=[k1_])
                                        S.op('dve', lambda e, x3=x3, sAlo=sAlo, t2=t2: e.tensor_tensor(t2[:, 0:6, 0:32], x3[:, :, 32:64], sAlo, ALU.mult), reads=[kp, 'rAs'], writes=[k2_])
                                        S.op('dve', lambda e, x3=x3, sAhi=sAhi, t2=t2: e.tensor_tensor(t2[:, 0:6, 32:64], x3[:, :, 0:32], sAhi, ALU.mult), reads=[kp, 'rAs'], writes=[k2_])
                                        S.op('pool', lambda e, gi=gi, qka=qka, t1=t1, t2=t2: e.tensor_tensor(qka[:, gi * 6:(gi + 1) * 6, :], t1[:, 0:6, :], t2[:, 0:6, :], ALU.add),
                                             reads=[k1_, k2_], writes=[kqa])
                                    elif gi == 2:
                                        va, vc = vAs[slot], vCs[slot]
                                        S.op('act', lambda e, ps=ps, va=va: e.activation(va[:, :, 0:64], ps[:, 0:384].rearrange("p (h e) -> p h e", e=64), AF.Copy),
                                             reads=[kp, 'vst%d' % slot], writes=['vAs%d' % slot])
                                        S.op('act', lambda e, ps=ps, vc=vc: e.activation(vc[:, :, 0:64], ps[:, 384:512].rearrange("p (h e) -> p h e", e=64), AF.Copy),
                                             reads=[kp, 'vst%d' % slot], writes=['vCs%d' % slot])
                                        S.dma(vA[b, :, tt, :], va[:].rearrange("p h e -> p (h e)"), 'pv%d' % slot, reads=['vAs%d' % slot], writes=['vst%d' % slot])
                                        S.dma(vC[b, :, tt, :], vc[:].rearrange("p h e -> p (h e)"), 'pv%d' % slot, reads=['vCs%d' % slot], writes=['vst%d' % slot])
                                    else:
                                        x3 = ps[:, :].rearrange("p (h e) -> p h e", e=64)
                                        S.op('act', lambda e, x3=x3, t1=t1: e.activation(t1[:], x3, AF.Square), reads=[kp], writes=[k1_])
                                        S.op('dve', lambda e, t1=t1: e.reduce_sum(ss8[:], t1[:], axis=AX.X), reads=[k1_], writes=['ss8'])
                                        S.op('dve', lambda e: e.tensor_scalar(ss8[:], ss8[:], 1.0 / 64, EPS, ALU.mult, ALU.add), reads=['ss8'], writes=['ss8'])
                                        S.op('act', lambda e: e.activation(ss8[:], ss8[:], AF.Sqrt), reads=['ss8'], writes=['ss8'])
                                        S.op('dve', lambda e: e.reciprocal(ss8[:], ss8[:]), reads=['ss8'], writes=['ss8'])
                                        S.op('dve', lambda e, x3=x3, t3=t3: e.tensor_tensor(t3[:], x3, ss8[:, :].unsqueeze(2).broadcast_to([128, 8, 64]), ALU.mult),
                                             reads=[kp, 'ss8'], writes=[k3_])
                                        S.op('dve', lambda e, t3=t3: e.tensor_tensor(t3[:], t3[:], g8[:], ALU.mult), reads=[k3_, 'g8'], writes=[k3_])
                                        cC = rCc[:, tt, :].unsqueeze(1).broadcast_to([128, 8, 64])
                                        S.op('dve', lambda e, cC=cC, t1=t1, t3=t3: e.tensor_tensor(t1[:], t3[:], cC, ALU.mult), reads=[k3_, 'rCc'], writes=[k1_])
                                        t3v = t3[:].rearrange("p h (a b c) -> p h a b c", a=2, b=2)
                                        t2v = t2[:].rearrange("p h (a b c) -> p h a b c", a=2, b=2)
                                        sv = rCs[:, tt, :].rearrange("p (a b c) -> p a b c", a=2, b=2)
                                        for hb in range(2):
                                            sC = sv[:, :, hb, :].unsqueeze(1).broadcast_to([128, 8, 2, 16])
                                            S.op('dve', lambda e, hb=hb, sC=sC, t3v=t3v, t2v=t2v: e.tensor_tensor(t2v[:, :, :, hb, :], t3v[:, :, :, 1 - hb, :], sC, ALU.mult),
                                                 reads=[k3_, 'rCs'], writes=[k2_])
                                        S.op('pool', lambda e, qkc=qkc, t1=t1, t2=t2: e.tensor_tensor(qkc[:, 0:6, :].rearrange("p (j two) e -> p two j e", two=2),
                                                                               t1[:, 0:6, :].rearrange("p (two j) e -> p two j e", two=2),
                                                                               t2[:, 0:6, :].rearrange("p (two j) e -> p two j e", two=2), ALU.add),
                                             reads=[k1_, k2_], writes=[kqc])
                                        S.op('pool', lambda e, qkc=qkc, t1=t1, t2=t2: e.tensor_tensor(qkc[:, 6:8, :], t1[:, 6:8, :], t2[:, 6:8, :], ALU.add), reads=[k1_, k2_, kqc], writes=[kqc])
                                def do_tr(ti=ti, qTs=qTs, qka=qka, qkc=qkc, kqa=kqa, kqc=kqc, kq_=kq_):
                                    for j in range(6):
                                        S.op('pe', lambda e, j=j: e.transpose(pq[0][:, j, :], qka[:, 2 * j:2 * j + 2, :], ident[:]), reads=[kqa, 'ident'], writes=['pq0'])
                                    S.op('act', lambda e: e.activation(qTs[:, 0:6, ti * 128:(ti + 1) * 128], pq[0][:, 0:6, :], AF.Copy), reads=['pq0'], writes=[kq_])
                                    for j in range(3):
                                        S.op('pe', lambda e, j=j: e.transpose(pq[1][:, j, :], qkc[:, 2 * j:2 * j + 2, :], ident[:]), reads=[kqc, 'ident'], writes=['pq1'])
                                    S.op('pe', lambda e: e.transpose(pq[1][:, 3, :], qkc[:, 6:8, :], ident[:]), reads=[kqc, 'ident'], writes=['pq1'])
                                    S.op('dve', lambda e: e.tensor_copy(qTs[:, 6:10, ti * 128:(ti + 1) * 128], pq[1][:, 0:4, :]), reads=['pq1'], writes=[kq_])
                                if pend_tr:
                                    pend_tr.pop(0)()
                                pend_tr.append(do_tr)
                                if k + 1 < len(GL):
                                    partB(k + 1, ti)
                            for ct in range(6):
                                ps = ph[phn % 2]
                                kp = 'ph%d' % (phn % 2)
                                hs = hys[phn % 2]
                                kh = 'hys%d' % (phn % 2)
                                phn += 1
                                for c in range(8):
                                    S.op('pe', lambda e, ps=ps, c=c, ct=ct, hT=hT: e.matmul(ps[:, :], wsb[:, c, 1152 + ct * 128:1152 + (ct + 1) * 128], hT[:, c, :],
                                                                                     start=(c == 0), stop=(c == 7)),
                                         reads=hk + ['wsb'], writes=[kp])
                                S.op('act', lambda e, ps=ps, hs=hs: e.activation(hs[:], ps[:, :], AF.Copy), reads=[kp, kh + 'st'], writes=[kh])
                                S.dma(hyraw[b, ct, :, tg * 512:(tg + 1) * 512], hs[:], 'ph%d' % (phn % 2), reads=[kh], writes=[kh + 'st'])
                            while pend_tr:
                                pend_tr.pop(0)()
                            tsl = slice(tg * 512, (tg + 1) * 512)
                            S.dma(qTA[b, :, :, tsl].rearrange("j p t -> p j t"), qTs[:, 0:3, :], 'pq%d' % (k % 2), reads=[kq_], writes=[])
                            S.dma(kTA[b, :, :, tsl].rearrange("j p t -> p j t"), qTs[:, 3:6, :], 'pq%d' % (k % 2), reads=[kq_], writes=[])
                            S.dma(qTC[b, :, :, tsl].rearrange("j p t -> p j t"), qTs[:, 6:9, :], 'pq%d' % (k % 2), reads=[kq_], writes=[])
                            S.dma(kTC[b, :, tsl], qTs[:, 9, :], 'pq%d' % (k % 2), reads=[kq_], writes=[])
                    S.barrier()

            def attention(kind, l=l):
                with ExitStack() as st:
                    A = st.enter_context
                    pfx = 'a' + kind
                    LA = 4
                    NPS, NET = 5, 7
                    qT = [A(SB(pfx + "_qT%d" % i, [128, S_], BF16)) for i in range(2)]
                    kT = [[A(SB(pfx + "_kT%d_%d" % (i, h), [128, S_], BF16)) for h in range(2)] for i in range(2)]
                    for i in range(2):
                        for h in range(2):
                            S.op('pool', lambda e, i=i, h=h: e.memset(kT[i][h][:], 0.0), writes=['kT%d' % i])
                    nvh = 6 if kind == 'A' else 2
                    vs = [A(SB(pfx + "_v%d" % i, [128, NT, nvh * 65], BF16)) for i in range(2)]
                    et = [A(SB(pfx + "_et%d" % i, [128, 512], BF16)) for i in range(NET)]
                    em = [A(SB(pfx + "_em%d" % i, [128, 512], BF16)) for i in range(NET)]
                    mask = A(SB(pfx + "_mask", [128, 3968], BF16))
                    rec = A(SB(pfx + "_rec", [128, 512], F32))
                    osb = A(SB(pfx + "_osb", [64, 512], F32))
                    ob = [A(SB(pfx + "_ob%d" % i, [64, 512], BF16)) for i in range(2)]
                    ps = [A(PS(pfx + "_ps%d" % i, [128, 512], F32)) for i in range(NPS)]
                    po = [A(PS(pfx + "_po%d" % i, [128, 512], F32)) for i in range(2)]
                    pb = A(PS(pfx + "_pb", [128, 512], F32))
                    pdum = None
                    NDUM = ATT_NDUM
                    if kind == 'A':
                        S.dma(mask[:], c_mask[:, :], 'am', writes=['mask'])
                    items = []
                    grp = 0
                    for b in range(NB):
                        for j in range(3):
                            for half in range(2):
                                for g in range(4):
                                    tiles = []
                                    for i in range(NT):
                                        dmin = i * 128 - g * 512 - 511
                                        dmax = i * 128 + 127 - g * 512
                                        if kind == 'A' and (dmin > 1024 or dmax < -1024):
                                            continue
                                        tiles.append(i)
                                    for ii, i in enumerate(tiles):
                                        items.append(dict(b=b, j=j, half=half, g=g, i=i, ii=ii, nt=len(tiles), grp=grp))
                                    grp += 1
                    cur = dict(b=-1, bj=-1, nq=0, nv=0)
                    part2 = []

                    def issue_loads(it):
                        b, j = it['b'], it['j']
                        if b != cur['b']:
                            cur['b'] = b
                            sl = cur['nv'] % 2
                            cur['nv'] += 1
                            cur['vt'], cur['kv'] = vs[sl], 'v%d' % sl
                            S.dma(vs[sl][:], (vA if kind == 'A' else vC)[b], 'av%d' % sl, writes=['v%d' % sl])
                            if kind == 'C':
                                cur['kt'], cur['kk'] = kT[b % 2], 'kT%d' % (b % 2)
                                for h in range(2):
                                    S.dma(kT[b % 2][h][64 * h:64 * h + 64, :], kTC[b, 64 * h:64 * h + 64, :], 'ak%d' % (b % 2), writes=['kT%d' % (b % 2)])
                        if (b, j) != cur['bj']:
                            cur['bj'] = (b, j)
                            sl = cur['nq'] % 2
                            cur['nq'] += 1
                            cur['qt'], cur['kq'] = qT[sl], 'qT%d' % sl
                            S.dma(qT[sl][:], (qTA if kind == 'A' else qTC)[b, j], 'aq%d' % sl, writes=['qT%d' % sl])
                            if kind == 'A':
                                cur['kt'], cur['kk'] = kT[sl], 'kT%d' % sl
                                for h in range(2):
                                    S.dma(kT[sl][h][64 * h:64 * h + 64, :], kTA[b, j, 64 * h:64 * h + 64, :], 'ak%d' % sl, writes=['kT%d' % sl])
                        for k_ in ('vt', 'kv', 'kt', 'kk', 'qt', 'kq'):
                            it[k_] = cur[k_]

                    def emit_qk(n, it):
                        base = 64 * it['half']
                        i, g = it['i'], it['g']
                        pst, kps = ps[n % NPS], 'ps%d' % (n % NPS)
                        ett, ke = et[n % NET], 'et%d' % (n % NET)
                        emt, kem = em[n % NET], 'em%d' % (n % NET)
                        kt, qt = it['kt'][it['half']], it['qt']
                        S.op('pe', lambda e: e.matmul(pst[:, :], kt[:, i * 128:(i + 1) * 128], qt[:, g * 512:(g + 1) * 512], start=True, stop=True),
                             reads=[it['kk'], it['kq']], writes=[kps])
                        S.op('act', lambda e: e.activation(ett[:], pst[:, :], AF.Exp, scale=0.125), reads=[kps], writes=[ke])
                        for _ in range(NDUM):
                            S.op('pe', lambda e: e.matmul(pdum[:, :], kt[:, i * 128:(i + 1) * 128], qt[:, g * 512:(g + 1) * 512], start=True, stop=True),
                                 reads=[it['kk'], it['kq']], writes=['pdum'])
                        if kind == 'A':
                            x0 = g * 512 - i * 128 + 1920
                            eng = 'dve'
                            S.op(eng, lambda e: e.tensor_tensor(emt[:], ett[:], mask[:, x0:x0 + 512], ALU.mult), reads=[ke, 'mask'], writes=[kem])
                            it['rhs'], it['kr'] = emt, kem
                        else:
                            it['rhs'], it['kr'] = ett, ke

                    def emit_pv(m, it):
                        b, j, half, g, i, ii, nt = it['b'], it['j'], it['half'], it['g'], it['i'], it['ii'], it['nt']
                        if kind == 'A':
                            head = 2 * j + half
                            vh, chunk = head, head
                        else:
                            head = j + 3 * half
                            vh, chunk = half, 10 + head
                        pot, kpo = po[it['grp'] % 2], 'po%d' % (it['grp'] % 2)
                        vt, rhs = it['vt'], it['rhs']
                        S.op('pe', lambda e: e.matmul(pot[0:65, :], vt[:, i, vh * 65:(vh + 1) * 65], rhs[:], start=(ii == 0), stop=(ii == nt - 1)),
                             reads=[it['kv'], it['kr']], writes=[kpo])
                        if ii == nt - 1:
                            if kind == 'A':
                                S.op('act', lambda e: e.activation(rec[64:65, :], pot[64:65, :], AF.Ln), reads=[kpo], writes=['rec'])
                                S.op('act', lambda e: e.activation(rec[64:65, :], rec[64:65, :], AF.Exp, scale=-1.0), reads=['rec'], writes=['rec'])
                            else:
                                S.op('dve', lambda e: e.reciprocal(rec[64:65, :], pot[64:65, :]), reads=[kpo], writes=['rec'])
                            S.op('act', lambda e: e.activation(osb[:], pot[0:64, :], AF.Copy), reads=[kpo], writes=['osb'])
                            obt, kob = ob[it['grp'] % 2], 'ob%d' % (it['grp'] % 2)

                            def fin():
                                S.op('pe', lambda e: e.matmul(pb[0:64, :], ones_f[64:65, 0:64], rec[64:65, :], start=True, stop=True), reads=['rec', 'ones_f'], writes=['pb'])
                                S.op('dve', lambda e: e.tensor_tensor(obt[:], osb[:], pb[0:64, :], ALU.mult), reads=['osb', 'pb'], writes=[kob])
                                S.dma(oT[b, chunk * 64:(chunk + 1) * 64, g * 512:(g + 1) * 512], obt[:], 'ao%d' % (it['grp'] % 2), reads=[kob], writes=[])
                            part2.append((m + 9, fin))

                    N_ = len(items)
                    PFD = 24
                    nload = 0
                    for n in range(N_ + LA):
                        while nload < N_ and nload <= n + PFD:
                            issue_loads(items[nload])
                            nload += 1
                        if n < N_:
                            emit_qk(n, items[n])
                        m = n - LA
                        if m >= 0:
                            emit_pv(m, items[m])
                            while part2 and part2[0][0] <= m:
                                part2.pop(0)[1]()
                    while part2:
                        part2.pop(0)[1]()
                    S.barrier()

            def stage_F(l=l, xcur=xcur, xnext=xnext):
                with ExitStack() as st:
                    A = st.enter_context
                    zT = A(SB("f_zT", [33, S_], F32))
                    w1 = A(SB("f_w1", [33, 64], F32))
                    w2 = A(SB("f_w2", [64, 64], F32))
                    w3 = A(SB("f_w3", [64, 1024], F32))
                    b1 = A(SB("f_b1", [64, 1], F32))
                    b2 = A(SB("f_b2", [64, 1], F32))
                    fr = A(SB("f_fr", [64, 2], F32))
                    sc = A(SB("f_sc", [64, 8], F32))
                    h1 = A(SB("f_h1", [64, S_], F32))
                    h2 = A(SB("f_h2", [64, S_], F32))
                    sa = A(SB("f_sa", [64, 512], F32))
                    sb_ = A(SB("f_sb", [64, 512], F32))
                    sc_ = A(SB("f_sc2", [64, 512], F32))
                    decB = A(SB("f_decB", [128, 1024], F32))
                    tneg = A(SB("f_tneg", [128, NT], F32))
                    wins = [A(SB("f_win%d" % i, [128, 1024], F32)) for i in range(2)]
                    filt = A(SB("f_filt", [128, NT, 1024], F32))
                    absfs = [A(SB("f_abs%d" % i, [128, 1024], F32)) for i in range(2)]
                    rn = A(SB("f_rn", [128, 512], F32))
                    tmp = A(SB("f_tmp", [128, 512], F32))
                    tmp2 = A(SB("f_tmp2", [128, 512], F32))
                    Pm = A(SB("f_P", [128, NT, 512], BF16))
                    Qm = A(SB("f_Q", [128, NT, 512], BF16))
                    fwb = [A(SB("f_fw%d" % i, [128, NT, 128], BF16)) for i in range(3)]
                    ko = [A(SB("f_ko%d" % i, [128, 512], F32)) for i in range(2)]
                    pm = [A(PS("f_pm%d" % i, [128, 512], F32)) for i in range(4)]
                    pn = [A(PS("f_pn%d" % i, [128, 512], F32)) for i in range(2)]

                    S.dma(zT[:], c_zT[:, :], 'fc', writes=['zT'])
                    for rt_ in range(2):
                        S.dma(fwb[rt_][:], c_fw[rt_], 'ff%d' % rt_, writes=['fwb%d' % rt_])
                    S.dma(w1[:], hy_w1[l], 'fc', writes=['w1'])
                    S.dma(w2[:], hy_w2[l], 'fc', writes=['w2'])
                    S.dma(w3[:], hy_w3[l], 'fc', writes=['w3'])
                    S.dma(b1[:], hy_b1[l], 'fc', writes=['b1'])
                    S.dma(b2[:], hy_b2[l], 'fc', writes=['b2'])
                    S.dma(fr[:], hy_fr[l], 'fc', writes=['fr'])
                    S.dma(decB[:], hy_dec[l:l + 1, :].broadcast_to([128, 1024]), 'fc', writes=['decB'])
                    S.dma(tneg[:], c_tneg[:, :], 'fc', writes=['tneg'])
                    for li, bb in ((0, b1), (1, b2)):
                        o = 4 * li
                        S.op('dve', lambda e, li=li, bb=bb, o=o: e.tensor_tensor(sc[:, o + 3:o + 4], fr[:, li:li + 1], bb[:, 0:1], ALU.mult), reads=['fr', 'b1', 'b2'], writes=['sc'])
                        S.op('dve', lambda e, li=li, o=o: e.tensor_copy(sc[:, o + 2:o + 3], fr[:, li:li + 1]), reads=['fr', 'sc'], writes=['sc'])
                        S.op('dve', lambda e, o=o: e.tensor_scalar(sc[:, o:o + 2], sc[:, o + 2:o + 4], 0.25, None, ALU.mult), reads=['sc'], writes=['sc'])

                    def sin_layer(li, wmat, kw, src, ksrc, dst, kdst, K):
                        o = 4 * li
                        for n in range(4):
                            p = pm[n % 4]
                            kp = 'pm%d' % (n % 4)
                            sl = slice(n * 512, (n + 1) * 512)
                            S.op('pe', lambda e, p=p, sl=sl: e.matmul(p[0:64, :], wmat[0:K, :], src[0:K, sl], start=True, stop=True), reads=[kw, ksrc], writes=[kp])
                            S.op('act', lambda e, p=p: e.activation(sa[:], p[0:64, :], AF.Sin, scale=sc[:, o:o + 1], bias=sc[:, o + 1:o + 2]), reads=[kp, 'sc'], writes=['sa'])
                            S.op('act', lambda e, p=p: e.activation(sb_[:], p[0:64, :], AF.Abs, scale=sc[:, o + 2:o + 3], bias=sc[:, o + 3:o + 4]), reads=[kp, 'sc'], writes=['sb'])
                            S.op('dve', lambda e: e.tensor_scalar(sb_[:], sb_[:], -0.25, float(math.pi / 2), ALU.mult, ALU.add), reads=['sb'], writes=['sb'])
                            S.op('act', lambda e: e.activation(sb_[:], sb_[:], AF.Sin), reads=['sb'], writes=['sb'])
                            S.op('dve', lambda e: e.tensor_tensor(sc_[:], sa[:], sa[:], ALU.mult), reads=['sa'], writes=['sc2'])
                            S.op('dve', lambda e: e.tensor_scalar(sc_[:], sc_[:], -8.0, 4.0, ALU.mult, ALU.add), reads=['sc2'], writes=['sc2'])
                            S.op('dve', lambda e: e.tensor_tensor(sa[:], sa[:], sb_[:], ALU.mult), reads=['sa', 'sb'], writes=['sa'])
                            S.op('dve', lambda e, sl=sl: e.tensor_tensor(dst[:, sl], sa[:], sc_[:], ALU.mult), reads=['sa', 'sc2'], writes=[kdst])

                    sin_layer(0, w1, 'w1', zT, 'zT', h1, 'h1', 33)
                    sin_layer(1, w2, 'w2', h1, 'h1', h2, 'h2', 64)
                    def f_win(tt):
                        S.op('act', lambda e: e.activation(wins[tt % 2][:], decB[:], AF.Exp, scale=tneg[:, tt:tt + 1]), reads=['decB', 'tneg'], writes=['win%d' % (tt % 2)])

                    def f_h3(tt):
                        for n in range(2):
                            p = pm[(tt * 2 + n) % 4]
                            kp = 'pm%d' % ((tt * 2 + n) % 4)
                            sl = slice(n * 512, (n + 1) * 512)
                            S.op('pe', lambda e, p=p, sl=sl: e.matmul(p[:, :], h2[:, tt * 128:(tt + 1) * 128], w3[:, sl], start=True, stop=True), reads=['h2', 'w3'], writes=[kp])

                    f_win(0)
                    f_h3(0)
                    for tt in range(NT):
                        if tt + 1 < NT:
                            f_win(tt + 1)
                            f_h3(tt + 1)
                        win = wins[tt % 2]
                        absf = absfs[tt % 2]
                        for n in range(2):
                            p = pm[(tt * 2 + n) % 4]
                            kp = 'pm%d' % ((tt * 2 + n) % 4)
                            sl = slice(n * 512, (n + 1) * 512)
                            S.op('dve', lambda e, p=p, tt=tt, sl=sl, win=win: e.tensor_tensor(filt[:, tt, sl], p[:, :], win[:, sl], ALU.mult), reads=[kp, 'win%d' % (tt % 2)], writes=['filt%d' % tt])
                        if tt == 0:
                            fv0 = filt[0:1, 0, :].rearrange("p (o f c) -> p o f c", o=2, f=2)
                            S.op('dve', lambda e, fv0=fv0: e.memset(fv0[:, :, 1, :], 0.0), reads=['filt0'], writes=['filt0'])
                        S.op('act', lambda e, tt=tt, absf=absf: e.activation(absf[:], filt[:, tt, :], AF.Abs), reads=['filt%d' % tt], writes=['absf%d' % (tt % 2)])
                        for n in range(2):
                            S.op('pe', lambda e, n=n, tt=tt, absf=absf: e.matmul(pn[n][:, :], ones_f[:, :], absf[:, n * 512:(n + 1) * 512], start=(tt == 0), stop=(tt == NT - 1)),
                                 reads=['absf%d' % (tt % 2), 'ones_f'], writes=['pn%d' % n])
                    for o in range(2):
                        S.op('act', lambda e, o=o: e.activation(tmp[:, 0:256], pn[o][:, 0:256], AF.Copy), reads=['pn%d' % o], writes=['tmp'])
                        S.op('dve', lambda e, o=o: e.tensor_tensor(rn[:, o * 256:(o + 1) * 256], tmp[:, 0:256], pn[o][:, 256:512], ALU.add), reads=['tmp', 'pn%d' % o], writes=['rn'])
                    S.op('dve', lambda e: e.reciprocal(rn[:], rn[:]), reads=['rn'], writes=['rn'])
                    rn3 = rn[:].rearrange("p (o c) -> p o c", o=2)
                    for tt in range(NT):
                        fv = filt[:, tt, :].rearrange("p (o f c) -> p o f c", o=2, f=2)
                        ta_ = tmp[:].rearrange("p (o c) -> p o c", o=2)
                        tb_ = tmp2[:].rearrange("p (o c) -> p o c", o=2)
                        S.op('pool', lambda e, fv=fv, ta_=ta_: e.tensor_tensor(ta_, fv[:, :, 0, :], fv[:, :, 1, :], ALU.add), reads=['filt%d' % tt], writes=['tmp'])
                        S.op('pool', lambda e, fv=fv, tb_=tb_: e.tensor_tensor(tb_, fv[:, :, 0, :], fv[:, :, 1, :], ALU.subtract), reads=['filt%d' % tt], writes=['tmp2'])
                        S.op('dve', lambda e, tt=tt, ta_=ta_: e.tensor_tensor(Pm[:, tt, :].rearrange("p (o c) -> p o c", o=2), ta_, rn3, ALU.mult), reads=['tmp', 'rn'], writes=['Pm'])
                        S.op('dve', lambda e, tt=tt, tb_=tb_: e.tensor_tensor(Qm[:, tt, :].rearrange("p (o c) -> p o c", o=2), tb_, rn3, ALU.mult), reads=['tmp2', 'rn'], writes=['Qm'])
                    for rt in range(32):
                        if rt + 2 < 32:
                            S.dma(fwb[(rt + 2) % 3][:], c_fw[rt + 2], 'ff%d' % ((rt + 2) % 3), writes=['fwb%d' % ((rt + 2) % 3)])
                        fb = fwb[rt % 3]
                        kfb = 'fwb%d' % (rt % 3)
                        p = pm[rt % 4]
                        kp = 'pm%d' % (rt % 4)
                        src, ks = (Pm, 'Pm') if rt < 16 else (Qm, 'Qm')
                        for tt in range(NT):
                            S.op('pe', lambda e, p=p, fb=fb, tt=tt, src=src: e.matmul(p[:, :], fb[:, tt, :], src[:, tt, :], start=(tt == 0), stop=(tt == NT - 1)),
                                 reads=[kfb, ks], writes=[kp])
                        kot = ko[rt % 2]
                        kko = 'ko%d' % (rt % 2)
                        S.op('act', lambda e, p=p, kot=kot: e.activation(kot[:], p[:, :], AF.Copy), reads=[kp], writes=[kko])
                        S.dma(kfd[rt], kot[:], 'fk%d' % (rt % 2), reads=[kko], writes=[])
                    S.barrier()

            def stage_H(l=l, xcur=xcur, xnext=xnext):
                with ExitStack() as st:
                    A = st.enter_context
                    NC_ = NB * 256
                    V = A(SB("h_V", [128, NT, NC_], BF16))
                    Y = A(SB("h_Y", [128, 32, NC_], BF16))
                    cw = A(SB("h_cw", [128, 6, 3], F32))
                    cb = A(SB("h_cb", [128, 6], F32))
                    dB = A(SB("h_dB", [128, 512], F32))
                    raw = [A(SB("h_raw%d" % i, [128, S_], BF16)) for i in range(3)]
                    u0s = [A(SB("h_u0_%d" % i, [128, S_], F32)) for i in range(2)]
                    ubs = [A(SB("h_ub_%d" % i, [128, S_], BF16)) for i in range(2)]
                    xs = [A(SB("h_xs%d" % i, [128, NT, 128], BF16)) for i in range(2)]
                    fwb = [A(SB("h_fw%d" % i, [128, NT, 128], BF16)) for i in range(3)]
                    ivb = [A(SB("h_iv%d" % i, [128, 32, 128], BF16)) for i in range(2)]
                    kre = [A(SB("h_kre%d" % i, [128, 256], F32)) for i in range(2)]
                    kim = [A(SB("h_kim%d" % i, [128, 256], F32)) for i in range(2)]
                    ure = A(SB("h_ure", [128, NC_], F32))
                    uim = A(SB("h_uim", [128, NC_], F32))
                    ta = A(SB("h_ta", [128, NC_], F32))
                    tb = A(SB("h_tb", [128, NC_], F32))
                    xg = [A(SB("h_xg%d" % i, [128, NC_], BF16)) for i in range(2)]
                    obts = [A(SB("h_obt%d" % i, [128, NC_], BF16)) for i in range(2)]
                    obT = [A(SB("h_obT%d" % i, [128, NB * 2, 128], BF16)) for i in range(2)]
                    pp = A(PS("h_pp", [128, 8, 512], F32))

                    S.dma(cw[:], hy_cw[l], 'hc', writes=['cw'])
                    S.dma(cb[:], hy_cb[l], 'hc', writes=['cb'])
                    S.dma(dB[:], hy_d[l:l + 1, :].broadcast_to([128, 512]), 'hc', writes=['dB'])
                    nr = 0
                    nx = 0
                    chains = [(b, ct) for b in range(NB) for ct in range(6)]

                    def load_raw(i):
                        S.dma(raw[i % 3][:], hyraw[chains[i][0], chains[i][1]], 'hr%d' % (i % 3), writes=['raw%d' % (i % 3)])

                    load_raw(0)
                    load_raw(1)
                    def S12(ci):
                        b, ct = chains[ci]
                        if ci + 2 < len(chains):
                            load_raw(ci + 2)
                        r = raw[ci % 3]
                        kr = 'raw%d' % (ci % 3)
                        u0, ub = u0s[ci % 2], ubs[ci % 2]
                        ku0, kub = 'u0_%d' % (ci % 2), 'ub_%d' % (ci % 2)
                        S.op('act', lambda e: e.activation(u0[:], r[:], AF.Identity, scale=cw[:, ct, 1:2], bias=cb[:, ct:ct + 1]), reads=[kr, 'cw', 'cb'], writes=[ku0])
                        S.op('dve', lambda e: e.scalar_tensor_tensor(u0[:, 1:S_], r[:, 0:S_ - 1], cw[:, ct, 0:1], u0[:, 1:S_], ALU.mult, ALU.add),
                             reads=[kr, ku0, 'cw'], writes=[ku0])
                        S.op('dve', lambda e: e.scalar_tensor_tensor(ub[:, 0:S_ - 1], r[:, 1:S_], cw[:, ct, 2:3], u0[:, 0:S_ - 1], ALU.mult, ALU.add),
                             reads=[kr, ku0, 'cw'], writes=[kub])

                    def S3(ci):
                        b, ct = chains[ci]
                        u0, ub = u0s[ci % 2], ubs[ci % 2]
                        ku0, kub = 'u0_%d' % (ci % 2), 'ub_%d' % (ci % 2)
                        S.op('act', lambda e: e.activation(ub[:, S_ - 1:S_], u0[:, S_ - 1:S_], AF.Copy), reads=[ku0, kub], writes=[kub])
                        bank = ci % 2
                        pt = pp[:, 4 * bank:4 * bank + 2, :].rearrange("p a n -> p (a n)").bitcast(BF16).rearrange("p (t n) -> p t n", n=128)[:, 0:NT, :]
                        kpt = 'ppt%d' % bank
                        for tt in range(NT):
                            S.op('pe', lambda e, tt=tt: e.transpose(pt[:, tt, :], ub[:, tt * 128:(tt + 1) * 128], ident[:]), reads=[kub, 'ident'], writes=[kpt])
                        col = b * 256 + (ct % 2) * 128
                        if ct < 2:
                            S.op('act', lambda e: e.activation(V[:, :, col:col + 128], pt, AF.Copy), reads=[kpt], writes=['V'])
                        else:
                            nx = xcnt[0]
                            xcnt[0] += 1
                            x_ = xs[nx % 2]
                            kx = 'xs%d' % (nx % 2)
                            S.op('act', lambda e: e.activation(x_[:], pt, AF.Copy), reads=[kpt], writes=[kx])
                            dstt = (x1tm if ct < 4 else x2tm)
                            S.dma(dstt[:, :, col:col + 128].rearrange("t p c -> p t c"), x_[:], 'hx%d' % (nx % 2), reads=[kx], writes=[])

                    xcnt = [0]
                    S12(0)
                    for ci in range(len(chains)):
                        if ci + 1 < len(chains):
                            S12(ci + 1)
                        S3(ci)
                    S.barrier()
                    nfw = 0
                    niv = 0
                    pend_h = []
                    nk = 0
                    ngx = 0
                    nob = 0
                    GW = min(512, NC_)
                    ngrp = NC_ // GW
                    for order in range(2):
                        for a in range(16):
                            k1, k2 = kre[nk % 2], kim[nk % 2]
                            kk = 'kf%d' % (nk % 2)
                            S.dma(k1[:], kfd[a, :, order * 256:(order + 1) * 256], 'hk%d' % (nk % 2), writes=[kk])
                            S.dma(k2[:], kfd[16 + a, :, order * 256:(order + 1) * 256], 'hk%d' % (nk % 2), writes=[kk])
                            nk += 1
                            for part in range(2):
                                rt = a + 16 * part
                                fb = fwb[nfw % 3]
                                kfb = 'fwb%d' % (nfw % 3)
                                S.dma(fb[:], c_fw[rt], 'hf%d' % (nfw % 3), writes=[kfb])
                                nfw += 1
                                for tt in range(NT):
                                    for n in range(ngrp):
                                        bk = (a % 2) * 4 + part * 2 + n
                                        S.op('pe', lambda e, bk=bk, fb=fb, tt=tt, n=n: e.matmul(pp[:, bk, 0:GW], fb[:, tt, :], V[:, tt, n * GW:(n + 1) * GW],
                                                                                            start=(tt == 0), stop=(tt == NT - 1)),
                                             reads=[kfb, 'V'], writes=['pp%d' % bk])
                            bre = [(a % 2) * 4 + n for n in range(ngrp)]
                            bim = [(a % 2) * 4 + 2 + n for n in range(ngrp)]
                            for n in range(ngrp):
                                sl = slice(n * GW, (n + 1) * GW)
                                S.op('act', lambda e, n=n, sl=sl, bre=bre: e.activation(ure[:, sl], pp[:, bre[n], 0:GW], AF.Copy), reads=['pp%d' % bre[n]], writes=['ure'])
                                S.op('act', lambda e, n=n, sl=sl, bim=bim: e.activation(uim[:, sl], pp[:, bim[n], 0:GW], AF.Copy), reads=['pp%d' % bim[n]], writes=['uim'])
                            nb_ = NC_ // 256
                            k1b = k1[:].unsqueeze(1).broadcast_to([128, nb_, 256])
                            k2b = k2[:].unsqueeze(1).broadcast_to([128, nb_, 256])
                            v3 = lambda t_: t_.rearrange("p (b c) -> p b c", c=256)
                            S.op('pool', lambda e, k1b=k1b: e.tensor_tensor(v3(ta[:]), v3(ure[:]), k1b, ALU.mult), reads=['ure', kk], writes=['ta'])
                            S.op('dve', lambda e, k2b=k2b: e.tensor_tensor(v3(tb[:]), v3(uim[:]), k2b, ALU.mult), reads=['uim', kk], writes=['tb'])
                            S.op('pool', lambda e, a=a: e.tensor_tensor(Y[:, a, :], ta[:], tb[:], ALU.subtract), reads=['ta', 'tb'], writes=['Y'])
                            S.op('dve', lambda e, k2b=k2b: e.tensor_tensor(v3(tb[:]), v3(ure[:]), k2b, ALU.mult), reads=['ure', kk, 'tb'], writes=['tb'])
                            S.op('pool', lambda e, k1b=k1b: e.tensor_tensor(v3(ta[:]), v3(uim[:]), k1b, ALU.mult), reads=['uim', kk, 'ta'], writes=['ta'])
                            S.op('dve', lambda e, a=a: e.tensor_tensor(Y[:, 16 + a, :], ta[:], tb[:], ALU.add), reads=['ta', 'tb'], writes=['Y'])
                        for tt in range(NT):
                            ib = ivb[niv % 2]
                            kib = 'ivb%d' % (niv % 2)
                            S.dma(ib[:], c_inv[tt], 'hi%d' % (niv % 2), writes=[kib])
                            niv += 1
                            xg_ = xg[ngx % 2]
                            kxg = 'xg%d' % (ngx % 2)
                            S.dma(xg_[:], (x1tm if order == 0 else x2tm)[tt], 'hg%d' % (ngx % 2), writes=[kxg])
                            ngx += 1
                            bks = [(tt % 2) * ngrp + n for n in range(ngrp)]
                            for rt in range(32):
                                for n in range(ngrp):
                                    S.op('pe', lambda e, ib=ib, rt=rt, n=n, bk=bks[n]: e.matmul(pp[:, bk, 0:GW], ib[:, rt, :], Y[:, rt, n * GW:(n + 1) * GW],
                                                                                               start=(rt == 0), stop=(rt == 31)),
                                         reads=[kib, 'Y'], writes=['pp%d' % bks[n]])
                            while pend_h:
                                pend_h.pop(0)()
                            nb_ = NC_ // 256
                            dv = dB[:, order * 256:(order + 1) * 256].unsqueeze(1).broadcast_to([128, nb_, 256])
                            v3 = lambda t_: t_.rearrange("p (b c) -> p b c", c=256)
                            S.op('pool', lambda e, tt=tt, dv=dv: e.tensor_tensor(v3(ta[:]), v3(V[:, tt, :]), dv, ALU.mult), reads=['V', 'dB', 'ta'], writes=['ta'])
                            for n in range(ngrp):
                                sl = slice(n * GW, (n + 1) * GW)
                                S.op('dve', lambda e, sl=sl, bk=bks[n]: e.tensor_tensor(ta[:, sl], ta[:, sl], pp[:, bk, 0:GW], ALU.add), reads=['ta', 'pp%d' % bks[n]], writes=['ta'])
                            if order == 0:
                                S.op('pool', lambda e, tt=tt, xg_=xg_: e.tensor_tensor(V[:, tt, :], ta[:], xg_[:], ALU.mult), reads=['ta', kxg, 'V'], writes=['V'])
                            else:
                                obt = obts[nob % 2]
                                kobt = 'obt%d' % (nob % 2)
                                S.op('pool', lambda e, xg_=xg_, obt=obt: e.tensor_tensor(obt[:], ta[:], xg_[:], ALU.mult), reads=['ta', kxg], writes=[kobt])

                                def fin_tt(tt=tt, obt=obt, kobt=kobt, nob=nob):
                                    pt = pp[:, 6:8, :].rearrange("p a n -> p (a n)").bitcast(BF16).rearrange("p (t n) -> p t n", n=128)
                                    for q in range(NB * 2):
                                        S.op('pe', lambda e, q=q: e.transpose(pt[:, q, :], obt[:, q * 128:(q + 1) * 128], ident[:]), reads=[kobt, 'ident'], writes=['pp6'])
                                    oo = obT[nob % 2]
                                    koo = 'obT%d' % (nob % 2)
                                    S.op('act', lambda e: e.activation(oo[:], pt[:, 0:NB * 2, :], AF.Copy), reads=['pp6'], writes=[koo])
                                    for b in range(NB):
                                        S.dma(oT[b, 384:640, tt * 128:(tt + 1) * 128].rearrange("(h p) t -> p h t", p=128), oo[:, 2 * b:2 * b + 2, :], 'ho%d' % (nob % 2),
                                              reads=[koo], writes=[])
                                pend_h.append(fin_tt)
                                nob += 1
                    while pend_h:
                        pend_h.pop(0)()
                    S.barrier()

            def stage_O(l=l, xcur=xcur, xnext=xnext):
                with ExitStack() as st:
                    A = st.enter_context
                    wo = A(SB("o_w", [128, 8, D_], BF16))
                    gB = A(SB("o_gB", [128, D_], F32))
                    oTs = [A(SB("o_oT%d" % i, [128, 8, S_], BF16)) for i in range(2)]
                    NSL = 4
                    NPSL = 3
                    ctx = dict(xt=[A(SB("o_xt%d" % i, [128, D_], F32)) for i in range(NSL)],
                               ssq=[A(SB("o_ssq%d" % i, [128, 1], F32)) for i in range(NSL)],
                               yn=[A(SB("o_yn%d" % i, [128, D_], F32)) for i in range(NSL)])
                    xn2 = [A(SB("o_xn%d" % i, [128, D_], BF16)) for i in range(NSL)]
                    ss2 = [A(SB("o_ss2_%d" % i, [128, 1], F32)) for i in range(NSL)]
                    gcol2 = A(SB("o_gcol2", [128, 8], F32))
                    hst = [A(SB("o_hst%d" % i, [128, 8, 512], BF16)) for i in range(2)]
                    pm = [A(PS("o_pm%d" % i, [128, D_], F32)) for i in range(NPSL)]
                    pT2 = [A(PS("o_pT%d" % i, [128, 8, 128], BF16)) for i in range(2)]
                    S.dma(wo[:], wbf_out[l].rearrange("(c p) n -> p c n", p=128), 'ow', writes=['wo'])
                    S.dma(gB[:], g_post[l:l + 1, :].broadcast_to([128, D_]), 'oc', writes=['gB'])
                    S.dma(gcol2[:], g_fpre_col[l], 'oc', writes=['gcol2'])
                    TL = [(b, tt) for b in range(NB) for tt in range(NT)]
                    PF = 2

                    def load_oT(b):
                        for c in range(8):
                            S.dma(oTs[b % 2][:, c, :], oT[b, c * 128:(c + 1) * 128, :], 'oo%d' % (b % 2), writes=['oTs%d' % (b % 2)])

                    def mm(n):
                        b, tt = TL[n]
                        ot, ko = oTs[b % 2], 'oTs%d' % (b % 2)
                        p, kp = pm[n % NPSL], 'opm%d' % (n % NPSL)
                        for nn in range(2):
                            for c in range(8):
                                S.op('pe', lambda e, nn=nn, c=c: e.matmul(p[:, nn * 512:(nn + 1) * 512], ot[:, c, tt * 128:(tt + 1) * 128],
                                                                          wo[:, c, nn * 512:(nn + 1) * 512], start=(c == 0), stop=(c == 7)),
                                     reads=[ko, 'wo'], writes=[kp])

                    def p1(n):
                        post_part1(ctx, pm[n % NPSL][:, :], ['opm%d' % (n % NPSL)], None, n % NSL, 'o_')

                    def p2(n):
                        b, tt = TL[n]
                        post_part2(ctx, pm[n % NPSL][:, :], ['opm%d' % (n % NPSL)], gB, xmid[b, tt * 128:(tt + 1) * 128, :], n % NSL, 'o_')

                    def p3a(n):
                        sl = n % NSL
                        yn, xn, ss = ctx['yn'][sl], xn2[sl], ss2[sl]
                        ky, kx, ks = 'o_yn%d' % sl, 'o_xn2_%d' % sl, 'o_ss2_%d' % sl
                        S.op('act', lambda e: e.activation(xn[:], yn[:], AF.Square, accum_out=ss[:]), reads=[ky], writes=[kx, ks])
                        S.op('dve', lambda e: e.tensor_scalar(ss[:], ss[:], 1.0 / D_, EPS, ALU.mult, ALU.add), reads=[ks], writes=[ks])
                        S.op('act', lambda e: e.activation(ss[:], ss[:], AF.Sqrt), reads=[ks], writes=[ks])
                        S.op('dve', lambda e: e.reciprocal(ss[:], ss[:]), reads=[ks], writes=[ks])
                        S.op('act', lambda e: e.activation(xn[:], yn[:], AF.Copy, scale=ss[:]), reads=[ky, ks], writes=[kx])

                    def p3b(n):
                        b, tt = TL[n]
                        sl = n % NSL
                        xn = xn2[sl]
                        pT, kpT = pT2[n % 2], 'o_pT%d' % (n % 2)
                        hs, khs = hst[(n // 4) % 2], 'o_hst%d' % ((n // 4) % 2)
                        ti = tt % 4
                        for c in range(8):
                            S.op('pe', lambda e, c=c: e.transpose(pT[:, c, :], xn[:, c * 128:(c + 1) * 128], ident[:]), reads=['o_xn2_%d' % sl, 'ident'], writes=[kpT])
                        S.op('dve', lambda e: e.tensor_tensor(hs[:, :, ti * 128:(ti + 1) * 128], pT[:, :, :], gcol2[:, :].unsqueeze(2).broadcast_to([128, 8, 128]), ALU.mult),
                             reads=[kpT, 'gcol2'], writes=[khs])
                        if ti == 3:
                            tg = tt // 4
                            S.dma(hTd[b, :, tg * 512:(tg + 1) * 512].rearrange("(c p) t -> p c t", p=128), hs[:], 'oh%d' % ((n // 4) % 2), reads=[khs], writes=[])

                    load_oT(0)
                    for i_ in range(min(PF, len(TL))):
                        post_load(ctx, xcur[TL[i_][0], TL[i_][1] * 128:(TL[i_][1] + 1) * 128, :], i_ % NSL, 'o_')
                    NTL = len(TL)
                    for n in range(NTL + 3):
                        if n < NTL:
                            b, tt = TL[n]
                            if tt == 4 and b + 1 < NB:
                                load_oT(b + 1)
                            mm(n)
                        if 0 <= n - 3 < NTL:
                            p3b(n - 3)
                        if 0 <= n - 1 < NTL:
                            p2(n - 1)
                        if 0 <= n - 2 < NTL:
                            p3a(n - 2)
                        if n + PF < NTL:
                            b2, t2 = TL[n + PF]
                            post_load(ctx, xcur[b2, t2 * 128:(t2 + 1) * 128, :], (n + PF) % NSL, 'o_')
                        if n < NTL:
                            p1(n)
                    S.barrier()

            def stage_M(l=l, xcur=xcur, xnext=xnext):
                with ExitStack() as st:
                    A = st.enter_context
                    hT = A(SB("m_hT", [128, 8, S_], BF16))
                    wd = A(SB("m_wd", [128, NF, D_], BF16))
                    actT = A(SB("m_actT", [128, NF, S_], BF16))
                    gcol = A(SB("m_gcol", [128, 8], F32))
                    gB = A(SB("m_gB", [128, D_], F32))
                    cw = A(SB("m_cw", [128, NF, 3], F32))
                    cb = A(SB("m_cb", [128, NF], F32))
                    wg = [A(SB("m_wg%d" % i, [128, 8, 128], BF16)) for i in range(2)]
                    wu = [A(SB("m_wu%d" % i, [128, 8, 128], BF16)) for i in range(2)]
                    u0 = A(SB("m_u0", [128, S_], F32))
                    MS = 2
                    ctx = dict(xt=[A(SB("m_xt%d" % i, [128, D_], F32)) for i in range(MS)],
                               ssq=[A(SB("m_ssq%d" % i, [128, 1], F32)) for i in range(MS)],
                               yn=[A(SB("m_yn%d" % i, [128, D_], F32)) for i in range(MS)])
                    pp = A(PS("m_pp", [128, 8 * 512], F32))
                    pTb = [pp[:, o_:o_ + 512].bitcast(BF16).rearrange("p (c n) -> p c n", n=128) for o_ in (0, 2048)]
                    kpTb = [['mpg', 'mpd0'], ['mpu', 'mpd2']]
                    S.dma(gcol[:], g_fpre_col[l], 'mc', writes=['gcol'])
                    S.dma(gB[:], g_fpost[l:l + 1, :].broadcast_to([128, D_]), 'mc', writes=['gB'])
                    S.dma(cw[:], f_cw[l], 'mc', writes=['cw'])
                    S.dma(cb[:], f_cb[l], 'mc', writes=['cb'])
                    for f0 in range(0, NF, 2):
                        S.dma(wd[:, f0:f0 + 2, :], wbf_down[l, f0 * 128:(f0 + 2) * 128, :].rearrange("(f p) n -> p f n", p=128), 'md', writes=['wd'])
                    nw = 0
                    nd = 0
                    pend = []
                    for b in range(NB):
                        for c in range(8):
                            S.dma(hT[:, c, :], hTd[b, c * 128:(c + 1) * 128, :], 'mh', writes=['hT'])
                        for f in range(NF):
                            g_, u_ = wg[nw % 2], wu[nw % 2]
                            kw = 'wgu%d' % (nw % 2)
                            S.dma(g_[:], wbf_gate[l, :, f * 128:(f + 1) * 128].rearrange("(c p) n -> p c n", p=128), 'mw%d' % (nw % 2), writes=[kw])
                            S.dma(u_[:], wbf_up[l, :, f * 128:(f + 1) * 128].rearrange("(c p) n -> p c n", p=128), 'mw%d' % (nw % 2), writes=[kw])
                            nw += 1
                            for (wt, off, kps) in ((g_, 0, ['mpg', 'mpd0', 'mpd1']), (u_, 2048, ['mpu', 'mpd2', 'mpd3'])):
                                for n in range(4):
                                    for c in range(8):
                                        S.op('pe', lambda e, wt=wt, off=off, n=n, c=c: e.matmul(pp[:, off + n * 512:off + (n + 1) * 512], wt[:, c, :], hT[:, c, n * 512:(n + 1) * 512],
                                                                                                 start=(c == 0), stop=(c == 7)),
                                             reads=[kw, 'hT'], writes=kps)
                            gp = pp[:, 0:S_]
                            up = pp[:, S_:2 * S_]
                            S.op('act', lambda e, f=f, gp=gp: e.activation(u0[:], gp, AF.Identity, scale=cw[:, f, 1:2], bias=cb[:, f:f + 1]), reads=['mpg', 'cw', 'cb'], writes=['u0'])
                            S.op('dve', lambda e, f=f, gp=gp: e.scalar_tensor_tensor(u0[:, 1:S_], gp[:, 0:S_ - 1], cw[:, f, 0:1], u0[:, 1:S_], ALU.mult, ALU.add),
                                 reads=['mpg', 'u0', 'cw'], writes=['u0'])
                            S.op('dve', lambda e, f=f, gp=gp: e.scalar_tensor_tensor(u0[:, 0:S_ - 1], gp[:, 1:S_], cw[:, f, 2:3], u0[:, 0:S_ - 1], ALU.mult, ALU.add),
                                 reads=['mpg', 'u0', 'cw'], writes=['u0'])
                            S.op('act', lambda e: e.activation(u0[:], u0[:], AF.Gelu_apprx_tanh), reads=['u0'], writes=['u0'])
                            S.op('dve', lambda e, f=f, up=up: e.tensor_tensor(actT[:, f, :], u0[:], up, ALU.mult), reads=['u0', 'mpu'], writes=['actT'])
                        for tt in range(NT):
                            sl = nd % 4
                            ms = nd % MS
                            off = sl * 1024
                            kp = ['mpd%d' % sl, 'mpg' if sl < 2 else 'mpu']
                            for nn in range(2):
                                for f in range(NF):
                                    S.op('pe', lambda e, off=off, nn=nn, f=f, tt=tt: e.matmul(pp[:, off + nn * 512:off + (nn + 1) * 512], actT[:, f, tt * 128:(tt + 1) * 128],
                                                                                               wd[:, f, nn * 512:(nn + 1) * 512], start=(f == 0), stop=(f == NF - 1)),
                                         reads=['actT', 'wd'], writes=kp)
                            while pend:
                                pend.pop(0)()
                            if tt == 0:
                                post_load(ctx, xmid[b, 0:128, :], nd % MS, 'm_')
                            if tt + 1 < NT:
                                post_load(ctx, xmid[b, (tt + 1) * 128:(tt + 2) * 128, :], (nd + 1) % MS, 'm_')
                            post_part1(ctx, pp[:, off:off + 1024], ['mpd%d' % sl], None, ms, 'm_')
                            pend.append(lambda off=off, sl=sl, ms=ms, b=b, tt=tt: post_part2(ctx, pp[:, off:off + 1024], ['mpd%d' % sl], gB, xnext[b, tt * 128:(tt + 1) * 128, :], ms, 'm_'))
                            nd += 1
                        while pend:
                            pend.pop(0)()
                    S.barrier()
            if 'P' in stages:
                stage_P()
            if 'A' in stages:
                attention('A')
            if 'C' in stages:
                attention('C')
            if 'F' in stages:
                stage_F()
            if 'H' in stages:
                stage_H()
            if 'O' in stages:
                stage_O()
            if 'M' in stages:
                stage_M()
    S.emit()
    return nc


def make_inputs(inputs, NL=2):
    f = lambda a: np.ascontiguousarray(np.asarray(a, dtype=np.float32))
    c = host_consts()
    m = {}
    for k in ("w_in", "w_out", "ffn_w_gate", "ffn_w_up", "ffn_w_down", "hy_w1", "hy_w2", "hy_w3"):
        m[k] = f(inputs[k])[:NL]
    m["g_mix_pre_col"] = f(f(inputs["g_mix_pre"])[:NL].reshape(NL, 8, 128).transpose(0, 2, 1))
    m["g_ffn_pre_col"] = f(f(inputs["g_ffn_pre"])[:NL].reshape(NL, 8, 128).transpose(0, 2, 1))
    m["g_mix_post"] = f(inputs["g_mix_post"])[:NL]
    m["g_ffn_post"] = f(inputs["g_ffn_post"])[:NL]
    gq, gk = f(inputs["g_q"])[:NL], f(inputs["g_k"])[:NL]
    m["g_qk"] = f(np.concatenate([np.tile(gq, (1, 6)), np.tile(gk, (1, 2))], axis=1))
    m["hy_cw"] = f(f(inputs["hy_conv_w"])[:NL].reshape(NL, 3, 6, 128).transpose(0, 3, 2, 1))
    m["hy_cb"] = f(f(inputs["hy_conv_b"])[:NL].reshape(NL, 6, 128).transpose(0, 2, 1))
    m["hy_b1c"] = f(f(inputs["hy_b1"])[:NL].reshape(NL, 64, 1))
    m["hy_b2c"] = f(f(inputs["hy_b2"])[:NL].reshape(NL, 64, 1))
    m["hy_freqc"] = f(f(inputs["hy_freq"])[:NL].transpose(0, 2, 1))
    m["hy_decay"] = f(f(inputs["hy_decay"])[:NL].reshape(NL, 1024))
    m["hy_d"] = f(f(inputs["hy_d"])[:NL].reshape(NL, 512))
    m["ffn_cw"] = f(f(inputs["ffn_conv_w"])[:NL].reshape(NL, 3, NF, 128).transpose(0, 3, 2, 1))
    m["ffn_cb"] = f(f(inputs["ffn_conv_b"])[:NL].reshape(NL, NF, 128).transpose(0, 2, 1))
    for k in ("ident", "ropeA_c", "ropeA_s", "ropeC_c", "ropeC_s", "maskW", "zT", "tneg", "fw", "inv", "ones_bf", "ones_f"):
        m[k] = c[k]
    return m


_PROG = {}


def kernel(**inputs):
    NB = 4
    x = np.ascontiguousarray(np.asarray(inputs["x"], dtype=np.float32))
    shared = make_inputs(inputs)
    if 'nc' not in _PROG:
        _PROG['nc'] = build_program(NB=NB, NL=2)
    nc = _PROG['nc']
    in_maps = []
    for c in range(NCORES):
        d = dict(shared)
        d["x"] = np.ascontiguousarray(x[c * NB:(c + 1) * NB])
        in_maps.append(d)
    res = run_bass_kernel_spmd(nc, in_maps, core_ids=list(range(NCORES)))
    return np.concatenate([np.asarray(r["out"], dtype=np.float32) for r in res.results], axis=0)
```

```python
import math
from contextlib import ExitStack

import numpy as np
import ml_dtypes

import concourse.bass as bass
import concourse.mybir as mybir
from concourse.bass_utils import run_bass_kernel_spmd

F32 = mybir.dt.float32
BF16 = mybir.dt.bfloat16
AF = mybir.ActivationFunctionType
ALU = mybir.AluOpType
AX = mybir.AxisListType

S_ = 2048
D_ = 1024
NT = 16
PW = 2560
DFF = 2816
NF = 22
EPS = 1e-6
NCORES = 8
ATT_NDUM = 0
A_RECIP_ACT = True

COMPUTE = ('pe', 'act', 'dve', 'pool')
ENGS = ('pe', 'act', 'dve', 'pool', 'sp')


class _Op:
    __slots__ = ('eng', 'fn', 'deps', 'need', 'val', 'dsem', 'is_dma', 'bar')

    def __init__(self, eng, fn):
        self.eng = eng
        self.fn = fn
        self.deps = []
        self.need = False
        self.val = None
        self.dsem = None
        self.is_dma = False
        self.bar = None


class Sched:
    def __init__(self, nc):
        self.nc = nc
        self.eops = {e: [] for e in ENGS}
        self.last_w = {}
        self.readers = {}
        self.dma_cnt = {}
        self.dma_last = {}
        self.n_ops = 0

    def _track(self, op, reads, writes):
        deps = {}
        for k in reads:
            w = self.last_w.get(k)
            if w is not None:
                deps[id(w)] = w
        for k in writes:
            w = self.last_w.get(k)
            if w is not None:
                deps[id(w)] = w
            for r in self.readers.get(k, {}).values():
                if isinstance(r, list):
                    for rr in r:
                        deps[id(rr)] = rr
                else:
                    deps[id(r)] = r
        for k in reads:
            d = self.readers.setdefault(k, {})
            if op.is_dma:
                d.setdefault('dma', []).append(op)
            else:
                d[op.eng] = op
        for k in writes:
            self.last_w[k] = op
            self.readers[k] = {}
        for d in deps.values():
            if d is op:
                continue
            if (not d.is_dma) and d.eng == 'pe' and op.eng == 'pe' and not op.is_dma:
                continue
            d.need = True
            op.deps.append((d, self.dma_cnt[d.dsem] if d.is_dma else None))

    def op(self, eng, fn, reads=(), writes=()):
        o = _Op(eng, fn)
        self._track(o, reads, writes)
        self.eops[eng].append(o)
        self.n_ops += 1
        return o

    def dma(self, out, in_, sem, reads=(), writes=(), eng='sp', **kw):
        fn = lambda e, o=out, i=in_: e.dma_start(out=o, in_=i, **kw)
        o = _Op(eng, fn)
        o.is_dma = True
        o.dsem = sem
        self.dma_cnt.setdefault(sem, 0)
        self._track(o, reads, writes)
        self.dma_cnt[sem] += 16
        o.val = self.dma_cnt[sem]
        o.need = True
        self.eops[eng].append(o)
        self.dma_last[sem] = o
        self.n_ops += 1
        return o

    def barrier(self):
        lasts = []
        for e in COMPUTE:
            for o in reversed(self.eops[e]):
                if not o.is_dma and o.bar is None:
                    o.need = True
                    lasts.append(o)
                    break
        dl = list(self.dma_last.values())
        for e in ENGS:
            b = _Op(e, None)
            b.bar = True
            b.deps = [(o, None) for o in lasts] + [(o, None) for o in dl]
            self.eops[e].append(b)
        self.last_w = {}
        self.readers = {}

    def emit(self):
        nc = self.nc
        for e in COMPUTE:
            c = 0
            for o in self.eops[e]:
                if o.is_dma or o.bar:
                    continue
                if o.need:
                    c += 1
                    o.val = c
        with ExitStack() as es:
            sems = {}
            for e in COMPUTE:
                sems[e] = es.enter_context(nc.semaphore("s_" + e))
            for k in self.dma_cnt:
                sems['d_' + k] = es.enter_context(nc.semaphore("d_" + k))
            block = es.enter_context(nc.Block())

            def semof(o):
                return sems['d_' + o.dsem] if o.is_dma else sems[o.eng]

            def run(engname, eh):
                waited = {}
                for o in self.eops[engname]:
                    need = {}
                    for d, ov in o.deps:
                        s = semof(d)
                        dv = ov if ov is not None else d.val
                        if dv > need.get(s.num, (0, None))[0]:
                            need[s.num] = (dv, s)
                    for key, (v, s) in need.items():
                        if waited.get(key, 0) >= v:
                            continue
                        eh.wait_ge(s, v)
                        waited[key] = v
                    if o.bar:
                        continue
                    ins = o.fn(eh)
                    if o.is_dma:
                        ins.then_inc(sems['d_' + o.dsem], 16)
                    elif o.need:
                        ins.then_inc(sems[o.eng], 1)

            @block.tensor
            def _(eh):
                run('pe', eh)

            @block.scalar
            def _(eh):
                run('act', eh)

            @block.vector
            def _(eh):
                run('dve', eh)

            @block.gpsimd
            def _(eh):
                run('pool', eh)

            @block.sync
            def _(eh):
                run('sp', eh)


def _tok_tiled(a):
    return np.ascontiguousarray(a.reshape(NT, 128, -1).transpose(1, 0, 2))


_CONST_CACHE = {}


def host_consts():
    if _CONST_CACHE:
        return _CONST_CACHE
    c = {}
    bf = ml_dtypes.bfloat16
    c['ident'] = np.eye(128, dtype=np.float32).astype(bf)
    pos = np.arange(S_, dtype=np.float32)
    fr = (10000.0 ** (-np.arange(0, 64, 2, dtype=np.float32) / 64)).astype(np.float32)
    ang = pos[:, None] * fr[None, :]
    co, si = np.cos(ang), np.sin(ang)
    c['ropeA_c'] = _tok_tiled(np.concatenate([co, co], 1).astype(np.float32))
    c['ropeA_s'] = _tok_tiled(np.concatenate([-si, si], 1).astype(np.float32))
    rows = (np.arange(S_) // 64).astype(np.float32)
    cols = (np.arange(S_) % 64).astype(np.float32)
    fr2 = (10000.0 ** (-np.arange(0, 32, 2, dtype=np.float32) / 32)).astype(np.float32)
    ar, ac = rows[:, None] * fr2[None, :], cols[:, None] * fr2[None, :]
    c['ropeC_c'] = _tok_tiled(np.concatenate([np.cos(ar), np.cos(ar), np.cos(ac), np.cos(ac)], 1).astype(np.float32))
    c['ropeC_s'] = _tok_tiled(np.concatenate([-np.sin(ar), np.sin(ar), -np.sin(ac), np.sin(ac)], 1).astype(np.float32))
    d = np.arange(128)[:, None] - np.arange(3968)[None, :] + 1920
    ad = np.abs(d)
    m = (ad <= 64).astype(np.float32) + ((d % 4 == 0) & (ad <= 256)) + ((d % 16 == 0) & (ad <= 1024))
    c['maskW'] = m.astype(bf)
    L = S_
    t = np.linspace(0.0, 1.0, L, dtype=np.float32)
    bands = np.linspace(1e-4, 15, 16, dtype=np.float32)
    angz = (2.0 * math.pi * bands[None, :] * np.arange(L, dtype=np.float32)[:, None] / L).astype(np.float32)
    z = np.concatenate([t[:, None], np.cos(angz), -np.sin(angz)], -1).astype(np.float32)
    c['zT'] = np.ascontiguousarray(z.T)
    c['tneg'] = np.ascontiguousarray((-t).reshape(NT, 128).T.astype(np.float32))
    N = 2 * L
    tt = np.arange(L, dtype=np.float64)[:, None]
    kk = (np.arange(L, dtype=np.float64)[None, :] + 0.5)
    w = 2.0 * math.pi * tt * kk / N
    Fw = np.concatenate([np.cos(w), -np.sin(w)], 1)
    c['fw'] = np.ascontiguousarray(Fw.reshape(NT, 128, 32, 128).transpose(2, 1, 0, 3)).astype(np.float32).astype(bf)
    c['inv'] = np.ascontiguousarray(((2.0 / N) * Fw).reshape(NT, 128, 32, 128).transpose(0, 3, 2, 1)).astype(np.float32).astype(bf)
    c['ones_bf'] = np.ones((128, 128), dtype=np.float32).astype(bf)
    c['ones_f'] = np.ones((128, 128), dtype=np.float32)
    _CONST_CACHE.update(c)
    return c


def build_program(NB=4, NL=2, stages="WPACFHOM", dbg=()):
    nc = bass.Bass("TRN2", target_bir_lowering=False)
    S = Sched(nc)
    _cnt = [0]

    def SB(name, shape, dt):
        _cnt[0] += 1
        return nc.sbuf_tensor("%s_u%d" % (name, _cnt[0]), shape, dt)

    def PS(name, shape, dt):
        _cnt[0] += 1
        return nc.psum_tensor("%s_u%d" % (name, _cnt[0]), shape, dt)

    def din(name, shape, dt=F32):
        return nc.dram_tensor(name, list(shape), dt, kind="ExternalInput").ap()

    def dscr(name, shape, dt=F32):
        kind = "ExternalOutput" if name in dbg else "Internal"
        return nc.dram_tensor(name, list(shape), dt, kind=kind).ap()

    x_in = din("x", [NB, S_, D_])
    w_in = din("w_in", [NL, D_, PW])
    w_out = din("w_out", [NL, D_, D_])
    w_gate = din("ffn_w_gate", [NL, D_, DFF])
    w_up = din("ffn_w_up", [NL, D_, DFF])
    w_down = din("ffn_w_down", [NL, DFF, D_])
    g_pre_col = din("g_mix_pre_col", [NL, 128, 8])
    g_fpre_col = din("g_ffn_pre_col", [NL, 128, 8])
    g_post = din("g_mix_post", [NL, D_])
    g_fpost = din("g_ffn_post", [NL, D_])
    g_qk = din("g_qk", [NL, 8 * 64])
    hy_cw = din("hy_cw", [NL, 128, 6, 3])
    hy_cb = din("hy_cb", [NL, 128, 6])
    hy_w1 = din("hy_w1", [NL, 33, 64])
    hy_b1 = din("hy_b1c", [NL, 64, 1])
    hy_fr = din("hy_freqc", [NL, 64, 2])
    hy_w2 = din("hy_w2", [NL, 64, 64])
    hy_b2 = din("hy_b2c", [NL, 64, 1])
    hy_w3 = din("hy_w3", [NL, 64, 1024])
    hy_dec = din("hy_decay", [NL, 1024])
    hy_d = din("hy_d", [NL, 512])
    f_cw = din("ffn_cw", [NL, 128, NF, 3])
    f_cb = din("ffn_cb", [NL, 128, NF])
    c_ident = din("ident", [128, 128], BF16)
    c_rAc = din("ropeA_c", [128, NT, 64])
    c_rAs = din("ropeA_s", [128, NT, 64])
    c_rCc = din("ropeC_c", [128, NT, 64])
    c_rCs = din("ropeC_s", [128, NT, 64])
    c_mask = din("maskW", [128, 3968], BF16)
    c_zT = din("zT", [33, S_])
    c_tneg = din("tneg", [128, NT])
    c_fw = din("fw", [32, 128, NT, 128], BF16)
    c_inv = din("inv", [NT, 128, 32, 128], BF16)
    c_ones_bf = din("ones_bf", [128, 128], BF16)
    c_ones_f = din("ones_f", [128, 128])

    out = nc.dram_tensor("out", [NB, S_, D_], F32, kind="ExternalOutput").ap()

    wbf_in = dscr("wbf_in", [NL, D_, PW], BF16)
    wbf_out = dscr("wbf_out", [NL, D_, D_], BF16)
    wbf_gate = dscr("wbf_gate", [NL, D_, DFF], BF16)
    wbf_up = dscr("wbf_up", [NL, D_, DFF], BF16)
    wbf_down = dscr("wbf_down", [NL, DFF, D_], BF16)
    qTA = dscr("qTA", [NB, 3, 128, S_], BF16)
    kTA = dscr("kTA", [NB, 3, 128, S_], BF16)
    vA = dscr("vA", [NB, 128, NT, 6 * 65], BF16)
    qTC = dscr("qTC", [NB, 3, 128, S_], BF16)
    kTC = dscr("kTC", [NB, 128, S_], BF16)
    vC = dscr("vC", [NB, 128, NT, 2 * 65], BF16)
    hyraw = dscr("hyraw", [NB, 6, 128, S_], BF16)
    oT = dscr("oT", [NB, D_, S_], BF16)
    x1tm = dscr("x1tm", [NT, 128, NB * 256], BF16)
    x2tm = dscr("x2tm", [NT, 128, NB * 256], BF16)
    kfd = dscr("kfd", [32, 128, 512], F32)
    xmid = dscr("xmid", [NB, S_, D_], F32)
    xl = [x_in] + [dscr("xl%d" % i, [NB, S_, D_], F32) for i in range(1, NL)] + [out]
    xl[NL] = out

    with ExitStack() as top:
        T = top.enter_context
        ident = T(SB("sb_ident", [128, 128], BF16))
        ones_bf = T(SB("sb_ones_bf", [128, 128], BF16))
        ones_f = T(SB("sb_ones_f", [128, 128], F32))
        S.dma(ident[:], c_ident[:, :], 'c0', writes=['ident'])
        S.dma(ones_bf[:], c_ones_bf[:, :], 'c0', writes=['ones_bf'])
        S.dma(ones_f[:], c_ones_f[:, :], 'c0', writes=['ones_f'])
        S.barrier()

        if 'W' in stages:
            jobs_w = []
            for l in range(NL):
                for i, (dst, src, rows) in enumerate(((wbf_in, w_in, D_), (wbf_out, w_out, D_), (wbf_gate, w_gate, D_),
                                                      (wbf_up, w_up, D_), (wbf_down, w_down, DFF))):
                    jobs_w.append((l, i, dst, src, rows))
            for wi, (l, i, dst, src, rows) in enumerate(jobs_w):
                nchunk = 8 if rows == D_ else 11
                rc = rows // nchunk
                for ch in range(nchunk):
                    S.dma(dst[l, ch * rc:(ch + 1) * rc, :], src[l, ch * rc:(ch + 1) * rc, :], 'wc%d' % (i % 2), eng='pool',
                          writes=[])
                if wi == 0:
                    S.barrier()

        def norm_transpose(ctx, xsrc_tile, gcol, hT_dst, key_hT, slot, pT, pfx, kpT=None):
            xt, sq, ssq, xn = ctx['xt'][slot], ctx['sq'], ctx['ssq'][slot], ctx['xn'][slot]
            kx, kn = pfx + 'xt%d' % slot, pfx + 'xn%d' % slot
            kpT = kpT or (pfx + 'pT')
            S.dma(xt[:], xsrc_tile, pfx + 'x%d' % slot, writes=[kx])
            S.op('pool', lambda e: e.memset(ssq[:], 0.0), writes=[pfx + 'ssq%d' % slot])
            S.op('act', lambda e: e.activation(sq[:], xt[:], AF.Square, accum_out=ssq[:]), reads=[kx], writes=[pfx + 'sq', pfx + 'ssq%d' % slot])
            S.op('dve', lambda e: e.tensor_scalar(ssq[:], ssq[:], 1.0 / D_, EPS, ALU.mult, ALU.add), reads=[pfx + 'ssq%d' % slot], writes=[pfx + 'ssq%d' % slot])
            S.op('act', lambda e: e.activation(ssq[:], ssq[:], AF.Sqrt), reads=[pfx + 'ssq%d' % slot], writes=[pfx + 'ssq%d' % slot])
            S.op('dve', lambda e: e.reciprocal(ssq[:], ssq[:]), reads=[pfx + 'ssq%d' % slot], writes=[pfx + 'ssq%d' % slot])
            S.op('act', lambda e: e.activation(xn[:], xt[:], AF.Copy, scale=ssq[:]), reads=[kx, pfx + 'ssq%d' % slot], writes=[kn])
            for c in range(8):
                S.op('pe', lambda e, c=c: e.transpose(pT[:, c, :], xn[:, c * 128:(c + 1) * 128], ident[:]), reads=[kn, 'ident'], writes=[kpT])
            S.op('dve', lambda e: e.tensor_tensor(hT_dst, pT[:, :, :], gcol[:, :].unsqueeze(2).broadcast_to([128, 8, 128]), ALU.mult),
                 reads=[kpT, 'gcol'], writes=[key_hT])

        def norm_partA(ctx, jobs, pfx):
            for (src, slot) in jobs:
                xt, ssq = ctx['xt'][slot], ctx['ssq'][slot]
                S.dma(xt[:], src, pfx + 'x%d' % slot, writes=[pfx + 'xt%d' % slot])
            for (src, slot) in jobs:
                xt, ssq, xn = ctx['xt'][slot], ctx['ssq'][slot], ctx['xn'][slot]
                S.op('act', lambda e, xt=xt, ssq=ssq, xn=xn: e.activation(xn[:], xt[:], AF.Square, accum_out=ssq[:]),
                     reads=[pfx + 'xt%d' % slot], writes=[pfx + 'xn%d' % slot, pfx + 'ssq%d' % slot])
            for (src, slot) in jobs:
                ssq = ctx['ssq'][slot]
                S.op('dve', lambda e, ssq=ssq: e.tensor_scalar(ssq[:], ssq[:], 1.0 / D_, EPS, ALU.mult, ALU.add), reads=[pfx + 'ssq%d' % slot], writes=[pfx + 'ssq%d' % slot])
            for (src, slot) in jobs:
                ssq = ctx['ssq'][slot]
                S.op('act', lambda e, ssq=ssq: e.activation(ssq[:], ssq[:], AF.Sqrt), reads=[pfx + 'ssq%d' % slot], writes=[pfx + 'ssq%d' % slot])
            for (src, slot) in jobs:
                ssq = ctx['ssq'][slot]
                S.op('dve', lambda e, ssq=ssq: e.reciprocal(ssq[:], ssq[:]), reads=[pfx + 'ssq%d' % slot], writes=[pfx + 'ssq%d' % slot])
            for (src, slot) in jobs:
                xt, ssq, xn = ctx['xt'][slot], ctx['ssq'][slot], ctx['xn'][slot]
                S.op('act', lambda e, xt=xt, ssq=ssq, xn=xn: e.activation(xn[:], xt[:], AF.Copy, scale=ssq[:]),
                     reads=[pfx + 'xt%d' % slot, pfx + 'ssq%d' % slot], writes=[pfx + 'xn%d' % slot])

        def norm_partB(ctx, slot, gcol, hT_dst, key_hT, pT, kpT, pfx):
            xn = ctx['xn'][slot]
            for c in range(8):
                S.op('pe', lambda e, c=c: e.transpose(pT[:, c, :], xn[:, c * 128:(c + 1) * 128], ident[:]), reads=[pfx + 'xn%d' % slot, 'ident'], writes=kpT)
            S.op('dve', lambda e: e.tensor_tensor(hT_dst, pT[:, :, :], gcol[:, :].unsqueeze(2).broadcast_to([128, 8, 128]), ALU.mult),
                 reads=list(kpT) + ['gcol'], writes=[key_hT])

        def post_load(ctx, xsrc_tile, slot, pfx):
            S.dma(ctx['xt'][slot][:], xsrc_tile, pfx + 'x%d' % slot, writes=[pfx + 'xt%d' % slot])

        def post_part1(ctx, ps, kps, xsrc_tile, slot, pfx):
            xt, ssq, yn = ctx['xt'][slot], ctx['ssq'][slot], ctx['yn'][slot]
            kx, ks, ky = pfx + 'xt%d' % slot, pfx + 'ssq%d' % slot, pfx + 'yn%d' % slot
            if xsrc_tile is not None:
                S.dma(xt[:], xsrc_tile, pfx + 'x%d' % slot, writes=[kx])
            S.op('act', lambda e: e.activation(yn[:], ps, AF.Square, accum_out=ssq[:]), reads=list(kps), writes=[ky, ks])
            S.op('dve', lambda e: e.tensor_scalar(ssq[:], ssq[:], 1.0 / D_, EPS, ALU.mult, ALU.add), reads=[ks], writes=[ks])
            S.op('act', lambda e: e.activation(ssq[:], ssq[:], AF.Sqrt), reads=[ks], writes=[ks])
            S.op('dve', lambda e: e.reciprocal(ssq[:], ssq[:]), reads=[ks], writes=[ks])

        def post_part2(ctx, ps, kps, gB, dst_tile, slot, pfx):
            xt, ssq, yn = ctx['xt'][slot], ctx['ssq'][slot], ctx['yn'][slot]
            kx, ks, ky = pfx + 'xt%d' % slot, pfx + 'ssq%d' % slot, pfx + 'yn%d' % slot
            S.op('dve', lambda e: e.scalar_tensor_tensor(yn[:], ps, ssq[:], gB[:], ALU.mult, ALU.mult), reads=list(kps) + [ks, 'gB'], writes=[ky])
            S.op('pool', lambda e: e.tensor_tensor(yn[:], yn[:], xt[:], ALU.add), reads=[ky, kx], writes=[ky])
            S.dma(dst_tile, yn[:], pfx + 'st%d' % slot, reads=[ky], writes=[])

        def post_norm_residual(ctx, ps, kps, gB, xsrc_tile, dst_tile, slot, pfx):
            xt, sq, ssq, yn = ctx['xt'][slot], ctx['sq'], ctx['ssq'][slot], ctx['yn'][slot]
            kx = pfx + 'rx%d' % slot
            ks = pfx + 'rs%d' % slot
            ky = pfx + 'ry%d' % slot
            S.dma(xt[:], xsrc_tile, pfx + 'rx%d' % slot, reads=[pfx + 'st%d' % slot], writes=[kx])
            S.op('pool', lambda e: e.memset(ssq[:], 0.0), writes=[ks])
            S.op('act', lambda e: e.activation(sq[:], ps, AF.Square, accum_out=ssq[:]), reads=[kps], writes=[pfx + 'sq', ks])
            S.op('dve', lambda e: e.tensor_scalar(ssq[:], ssq[:], 1.0 / D_, EPS, ALU.mult, ALU.add), reads=[ks], writes=[ks])
            S.op('act', lambda e: e.activation(ssq[:], ssq[:], AF.Sqrt), reads=[ks], writes=[ks])
            S.op('dve', lambda e: e.reciprocal(ssq[:], ssq[:]), reads=[ks], writes=[ks])
            S.op('dve', lambda e: e.scalar_tensor_tensor(yn[:], ps, ssq[:], gB[:], ALU.mult, ALU.mult), reads=[kps, ks, 'gB'], writes=[ky])
            S.op('pool', lambda e: e.tensor_tensor(yn[:], yn[:], xt[:], ALU.add), reads=[ky, kx], writes=[ky])
            S.dma(dst_tile, yn[:], pfx + 'st%d' % slot, reads=[ky], writes=[pfx + 'st%d' % slot])

        for l in range(NL):
            xcur = xl[l]
            xnext = xl[l + 1]
            def stage_P(l=l, xcur=xcur, xnext=xnext):
                with ExitStack() as st:
                    A = st.enter_context
                    wsb = A(SB("p_w", [128, 8, PW], BF16))
                    gcol = A(SB("p_gcol", [128, 8], F32))
                    rAc = A(SB("p_rAc", [128, NT, 64], F32))
                    rAs = A(SB("p_rAs", [128, NT, 64], F32))
                    rCc = A(SB("p_rCc", [128, NT, 64], F32))
                    rCs = A(SB("p_rCs", [128, NT, 64], F32))
                    g8 = A(SB("p_g8", [128, 8, 64], F32))
                    ctx = dict(xt=[A(SB("p_xt%d" % i, [128, D_], F32)) for i in range(8)],
                               ssq=[A(SB("p_ssq%d" % i, [128, 1], F32)) for i in range(8)],
                               xn=[A(SB("p_xn%d" % i, [128, D_], BF16)) for i in range(8)])
                    hT2 = [A(SB("p_hT%d" % i, [128, 8, 512], BF16)) for i in range(2)]
                    tsets = [[(A(SB("p_t1_%d" % i, [128, 8, 64], F32)), A(SB("p_t2_%d" % i, [128, 8, 64], F32)), A(SB("p_t3_%d" % i, [128, 8, 64], F32)))
                              for i in range(6)]][0]
                    ss8 = A(SB("p_ss8", [128, 8], F32))
                    qka2 = [A(SB("p_qka%d" % i, [128, 12, 64], BF16)) for i in range(2)]
                    qkc2 = [A(SB("p_qkc%d" % i, [128, 8, 64], BF16)) for i in range(2)]
                    vAs = [A(SB("p_vA%d" % i, [128, 6, 65], BF16)) for i in range(2)]
                    vCs = [A(SB("p_vC%d" % i, [128, 2, 65], BF16)) for i in range(2)]
                    qTs2 = [A(SB("p_qTs%d" % i, [128, 10, 512], BF16)) for i in range(2)]
                    hys = [A(SB("p_hys%d" % i, [128, 512], BF16)) for i in range(2)]
                    pT = A(PS("p_pT", [128, 8, 128], BF16))
                    pj = [A(PS("p_pj%d" % i, [128, 512], F32)) for i in range(3)]
                    pq = [A(PS("p_pq%d" % i, [128, 8, 128], BF16)) for i in range(2)]
                    ph = [A(PS("p_ph%d" % i, [128, 512], F32)) for i in range(2)]

                    S.dma(wsb[:], wbf_in[l].rearrange("(c p) n -> p c n", p=128), 'pw', writes=['wsb'])
                    S.dma(gcol[:], g_pre_col[l], 'pc', writes=['gcol'])
                    S.dma(rAc[:], c_rAc[:, :, :], 'pc', writes=['rAc'])
                    S.dma(rAs[:], c_rAs[:, :, :], 'pc', writes=['rAs'])
                    S.dma(rCc[:], c_rCc[:, :, :], 'pc', writes=['rCc'])
                    S.dma(rCs[:], c_rCs[:, :, :], 'pc', writes=['rCs'])
                    S.dma(g8[:].rearrange("p h e -> p (h e)"), g_qk[l:l + 1, :].broadcast_to([128, 512]), 'pc', writes=['g8'])
                    for i in range(2):
                        S.op('pool', lambda e, i=i: e.memset(vAs[i][:], 1.0), writes=['vAs%d' % i])
                        S.op('pool', lambda e, i=i: e.memset(vCs[i][:], 1.0), writes=['vCs%d' % i])

                    groups = [
                        [(0, 0, 384)],
                        [(0, 384, 384)],
                        [(0, 768, 384), (384, 2432, 128)],
                        [(0, 1920, 512)],
                    ]
                    pjn = 0
                    phn = 0
                    GL = [(b, tg) for b in range(NB) for tg in range(4)]
                    pend_tr = []

                    def grp_jobs(k):
                        b, tg = GL[k]
                        return [(xcur[b, (tg * 4 + ti) * 128:(tg * 4 + ti + 1) * 128, :], (k % 2) * 4 + ti) for ti in range(4)]

                    def partB(k, ti):
                        norm_partB(ctx, (k % 2) * 4 + ti, gcol, hT2[k % 2][:, :, ti * 128:(ti + 1) * 128], 'hT%d_%d' % (k % 2, ti), pT, ['p_pT'], 'p_')

                    norm_partA(ctx, grp_jobs(0), 'p_')
                    for ti in range(4):
                        partB(0, ti)
                    for k, (b, tg) in enumerate(GL):
                            hT = hT2[k % 2]
                            qTs = qTs2[k % 2]
                            kq_ = 'qTs%d' % (k % 2)
                            hk = ['hT%d_%d' % (k % 2, i_) for i_ in range(4)]
                            if k + 1 < len(GL):
                                norm_partA(ctx, grp_jobs(k + 1), 'p_')
                            for ti in range(4):
                                tt = tg * 4 + ti
                                slot = tt % 2
                                lhs = lambda c, ti=ti, hT=hT: hT[:, c, ti * 128:(ti + 1) * 128]
                                qka, qkc = qka2[ti % 2], qkc2[ti % 2]
                                kqa, kqc = 'qka%d' % (ti % 2), 'qkc%d' % (ti % 2)
                                pss = []
                                for gi, grp in enumerate(groups):
                                    tsi = (ti % 2) * 3 + (gi if gi < 2 else 2)
                                    t1, t2, t3 = tsets[tsi]
                                    k1_, k2_, k3_ = 't1_%d' % tsi, 't2_%d' % tsi, 't3_%d' % tsi
                                    ps = pj[pjn % 3]
                                    kp = 'pj%d' % (pjn % 3)
                                    pjn += 1
                                    for (po, wo, wd) in grp:
                                        for c in range(8):
                                            S.op('pe', lambda e, ps=ps, po=po, wo=wo, wd=wd, c=c, lhs=lhs: e.matmul(
                                                ps[:, po:po + wd], lhs(c), wsb[:, c, wo:wo + wd], start=(c == 0), stop=(c == 7)),
                                                reads=[hk[ti], 'wsb'], writes=[kp])
                                    if gi in (0, 1):
                                        x3 = ps[:, 0:384].rearrange("p (h e) -> p h e", e=64)
                                        cA = rAc[:, tt, :].unsqueeze(1).broadcast_to([128, 6, 64])
                                        sAlo = rAs[:, tt, 0:32].unsqueeze(1).broadcast_to([128, 6, 32])
                                        sAhi = rAs[:, tt, 32:64].unsqueeze(1).broadcast_to([128, 6, 32])
                                        S.op('dve', lambda e, x3=x3, cA=cA, t1=t1: e.tensor_tensor(t1[:, 0:6, :], x3, cA, ALU.mult), reads=[kp, 'rAc'], writes=[k1_])
                                        S.op('dve', lambda e, x3=x3, sAlo=sAlo, t2=t2: e.tensor_tensor(t2[:, 0:6, 0:32], x3[:, :, 32:64], sAlo, ALU.mult), reads=[kp, 'rAs'], writes=[k2_])
                                        S.op('dve', lambda e, x3=x3, sAhi=sAhi, t2=t2: e.tensor_tensor(t2[:, 0:6, 32:64], x3[:, :, 0:32], sAhi, ALU.mult), reads=[kp, 'rAs'], writes=[k2_])
                                        S.op('pool', lambda e, gi=gi, qka=qka, t1=t1, t2=t2: e.tensor_tensor(qka[:, gi * 6:(gi + 1) * 6, :], t1[:, 0:6, :], t2[:, 0:6, :], ALU.add),
                                             reads=[k1_, k2_], writes=[kqa])
                                    elif gi == 2:
                                        va, vc = vAs[slot], vCs[slot]
                                        S.op('act', lambda e, ps=ps, va=va: e.activation(va[:, :, 0:64], ps[:, 0:384].rearrange("p (h e) -> p h e", e=64), AF.Copy),
                                             reads=[kp, 'vst%d' % slot], writes=['vAs%d' % slot])
                                        S.op('act', lambda e, ps=ps, vc=vc: e.activation(vc[:, :, 0:64], ps[:, 384:512].rearrange("p (h e) -> p h e", e=64), AF.Copy),
                                             reads=[kp, 'vst%d' % slot], writes=['vCs%d' % slot])
                                        S.dma(vA[b, :, tt, :], va[:].rearrange("p h e -> p (h e)"), 'pv%d' % slot, reads=['vAs%d' % slot], writes=['vst%d' % slot])
                                        S.dma(vC[b, :, tt, :], vc[:].rearrange("p h e -> p (h e)"), 'pv%d' % slot, reads=['vCs%d' % slot], writes=['vst%d' % slot])
                                    else:
                                        x3 = ps[:, :].rearrange("p (h e) -> p h e", e=64)
                                        S.op('act', lambda e, x3=x3, t1=t1: e.activation(t1[:], x3, AF.Square), reads=[kp], writes=[k1_])
                                        S.op('dve', lambda e, t1=t1: e.reduce_sum(ss8[:], t1[:], axis=AX.X), reads=[k1_], writes=['ss8'])
                                        S.op('dve', lambda e: e.tensor_scalar(ss8[:], ss8[:], 1.0 / 64, EPS, ALU.mult, ALU.add), reads=['ss8'], writes=['ss8'])
                                        S.op('act', lambda e: e.activation(ss8[:], ss8[:], AF.Sqrt), reads=['ss8'], writes=['ss8'])
                                        S.op('dve', lambda e: e.reciprocal(ss8[:], ss8[:]), reads=['ss8'], writes=['ss8'])
                                        S.op('dve', lambda e, x3=x3, t3=t3: e.tensor_tensor(t3[:], x3, ss8[:, :].unsqueeze(2).broadcast_to([128, 8, 64]), ALU.mult),
                                             reads=[kp, 'ss8'], writes=[k3_])
                                        S.op('dve', lambda e, t3=t3: e.tensor_tensor(t3[:], t3[:], g8[:], ALU.mult), reads=[k3_, 'g8'], writes=[k3_])
                                        cC = rCc[:, tt, :].unsqueeze(1).broadcast_to([128, 8, 64])
                                        S.op('dve', lambda e, cC=cC, t1=t1, t3=t3: e.tensor_tensor(t1[:], t3[:], cC, ALU.mult), reads=[k3_, 'rCc'], writes=[k1_])
                                        t3v = t3[:].rearrange("p h (a b c) -> p h a b c", a=2, b=2)
                                        t2v = t2[:].rearrange("p h (a b c) -> p h a b c", a=2, b=2)
                                        sv = rCs[:, tt, :].rearrange("p (a b c) -> p a b c", a=2, b=2)
                                        for hb in range(2):
                                            sC = sv[:, :, hb, :].unsqueeze(1).broadcast_to([128, 8, 2, 16])
                                            S.op('dve', lambda e, hb=hb, sC=sC, t3v=t3v, t2v=t2v: e.tensor_tensor(t2v[:, :, :, hb, :], t3v[:, :, :, 1 - hb, :], sC, ALU.mult),
                                                 reads=[k3_, 'rCs'], writes=[k2_])
                                        S.op('pool', lambda e, qkc=qkc, t1=t1, t2=t2: e.tensor_tensor(qkc[:, 0:6, :].rearrange("p (j two) e -> p two j e", two=2),
                                                                               t1[:, 0:6, :].rearrange("p (two j) e -> p two j e", two=2),
                                                                               t2[:, 0:6, :].rearrange("p (two j) e -> p two j e", two=2), ALU.add),
                                             reads=[k1_, k2_], writes=[kqc])
                                        S.op('pool', lambda e, qkc=qkc, t1=t1, t2=t2: e.tensor_tensor(qkc[:, 6:8, :], t1[:, 6:8, :], t2[:, 6:8, :], ALU.add), reads=[k1_, k2_, kqc], writes=[kqc])
                                def do_tr(ti=ti, qTs=qTs, qka=qka, qkc=qkc, kqa=kqa, kqc=kqc, kq_=kq_):
                                    for j in range(6):
                                        S.op('pe', lambda e, j=j: e.transpose(pq[0][:, j, :], qka[:, 2 * j:2 * j + 2, :], ident[:]), reads=[kqa, 'ident'], writes=['pq0'])
                                    S.op('act', lambda e: e.activation(qTs[:, 0:6, ti * 128:(ti + 1) * 128], pq[0][:, 0:6, :], AF.Copy), reads=['pq0'], writes=[kq_])
                                    for j in range(3):
                                        S.op('pe', lambda e, j=j: e.transpose(pq[1][:, j, :], qkc[:, 2 * j:2 * j + 2, :], ident[:]), reads=[kqc, 'ident'], writes=['pq1'])
                                    S.op('pe', lambda e: e.transpose(pq[1][:, 3, :], qkc[:, 6:8, :], ident[:]), reads=[kqc, 'ident'], writes=['pq1'])
                                    S.op('dve', lambda e: e.tensor_copy(qTs[:, 6:10, ti * 128:(ti + 1) * 128], pq[1][:, 0:4, :]), reads=['pq1'], writes=[kq_])
                                if pend_tr:
                                    pend_tr.pop(0)()
                                pend_tr.append(do_tr)
                                if k + 1 < len(GL):
                                    partB(k + 1, ti)
                            for ct in range(6):
                                ps = ph[phn % 2]
                                kp = 'ph%d' % (phn % 2)
                                hs = hys[phn % 2]
                                kh = 'hys%d' % (phn % 2)
                                phn += 1
                                for c in range(8):
                                    S.op('pe', lambda e, ps=ps, c=c, ct=ct, hT=hT: e.matmul(ps[:, :], wsb[:, c, 1152 + ct * 128:1152 + (ct + 1) * 128], hT[:, c, :],
                                                                                     start=(c == 0), stop=(c == 7)),
                                         reads=hk + ['wsb'], writes=[kp])
                                S.op('act', lambda e, ps=ps, hs=hs: e.activation(hs[:], ps[:, :], AF.Copy), reads=[kp, kh + 'st'], writes=[kh])
                                S.dma(hyraw[b, ct, :, tg * 512:(tg + 1) * 512], hs[:], 'ph%d' % (phn % 2), reads=[kh], writes=[kh + 'st'])
                            while pend_tr:
                                pend_tr.pop(0)()
                            tsl = slice(tg * 512, (tg + 1) * 512)
                            S.dma(qTA[b, :, :, tsl].rearrange("j p t -> p j t"), qTs[:, 0:3, :], 'pq%d' % (k % 2), reads=[kq_], writes=[])
                            S.dma(kTA[b, :, :, tsl].rearrange("j p t -> p j t"), qTs[:, 3:6, :], 'pq%d' % (k % 2), reads=[kq_], writes=[])
                            S.dma(qTC[b, :, :, tsl].rearrange("j p t -> p j t"), qTs[:, 6:9, :], 'pq%d' % (k % 2), reads=[kq_], writes=[])
                            S.dma(kTC[b, :, tsl], qTs[:, 9, :], 'pq%d' % (k % 2), reads=[kq_], writes=[])
                    S.barrier()

            def attention(kind, l=l):
                with ExitStack() as st:
                    A = st.enter_context
                    pfx = 'a' + kind
                    LA = 4
                    NPS, NET = 5, 7
                    qT = [A(SB(pfx + "_qT%d" % i, [128, S_], BF16)) for i in range(2)]
                    kT = [[A(SB(pfx + "_kT%d_%d" % (i, h), [128, S_], BF16)) for h in range(2)] for i in range(2)]
                    for i in range(2):
                        for h in range(2):
                            S.op('pool', lambda e, i=i, h=h: e.memset(kT[i][h][:], 0.0), writes=['kT%d' % i])
                    nvh = 6 if kind == 'A' else 2
                    vs = [A(SB(pfx + "_v%d" % i, [128, NT, nvh * 65], BF16)) for i in range(2)]
                    et = [A(SB(pfx + "_et%d" % i, [128, 512], BF16)) for i in range(NET)]
                    em = [A(SB(pfx + "_em%d" % i, [128, 512], BF16)) for i in range(NET)]
                    mask = A(SB(pfx + "_mask", [128, 3968], BF16))
                    rec = A(SB(pfx + "_rec", [128, 512], F32))
                    osb = A(SB(pfx + "_osb", [64, 512], F32))
                    ob = [A(SB(pfx + "_ob%d" % i, [64, 512], BF16)) for i in range(2)]
                    ps = [A(PS(pfx + "_ps%d" % i, [128, 512], F32)) for i in range(NPS)]
                    po = [A(PS(pfx + "_po%d" % i, [128, 512], F32)) for i in range(2)]
                    pb = A(PS(pfx + "_pb", [128, 512], F32))
                    pdum = None
                    NDUM = ATT_NDUM
                    if kind == 'A':
                        S.dma(mask[:], c_mask[:, :], 'am', writes=['mask'])
                    items = []
                    grp = 0
                    for b in range(NB):
                        for j in range(3):
                            for half in range(2):
                                for g in range(4):
                                    tiles = []
                                    for i in range(NT):
                                        dmin = i * 128 - g * 512 - 511
                                        dmax = i * 128 + 127 - g * 512
                                        if kind == 'A' and (dmin > 1024 or dmax < -1024):
                                            continue
                                        tiles.append(i)
                                    for ii, i in enumerate(tiles):
                                        items.append(dict(b=b, j=j, half=half, g=g, i=i, ii=ii, nt=len(tiles), grp=grp))
                                    grp += 1
                    cur = dict(b=-1, bj=-1, nq=0, nv=0)
                    part2 = []

                    def issue_loads(it):
                        b, j = it['b'], it['j']
                        if b != cur['b']:
                            cur['b'] = b
                            sl = cur['nv'] % 2
                            cur['nv'] += 1
                            cur['vt'], cur['kv'] = vs[sl], 'v%d' % sl
                            S.dma(vs[sl][:], (vA if kind == 'A' else vC)[b], 'av%d' % sl, writes=['v%d' % sl])
                            if kind == 'C':
                                cur['kt'], cur['kk'] = kT[b % 2], 'kT%d' % (b % 2)
                                for h in range(2):
                                    S.dma(kT[b % 2][h][64 * h:64 * h + 64, :], kTC[b, 64 * h:64 * h + 64, :], 'ak%d' % (b % 2), writes=['kT%d' % (b % 2)])
                        if (b, j) != cur['bj']:
                            cur['bj'] = (b, j)
                            sl = cur['nq'] % 2
                            cur['nq'] += 1
                            cur['qt'], cur['kq'] = qT[sl], 'qT%d' % sl
                            S.dma(qT[sl][:], (qTA if kind == 'A' else qTC)[b, j], 'aq%d' % sl, writes=['qT%d' % sl])
                            if kind == 'A':
                                cur['kt'], cur['kk'] = kT[sl], 'kT%d' % sl
                                for h in range(2):
                                    S.dma(kT[sl][h][64 * h:64 * h + 64, :], kTA[b, j, 64 * h:64 * h + 64, :], 'ak%d' % sl, writes=['kT%d' % sl])
                        for k_ in ('vt', 'kv', 'kt', 'kk', 'qt', 'kq'):
                            it[k_] = cur[k_]

                    def emit_qk(n, it):
                        base = 64 * it['half']
                        i, g = it['i'], it['g']
                        pst, kps = ps[n % NPS], 'ps%d' % (n % NPS)
                        ett, ke = et[n % NET], 'et%d' % (n % NET)
                        emt, kem = em[n % NET], 'em%d' % (n % NET)
                        kt, qt = it['kt'][it['half']], it['qt']
                        S.op('pe', lambda e: e.matmul(pst[:, :], kt[:, i * 128:(i + 1) * 128], qt[:, g * 512:(g + 1) * 512], start=True, stop=True),
                             reads=[it['kk'], it['kq']], writes=[kps])
                        S.op('act', lambda e: e.activation(ett[:], pst[:, :], AF.Exp, scale=0.125), reads=[kps], writes=[ke])
                        for _ in range(NDUM):
                            S.op('pe', lambda e: e.matmul(pdum[:, :], kt[:, i * 128:(i + 1) * 128], qt[:, g * 512:(g + 1) * 512], start=True, stop=True),
                                 reads=[it['kk'], it['kq']], writes=['pdum'])
                        if kind == 'A':
                            x0 = g * 512 - i * 128 + 1920
                            eng = 'dve'
                            S.op(eng, lambda e: e.tensor_tensor(emt[:], ett[:], mask[:, x0:x0 + 512], ALU.mult), reads=[ke, 'mask'], writes=[kem])
                            it['rhs'], it['kr'] = emt, kem
                        else:
                            it['rhs'], it['kr'] = ett, ke

                    def emit_pv(m, it):
                        b, j, half, g, i, ii, nt = it['b'], it['j'], it['half'], it['g'], it['i'], it['ii'], it['nt']
                        if kind == 'A':
                            head = 2 * j + half
                            vh, chunk = head, head
                        else:
                            head = j + 3 * half
                            vh, chunk = half, 10 + head
                        pot, kpo = po[it['grp'] % 2], 'po%d' % (it['grp'] % 2)
                        vt, rhs = it['vt'], it['rhs']
                        S.op('pe', lambda e: e.matmul(pot[0:65, :], vt[:, i, vh * 65:(vh + 1) * 65], rhs[:], start=(ii == 0), stop=(ii == nt - 1)),
                             reads=[it['kv'], it['kr']], writes=[kpo])
                        if ii == nt - 1:
                            if kind == 'A' and A_RECIP_ACT:
                                S.op('act', lambda e: e.activation(rec[64:65, :], pot[64:65, :], AF.Ln), reads=[kpo], writes=['rec'])
                                S.op('act', lambda e: e.activation(rec[64:65, :], rec[64:65, :], AF.Exp, scale=-1.0), reads=['rec'], writes=['rec'])
                            else:
                                S.op('dve', lambda e: e.reciprocal(rec[64:65, :], pot[64:65, :]), reads=[kpo], writes=['rec'])
                            S.op('dve', lambda e: e.tensor_copy(osb[:], pot[0:64, :]), reads=[kpo], writes=['osb'])
                            obt, kob = ob[it['grp'] % 2], 'ob%d' % (it['grp'] % 2)

                            def fin():
                                S.op('pe', lambda e: e.matmul(pb[0:64, :], ones_f[64:65, 0:64], rec[64:65, :], start=True, stop=True), reads=['rec', 'ones_f'], writes=['pb'])
                                S.op('dve', lambda e: e.tensor_tensor(obt[:], osb[:], pb[0:64, :], ALU.mult), reads=['osb', 'pb'], writes=[kob])
                                S.dma(oT[b, chunk * 64:(chunk + 1) * 64, g * 512:(g + 1) * 512], obt[:], 'ao%d' % (it['grp'] % 2), reads=[kob], writes=[])
                            part2.append((m + 9, fin))

                    N_ = len(items)
                    PFD = 24
                    nload = 0
                    for n in range(N_ + LA):
                        while nload < N_ and nload <= n + PFD:
                            issue_loads(items[nload])
                            nload += 1
                        if n < N_:
                            emit_qk(n, items[n])
                        m = n - LA
                        if m >= 0:
                            emit_pv(m, items[m])
                            while part2 and part2[0][0] <= m:
                                part2.pop(0)[1]()
                    while part2:
                        part2.pop(0)[1]()
                    S.barrier()

            def stage_F(l=l, xcur=xcur, xnext=xnext):
                with ExitStack() as st:
                    A = st.enter_context
                    zT = A(SB("f_zT", [33, S_], F32))
                    w1 = A(SB("f_w1", [33, 64], F32))
                    w2 = A(SB("f_w2", [64, 64], F32))
                    w3 = A(SB("f_w3", [64, 1024], F32))
                    b1 = A(SB("f_b1", [64, 1], F32))
                    b2 = A(SB("f_b2", [64, 1], F32))
                    fr = A(SB("f_fr", [64, 2], F32))
                    sc = A(SB("f_sc", [64, 8], F32))
                    h1 = A(SB("f_h1", [64, S_], F32))
                    h2 = A(SB("f_h2", [64, S_], F32))
                    sa = A(SB("f_sa", [64, 512], F32))
                    sb_ = A(SB("f_sb", [64, 512], F32))
                    sc_ = A(SB("f_sc2", [64, 512], F32))
                    decB = A(SB("f_decB", [128, 1024], F32))
                    tneg = A(SB("f_tneg", [128, NT], F32))
                    wins = [A(SB("f_win%d" % i, [128, 1024], F32)) for i in range(2)]
                    filt = A(SB("f_filt", [128, NT, 1024], F32))
                    absfs = [A(SB("f_abs%d" % i, [128, 1024], F32)) for i in range(2)]
                    rn = A(SB("f_rn", [128, 512], F32))
                    tmp = A(SB("f_tmp", [128, 512], F32))
                    tmp2 = A(SB("f_tmp2", [128, 512], F32))
                    Pm = A(SB("f_P", [128, NT, 512], BF16))
                    Qm = A(SB("f_Q", [128, NT, 512], BF16))
                    fwb = [A(SB("f_fw%d" % i, [128, NT, 128], BF16)) for i in range(3)]
                    ko = [A(SB("f_ko%d" % i, [128, 512], F32)) for i in range(2)]
                    pm = [A(PS("f_pm%d" % i, [128, 512], F32)) for i in range(4)]
                    pn = [A(PS("f_pn%d" % i, [128, 512], F32)) for i in range(2)]

                    S.dma(zT[:], c_zT[:, :], 'fc', writes=['zT'])
                    for rt_ in range(2):
                        S.dma(fwb[rt_][:], c_fw[rt_], 'ff%d' % rt_, writes=['fwb%d' % rt_])
                    S.dma(w1[:], hy_w1[l], 'fc', writes=['w1'])
                    S.dma(w2[:], hy_w2[l], 'fc', writes=['w2'])
                    S.dma(w3[:], hy_w3[l], 'fc', writes=['w3'])
                    S.dma(b1[:], hy_b1[l], 'fc', writes=['b1'])
                    S.dma(b2[:], hy_b2[l], 'fc', writes=['b2'])
                    S.dma(fr[:], hy_fr[l], 'fc', writes=['fr'])
                    S.dma(decB[:], hy_dec[l:l + 1, :].broadcast_to([128, 1024]), 'fc', writes=['decB'])
                    S.dma(tneg[:], c_tneg[:, :], 'fc', writes=['tneg'])
                    for li, bb in ((0, b1), (1, b2)):
                        o = 4 * li
                        S.op('dve', lambda e, li=li, bb=bb, o=o: e.tensor_tensor(sc[:, o + 3:o + 4], fr[:, li:li + 1], bb[:, 0:1], ALU.mult), reads=['fr', 'b1', 'b2'], writes=['sc'])
                        S.op('dve', lambda e, li=li, o=o: e.tensor_copy(sc[:, o + 2:o + 3], fr[:, li:li + 1]), reads=['fr', 'sc'], writes=['sc'])
                        S.op('dve', lambda e, o=o: e.tensor_scalar(sc[:, o:o + 2], sc[:, o + 2:o + 4], 0.25, None, ALU.mult), reads=['sc'], writes=['sc'])

                    def sin_layer(li, wmat, kw, src, ksrc, dst, kdst, K):
                        o = 4 * li
                        for n in range(4):
                            p = pm[n % 4]
                            kp = 'pm%d' % (n % 4)
                            sl = slice(n * 512, (n + 1) * 512)
                            S.op('pe', lambda e, p=p, sl=sl: e.matmul(p[0:64, :], wmat[0:K, :], src[0:K, sl], start=True, stop=True), reads=[kw, ksrc], writes=[kp])
                            S.op('act', lambda e, p=p: e.activation(sa[:], p[0:64, :], AF.Sin, scale=sc[:, o:o + 1], bias=sc[:, o + 1:o + 2]), reads=[kp, 'sc'], writes=['sa'])
                            S.op('act', lambda e, p=p: e.activation(sb_[:], p[0:64, :], AF.Abs, scale=sc[:, o + 2:o + 3], bias=sc[:, o + 3:o + 4]), reads=[kp, 'sc'], writes=['sb'])
                            S.op('dve', lambda e: e.tensor_scalar(sb_[:], sb_[:], -0.25, float(math.pi / 2), ALU.mult, ALU.add), reads=['sb'], writes=['sb'])
                            S.op('act', lambda e: e.activation(sb_[:], sb_[:], AF.Sin), reads=['sb'], writes=['sb'])
                            S.op('dve', lambda e: e.tensor_tensor(sc_[:], sa[:], sa[:], ALU.mult), reads=['sa'], writes=['sc2'])
                            S.op('dve', lambda e: e.tensor_scalar(sc_[:], sc_[:], -8.0, 4.0, ALU.mult, ALU.add), reads=['sc2'], writes=['sc2'])
                            S.op('dve', lambda e: e.tensor_tensor(sa[:], sa[:], sb_[:], ALU.mult), reads=['sa', 'sb'], writes=['sa'])
                            S.op('dve', lambda e, sl=sl: e.tensor_tensor(dst[:, sl], sa[:], sc_[:], ALU.mult), reads=['sa', 'sc2'], writes=[kdst])

                    sin_layer(0, w1, 'w1', zT, 'zT', h1, 'h1', 33)
                    sin_layer(1, w2, 'w2', h1, 'h1', h2, 'h2', 64)
                    def f_win(tt):
                        S.op('act', lambda e: e.activation(wins[tt % 2][:], decB[:], AF.Exp, scale=tneg[:, tt:tt + 1]), reads=['decB', 'tneg'], writes=['win%d' % (tt % 2)])

                    def f_h3(tt):
                        for n in range(2):
                            p = pm[(tt * 2 + n) % 4]
                            kp = 'pm%d' % ((tt * 2 + n) % 4)
                            sl = slice(n * 512, (n + 1) * 512)
                            S.op('pe', lambda e, p=p, sl=sl: e.matmul(p[:, :], h2[:, tt * 128:(tt + 1) * 128], w3[:, sl], start=True, stop=True), reads=['h2', 'w3'], writes=[kp])

                    f_win(0)
                    f_h3(0)
                    for tt in range(NT):
                        if tt + 1 < NT:
                            f_win(tt + 1)
                            f_h3(tt + 1)
                        win = wins[tt % 2]
                        absf = absfs[tt % 2]
                        for n in range(2):
                            p = pm[(tt * 2 + n) % 4]
                            kp = 'pm%d' % ((tt * 2 + n) % 4)
                            sl = slice(n * 512, (n + 1) * 512)
                            S.op('dve', lambda e, p=p, tt=tt, sl=sl, win=win: e.tensor_tensor(filt[:, tt, sl], p[:, :], win[:, sl], ALU.mult), reads=[kp, 'win%d' % (tt % 2)], writes=['filt%d' % tt])
                        if tt == 0:
                            fv0 = filt[0:1, 0, :].rearrange("p (o f c) -> p o f c", o=2, f=2)
                            S.op('dve', lambda e, fv0=fv0: e.memset(fv0[:, :, 1, :], 0.0), reads=['filt0'], writes=['filt0'])
                        S.op('act', lambda e, tt=tt, absf=absf: e.activation(absf[:], filt[:, tt, :], AF.Abs), reads=['filt%d' % tt], writes=['absf%d' % (tt % 2)])
                        for n in range(2):
                            S.op('pe', lambda e, n=n, tt=tt, absf=absf: e.matmul(pn[n][:, :], ones_f[:, :], absf[:, n * 512:(n + 1) * 512], start=(tt == 0), stop=(tt == NT - 1)),
                                 reads=['absf%d' % (tt % 2), 'ones_f'], writes=['pn%d' % n])
                    for o in range(2):
                        S.op('act', lambda e, o=o: e.activation(tmp[:, 0:256], pn[o][:, 0:256], AF.Copy), reads=['pn%d' % o], writes=['tmp'])
                        S.op('dve', lambda e, o=o: e.tensor_tensor(rn[:, o * 256:(o + 1) * 256], tmp[:, 0:256], pn[o][:, 256:512], ALU.add), reads=['tmp', 'pn%d' % o], writes=['rn'])
                    S.op('dve', lambda e: e.reciprocal(rn[:], rn[:]), reads=['rn'], writes=['rn'])
                    rn3 = rn[:].rearrange("p (o c) -> p o c", o=2)
                    for tt in range(NT):
                        fv = filt[:, tt, :].rearrange("p (o f c) -> p o f c", o=2, f=2)
                        ta_ = tmp[:].rearrange("p (o c) -> p o c", o=2)
                        tb_ = tmp2[:].rearrange("p (o c) -> p o c", o=2)
                        S.op('pool', lambda e, fv=fv, ta_=ta_: e.tensor_tensor(ta_, fv[:, :, 0, :], fv[:, :, 1, :], ALU.add), reads=['filt%d' % tt], writes=['tmp'])
                        S.op('pool', lambda e, fv=fv, tb_=tb_: e.tensor_tensor(tb_, fv[:, :, 0, :], fv[:, :, 1, :], ALU.subtract), reads=['filt%d' % tt], writes=['tmp2'])
                        S.op('dve', lambda e, tt=tt, ta_=ta_: e.tensor_tensor(Pm[:, tt, :].rearrange("p (o c) -> p o c", o=2), ta_, rn3, ALU.mult), reads=['tmp', 'rn'], writes=['Pm'])
                        S.op('dve', lambda e, tt=tt, tb_=tb_: e.tensor_tensor(Qm[:, tt, :].rearrange("p (o c) -> p o c", o=2), tb_, rn3, ALU.mult), reads=['tmp2', 'rn'], writes=['Qm'])
                    for rt in range(32):
                        if rt + 2 < 32:
                            S.dma(fwb[(rt + 2) % 3][:], c_fw[rt + 2], 'ff%d' % ((rt + 2) % 3), writes=['fwb%d' % ((rt + 2) % 3)])
                        fb = fwb[rt % 3]
                        kfb = 'fwb%d' % (rt % 3)
                        p = pm[rt % 4]
                        kp = 'pm%d' % (rt % 4)
                        src, ks = (Pm, 'Pm') if rt < 16 else (Qm, 'Qm')
                        for tt in range(NT):
                            S.op('pe', lambda e, p=p, fb=fb, tt=tt, src=src: e.matmul(p[:, :], fb[:, tt, :], src[:, tt, :], start=(tt == 0), stop=(tt == NT - 1)),
                                 reads=[kfb, ks], writes=[kp])
                        kot = ko[rt % 2]
                        kko = 'ko%d' % (rt % 2)
                        S.op('act', lambda e, p=p, kot=kot: e.activation(kot[:], p[:, :], AF.Copy), reads=[kp], writes=[kko])
                        S.dma(kfd[rt], kot[:], 'fk%d' % (rt % 2), reads=[kko], writes=[])
                    S.barrier()

            def stage_H(l=l, xcur=xcur, xnext=xnext):
                with ExitStack() as st:
                    A = st.enter_context
                    NC_ = NB * 256
                    V = A(SB("h_V", [128, NT, NC_], BF16))
                    Y = A(SB("h_Y", [128, 32, NC_], BF16))
                    cw = A(SB("h_cw", [128, 6, 3], F32))
                    cb = A(SB("h_cb", [128, 6], F32))
                    dB = A(SB("h_dB", [128, 512], F32))
                    raw = [A(SB("h_raw%d" % i, [128, S_], BF16)) for i in range(3)]
                    u0s = [A(SB("h_u0_%d" % i, [128, S_], F32)) for i in range(2)]
                    ubs = [A(SB("h_ub_%d" % i, [128, S_], BF16)) for i in range(2)]
                    xs = [A(SB("h_xs%d" % i, [128, NT, 128], BF16)) for i in range(2)]
                    fwb = [A(SB("h_fw%d" % i, [128, NT, 128], BF16)) for i in range(3)]
                    ivb = [A(SB("h_iv%d" % i, [128, 32, 128], BF16)) for i in range(2)]
                    kre = [A(SB("h_kre%d" % i, [128, 256], F32)) for i in range(2)]
                    kim = [A(SB("h_kim%d" % i, [128, 256], F32)) for i in range(2)]
                    ure = A(SB("h_ure", [128, NC_], F32))
                    uim = A(SB("h_uim", [128, NC_], F32))
                    ta = A(SB("h_ta", [128, NC_], F32))
                    tb = A(SB("h_tb", [128, NC_], F32))
                    xg = [A(SB("h_xg%d" % i, [128, NC_], BF16)) for i in range(2)]
                    obts = [A(SB("h_obt%d" % i, [128, NC_], BF16)) for i in range(2)]
                    obT = [A(SB("h_obT%d" % i, [128, NB * 2, 128], BF16)) for i in range(2)]
                    pp = A(PS("h_pp", [128, 8, 512], F32))

                    S.dma(cw[:], hy_cw[l], 'hc', writes=['cw'])
                    S.dma(cb[:], hy_cb[l], 'hc', writes=['cb'])
                    S.dma(dB[:], hy_d[l:l + 1, :].broadcast_to([128, 512]), 'hc', writes=['dB'])
                    nr = 0
                    nx = 0
                    chains = [(b, ct) for b in range(NB) for ct in range(6)]

                    def load_raw(i):
                        S.dma(raw[i % 3][:], hyraw[chains[i][0], chains[i][1]], 'hr%d' % (i % 3), writes=['raw%d' % (i % 3)])

                    load_raw(0)
                    load_raw(1)
                    def S12(ci):
                        b, ct = chains[ci]
                        if ci + 2 < len(chains):
                            load_raw(ci + 2)
                        r = raw[ci % 3]
                        kr = 'raw%d' % (ci % 3)
                        u0, ub = u0s[ci % 2], ubs[ci % 2]
                        ku0, kub = 'u0_%d' % (ci % 2), 'ub_%d' % (ci % 2)
                        S.op('act', lambda e: e.activation(u0[:], r[:], AF.Identity, scale=cw[:, ct, 1:2], bias=cb[:, ct:ct + 1]), reads=[kr, 'cw', 'cb'], writes=[ku0])
                        S.op('dve', lambda e: e.scalar_tensor_tensor(u0[:, 1:S_], r[:, 0:S_ - 1], cw[:, ct, 0:1], u0[:, 1:S_], ALU.mult, ALU.add),
                             reads=[kr, ku0, 'cw'], writes=[ku0])
                        S.op('dve', lambda e: e.scalar_tensor_tensor(ub[:, 0:S_ - 1], r[:, 1:S_], cw[:, ct, 2:3], u0[:, 0:S_ - 1], ALU.mult, ALU.add),
                             reads=[kr, ku0, 'cw'], writes=[kub])

                    def S3(ci):
                        b, ct = chains[ci]
                        u0, ub = u0s[ci % 2], ubs[ci % 2]
                        ku0, kub = 'u0_%d' % (ci % 2), 'ub_%d' % (ci % 2)
                        S.op('act', lambda e: e.activation(ub[:, S_ - 1:S_], u0[:, S_ - 1:S_], AF.Copy), reads=[ku0, kub], writes=[kub])
                        bank = ci % 2
                        pt = pp[:, 4 * bank:4 * bank + 2, :].rearrange("p a n -> p (a n)").bitcast(BF16).rearrange("p (t n) -> p t n", n=128)[:, 0:NT, :]
                        kpt = 'ppt%d' % bank
                        for tt in range(NT):
                            S.op('pe', lambda e, tt=tt: e.transpose(pt[:, tt, :], ub[:, tt * 128:(tt + 1) * 128], ident[:]), reads=[kub, 'ident'], writes=[kpt])
                        col = b * 256 + (ct % 2) * 128
                        if ct < 2:
                            S.op('act', lambda e: e.activation(V[:, :, col:col + 128], pt, AF.Copy), reads=[kpt], writes=['V'])
                        else:
                            nx = xcnt[0]
                            xcnt[0] += 1
                            x_ = xs[nx % 2]
                            kx = 'xs%d' % (nx % 2)
                            S.op('act', lambda e: e.activation(x_[:], pt, AF.Copy), reads=[kpt], writes=[kx])
                            dstt = (x1tm if ct < 4 else x2tm)
                            S.dma(dstt[:, :, col:col + 128].rearrange("t p c -> p t c"), x_[:], 'hx%d' % (nx % 2), reads=[kx], writes=[])

                    xcnt = [0]
                    S12(0)
                    for ci in range(len(chains)):
                        if ci + 1 < len(chains):
                            S12(ci + 1)
                        S3(ci)
                    S.barrier()
                    nfw = 0
                    niv = 0
                    pend_h = []
                    nk = 0
                    ngx = 0
                    nob = 0
                    GW = min(512, NC_)
                    ngrp = NC_ // GW
                    for order in range(2):
                        for a in range(16):
                            k1, k2 = kre[nk % 2], kim[nk % 2]
                            kk = 'kf%d' % (nk % 2)
                            S.dma(k1[:], kfd[a, :, order * 256:(order + 1) * 256], 'hk%d' % (nk % 2), writes=[kk])
                            S.dma(k2[:], kfd[16 + a, :, order * 256:(order + 1) * 256], 'hk%d' % (nk % 2), writes=[kk])
                            nk += 1
                            for part in range(2):
                                rt = a + 16 * part
                                fb = fwb[nfw % 3]
                                kfb = 'fwb%d' % (nfw % 3)
                                S.dma(fb[:], c_fw[rt], 'hf%d' % (nfw % 3), writes=[kfb])
                                nfw += 1
                                for tt in range(NT):
                                    for n in range(ngrp):
                                        bk = (a % 2) * 4 + part * 2 + n
                                        S.op('pe', lambda e, bk=bk, fb=fb, tt=tt, n=n: e.matmul(pp[:, bk, 0:GW], fb[:, tt, :], V[:, tt, n * GW:(n + 1) * GW],
                                                                                            start=(tt == 0), stop=(tt == NT - 1)),
                                             reads=[kfb, 'V'], writes=['pp%d' % bk])
                            bre = [(a % 2) * 4 + n for n in range(ngrp)]
                            bim = [(a % 2) * 4 + 2 + n for n in range(ngrp)]
                            for n in range(ngrp):
                                sl = slice(n * GW, (n + 1) * GW)
                                S.op('act', lambda e, n=n, sl=sl, bre=bre: e.activation(ure[:, sl], pp[:, bre[n], 0:GW], AF.Copy), reads=['pp%d' % bre[n]], writes=['ure'])
                                S.op('act', lambda e, n=n, sl=sl, bim=bim: e.activation(uim[:, sl], pp[:, bim[n], 0:GW], AF.Copy), reads=['pp%d' % bim[n]], writes=['uim'])
                            nb_ = NC_ // 256
                            k1b = k1[:].unsqueeze(1).broadcast_to([128, nb_, 256])
                            k2b = k2[:].unsqueeze(1).broadcast_to([128, nb_, 256])
                            v3 = lambda t_: t_.rearrange("p (b c) -> p b c", c=256)
                            S.op('pool', lambda e, k1b=k1b: e.tensor_tensor(v3(ta[:]), v3(ure[:]), k1b, ALU.mult), reads=['ure', kk], writes=['ta'])
                            S.op('dve', lambda e, k2b=k2b: e.tensor_tensor(v3(tb[:]), v3(uim[:]), k2b, ALU.mult), reads=['uim', kk], writes=['tb'])
                            S.op('pool', lambda e, a=a: e.tensor_tensor(Y[:, a, :], ta[:], tb[:], ALU.subtract), reads=['ta', 'tb'], writes=['Y'])
                            S.op('dve', lambda e, k2b=k2b: e.tensor_tensor(v3(tb[:]), v3(ure[:]), k2b, ALU.mult), reads=['ure', kk, 'tb'], writes=['tb'])
                            S.op('pool', lambda e, k1b=k1b: e.tensor_tensor(v3(ta[:]), v3(uim[:]), k1b, ALU.mult), reads=['uim', kk, 'ta'], writes=['ta'])
                            S.op('dve', lambda e, a=a: e.tensor_tensor(Y[:, 16 + a, :], ta[:], tb[:], ALU.add), reads=['ta', 'tb'], writes=['Y'])
                        for tt in range(NT):
                            ib = ivb[niv % 2]
                            kib = 'ivb%d' % (niv % 2)
                            S.dma(ib[:], c_inv[tt], 'hi%d' % (niv % 2), writes=[kib])
                            niv += 1
                            xg_ = xg[ngx % 2]
                            kxg = 'xg%d' % (ngx % 2)
                            S.dma(xg_[:], (x1tm if order == 0 else x2tm)[tt], 'hg%d' % (ngx % 2), writes=[kxg])
                            ngx += 1
                            bks = [(tt % 2) * ngrp + n for n in range(ngrp)]
                            for rt in range(32):
                                for n in range(ngrp):
                                    S.op('pe', lambda e, ib=ib, rt=rt, n=n, bk=bks[n]: e.matmul(pp[:, bk, 0:GW], ib[:, rt, :], Y[:, rt, n * GW:(n + 1) * GW],
                                                                                               start=(rt == 0), stop=(rt == 31)),
                                         reads=[kib, 'Y'], writes=['pp%d' % bks[n]])
                            while pend_h:
                                pend_h.pop(0)()
                            nb_ = NC_ // 256
                            dv = dB[:, order * 256:(order + 1) * 256].unsqueeze(1).broadcast_to([128, nb_, 256])
                            v3 = lambda t_: t_.rearrange("p (b c) -> p b c", c=256)
                            S.op('pool', lambda e, tt=tt, dv=dv: e.tensor_tensor(v3(ta[:]), v3(V[:, tt, :]), dv, ALU.mult), reads=['V', 'dB', 'ta'], writes=['ta'])
                            for n in range(ngrp):
                                sl = slice(n * GW, (n + 1) * GW)
                                S.op('dve', lambda e, sl=sl, bk=bks[n]: e.tensor_tensor(ta[:, sl], ta[:, sl], pp[:, bk, 0:GW], ALU.add), reads=['ta', 'pp%d' % bks[n]], writes=['ta'])
                            if order == 0:
                                S.op('pool', lambda e, tt=tt, xg_=xg_: e.tensor_tensor(V[:, tt, :], ta[:], xg_[:], ALU.mult), reads=['ta', kxg, 'V'], writes=['V'])
                            else:
                                obt = obts[nob % 2]
                                kobt = 'obt%d' % (nob % 2)
                                S.op('pool', lambda e, xg_=xg_, obt=obt: e.tensor_tensor(obt[:], ta[:], xg_[:], ALU.mult), reads=['ta', kxg], writes=[kobt])

                                def fin_tt(tt=tt, obt=obt, kobt=kobt, nob=nob):
                                    pt = pp[:, 6:8, :].rearrange("p a n -> p (a n)").bitcast(BF16).rearrange("p (t n) -> p t n", n=128)
                                    for q in range(NB * 2):
                                        S.op('pe', lambda e, q=q: e.transpose(pt[:, q, :], obt[:, q * 128:(q + 1) * 128], ident[:]), reads=[kobt, 'ident'], writes=['pp6'])
                                    oo = obT[nob % 2]
                                    koo = 'obT%d' % (nob % 2)
                                    S.op('act', lambda e: e.activation(oo[:], pt[:, 0:NB * 2, :], AF.Copy), reads=['pp6'], writes=[koo])
                                    for b in range(NB):
                                        S.dma(oT[b, 384:640, tt * 128:(tt + 1) * 128].rearrange("(h p) t -> p h t", p=128), oo[:, 2 * b:2 * b + 2, :], 'ho%d' % (nob % 2),
                                              reads=[koo], writes=[])
                                pend_h.append(fin_tt)
                                nob += 1
                    while pend_h:
                        pend_h.pop(0)()
                    S.barrier()

            def stage_O(l=l, xcur=xcur, xnext=xnext):
                with ExitStack() as st:
                    A = st.enter_context
                    wo = A(SB("o_w", [128, 8, D_], BF16))
                    gB = A(SB("o_gB", [128, D_], F32))
                    oTs = [A(SB("o_oT%d" % i, [128, 8, S_], BF16)) for i in range(2)]
                    NSL = 4
                    ctx = dict(xt=[A(SB("o_xt%d" % i, [128, D_], F32)) for i in range(NSL)],
                               ssq=[A(SB("o_ssq%d" % i, [128, 1], F32)) for i in range(NSL)],
                               yn=[A(SB("o_yn%d" % i, [128, D_], F32)) for i in range(NSL)])
                    pm = [A(PS("o_pm%d" % i, [128, D_], F32)) for i in range(NSL)]
                    S.dma(wo[:], wbf_out[l].rearrange("(c p) n -> p c n", p=128), 'ow', writes=['wo'])
                    S.dma(gB[:], g_post[l:l + 1, :].broadcast_to([128, D_]), 'oc', writes=['gB'])
                    n = 0
                    pend = []
                    TL = [(b, tt) for b in range(NB) for tt in range(NT)]
                    PF = 2

                    def load_oT(b):
                        for c in range(8):
                            S.dma(oTs[b % 2][:, c, :], oT[b, c * 128:(c + 1) * 128, :], 'oo%d' % (b % 2), writes=['oTs%d' % (b % 2)])

                    load_oT(0)
                    for i_ in range(min(PF, len(TL))):
                        post_load(ctx, xcur[TL[i_][0], TL[i_][1] * 128:(TL[i_][1] + 1) * 128, :], i_ % NSL, 'o_')
                    for n, (b, tt) in enumerate(TL):
                        ot = oTs[b % 2]
                        ko = 'oTs%d' % (b % 2)
                        if tt == 4 and b + 1 < NB:
                            load_oT(b + 1)
                        sl = n % NSL
                        p = pm[sl]
                        kp = 'opm%d' % sl
                        for nn in range(2):
                            for c in range(8):
                                S.op('pe', lambda e, p=p, nn=nn, c=c, ot=ot, tt=tt: e.matmul(p[:, nn * 512:(nn + 1) * 512], ot[:, c, tt * 128:(tt + 1) * 128],
                                                                                              wo[:, c, nn * 512:(nn + 1) * 512], start=(c == 0), stop=(c == 7)),
                                     reads=[ko, 'wo'], writes=[kp])
                        while pend:
                            pend.pop(0)()
                        if n + PF < len(TL):
                            b2, t2 = TL[n + PF]
                            post_load(ctx, xcur[b2, t2 * 128:(t2 + 1) * 128, :], (n + PF) % NSL, 'o_')
                        post_part1(ctx, p[:, :], [kp], None, sl, 'o_')
                        pend.append(lambda p=p, kp=kp, sl=sl, b=b, tt=tt: post_part2(ctx, p[:, :], [kp], gB, xmid[b, tt * 128:(tt + 1) * 128, :], sl, 'o_'))
                    while pend:
                        pend.pop(0)()
                    S.barrier()

            def stage_M(l=l, xcur=xcur, xnext=xnext):
                with ExitStack() as st:
                    A = st.enter_context
                    big = A(SB("m_big", [128, NF * D_], BF16))
                    hT = big[:, 0:8 * S_].rearrange("p (c t) -> p c t", c=8)
                    wd = big[:, :].rearrange("p (f n) -> p f n", f=NF)
                    actT = A(SB("m_actT", [128, NF, S_], BF16))
                    gcol = A(SB("m_gcol", [128, 8], F32))
                    gB = A(SB("m_gB", [128, D_], F32))
                    cw = A(SB("m_cw", [128, NF, 3], F32))
                    cb = A(SB("m_cb", [128, NF], F32))
                    wg = [A(SB("m_wg%d" % i, [128, 8, 128], BF16)) for i in range(2)]
                    wu = [A(SB("m_wu%d" % i, [128, 8, 128], BF16)) for i in range(2)]
                    u0 = A(SB("m_u0", [128, S_], F32))
                    gl = A(SB("m_gl", [128, S_], F32))
                    ctx = dict(xt=[A(SB("m_xt%d" % i, [128, D_], F32)) for i in range(4)],
                               ssq=[A(SB("m_ssq%d" % i, [128, 1], F32)) for i in range(4)],
                               xn=[A(SB("m_xn%d" % i, [128, D_], BF16)) for i in range(4)],
                               yn=[A(SB("m_yn%d" % i, [128, D_], F32)) for i in range(4)])
                    pp = A(PS("m_pp", [128, 8 * 512], F32))
                    pTb = [pp[:, o_:o_ + 512].bitcast(BF16).rearrange("p (c n) -> p c n", n=128) for o_ in (0, 2048)]
                    kpTb = [['mpg', 'mpd0'], ['mpu', 'mpd2']]
                    S.dma(gcol[:], g_fpre_col[l], 'mc', writes=['gcol'])
                    S.dma(gB[:], g_fpost[l:l + 1, :].broadcast_to([128, D_]), 'mc', writes=['gB'])
                    S.dma(cw[:], f_cw[l], 'mc', writes=['cw'])
                    S.dma(cb[:], f_cb[l], 'mc', writes=['cb'])
                    nw = 0
                    nd = 0
                    pend = []
                    for b in range(NB):
                        jobs = lambda kb: [(xmid[b, (2 * kb + i_) * 128:(2 * kb + i_ + 1) * 128, :], (kb % 2) * 2 + i_) for i_ in range(2)]
                        norm_partA(ctx, jobs(0), 'm_')
                        for kb in range(8):
                            if kb + 1 < 8:
                                norm_partA(ctx, jobs(kb + 1), 'm_')
                            for i_ in range(2):
                                tt = 2 * kb + i_
                                norm_partB(ctx, (kb % 2) * 2 + i_, gcol, hT[:, :, tt * 128:(tt + 1) * 128], 'big', pTb[tt % 2], kpTb[tt % 2], 'm_')
                        for f in range(NF):
                            g_, u_ = wg[nw % 2], wu[nw % 2]
                            kw = 'wgu%d' % (nw % 2)
                            S.dma(g_[:], wbf_gate[l, :, f * 128:(f + 1) * 128].rearrange("(c p) n -> p c n", p=128), 'mw%d' % (nw % 2), writes=[kw])
                            S.dma(u_[:], wbf_up[l, :, f * 128:(f + 1) * 128].rearrange("(c p) n -> p c n", p=128), 'mw%d' % (nw % 2), writes=[kw])
                            nw += 1
                            for (wt, off, kps) in ((g_, 0, ['mpg', 'mpd0', 'mpd1']), (u_, 2048, ['mpu', 'mpd2', 'mpd3'])):
                                for n in range(4):
                                    for c in range(8):
                                        S.op('pe', lambda e, wt=wt, off=off, n=n, c=c: e.matmul(pp[:, off + n * 512:off + (n + 1) * 512], wt[:, c, :], hT[:, c, n * 512:(n + 1) * 512],
                                                                                                 start=(c == 0), stop=(c == 7)),
                                             reads=[kw, 'big'], writes=kps)
                            gp = pp[:, 0:S_]
                            up = pp[:, S_:2 * S_]
                            S.op('act', lambda e, f=f, gp=gp: e.activation(u0[:], gp, AF.Identity, scale=cw[:, f, 1:2], bias=cb[:, f:f + 1]), reads=['mpg', 'cw', 'cb'], writes=['u0'])
                            S.op('dve', lambda e, f=f, gp=gp: e.scalar_tensor_tensor(u0[:, 1:S_], gp[:, 0:S_ - 1], cw[:, f, 0:1], u0[:, 1:S_], ALU.mult, ALU.add),
                                 reads=['mpg', 'u0', 'cw'], writes=['u0'])
                            S.op('dve', lambda e, f=f, gp=gp: e.scalar_tensor_tensor(u0[:, 0:S_ - 1], gp[:, 1:S_], cw[:, f, 2:3], u0[:, 0:S_ - 1], ALU.mult, ALU.add),
                                 reads=['mpg', 'u0', 'cw'], writes=['u0'])
                            S.op('act', lambda e: e.activation(gl[:], u0[:], AF.Gelu_apprx_tanh), reads=['u0'], writes=['gl'])
                            S.op('dve', lambda e, f=f, up=up: e.tensor_tensor(actT[:, f, :], gl[:], up, ALU.mult), reads=['gl', 'mpu'], writes=['actT'])
                        for f0 in range(0, NF, 2):
                            S.dma(wd[:, f0:f0 + 2, :], wbf_down[l, f0 * 128:(f0 + 2) * 128, :].rearrange("(f p) n -> p f n", p=128), 'md', writes=['big'])
                        for tt in range(NT):
                            sl = nd % 4
                            off = sl * 1024
                            kp = ['mpd%d' % sl, 'mpg' if sl < 2 else 'mpu']
                            for nn in range(2):
                                for f in range(NF):
                                    S.op('pe', lambda e, off=off, nn=nn, f=f, tt=tt: e.matmul(pp[:, off + nn * 512:off + (nn + 1) * 512], actT[:, f, tt * 128:(tt + 1) * 128],
                                                                                               wd[:, f, nn * 512:(nn + 1) * 512], start=(f == 0), stop=(f == NF - 1)),
                                         reads=['actT', 'big'], writes=kp)
                            while pend:
                                pend.pop(0)()
                            if tt == 0:
                                for i_ in range(2):
                                    post_load(ctx, xmid[b, i_ * 128:(i_ + 1) * 128, :], (nd + i_) % 4, 'm_')
                            if tt + 2 < NT:
                                post_load(ctx, xmid[b, (tt + 2) * 128:(tt + 3) * 128, :], (nd + 2) % 4, 'm_')
                            post_part1(ctx, pp[:, off:off + 1024], ['mpd%d' % sl], None, sl, 'm_')
                            pend.append(lambda off=off, sl=sl, b=b, tt=tt: post_part2(ctx, pp[:, off:off + 1024], ['mpd%d' % sl], gB, xnext[b, tt * 128:(tt + 1) * 128, :], sl, 'm_'))
                            nd += 1
                        while pend:
                            pend.pop(0)()
                    S.barrier()
            if 'P' in stages:
                stage_P()
            if 'A' in stages:
                attention('A')
            if 'C' in stages:
                attention('C')
            if 'F' in stages:
                stage_F()
            if 'H' in stages:
                stage_H()
            if 'O' in stages:
                stage_O()
            if 'M' in stages:
                stage_M()
    S.emit()
    return nc


def make_inputs(inputs, NL=2):
    f = lambda a: np.ascontiguousarray(np.asarray(a, dtype=np.float32))
    c = host_consts()
    m = {}
    for k in ("w_in", "w_out", "ffn_w_gate", "ffn_w_up", "ffn_w_down", "hy_w1", "hy_w2", "hy_w3"):
        m[k] = f(inputs[k])[:NL]
    m["g_mix_pre_col"] = f(f(inputs["g_mix_pre"])[:NL].reshape(NL, 8, 128).transpose(0, 2, 1))
    m["g_ffn_pre_col"] = f(f(inputs["g_ffn_pre"])[:NL].reshape(NL, 8, 128).transpose(0, 2, 1))
    m["g_mix_post"] = f(inputs["g_mix_post"])[:NL]
    m["g_ffn_post"] = f(inputs["g_ffn_post"])[:NL]
    gq, gk = f(inputs["g_q"])[:NL], f(inputs["g_k"])[:NL]
    m["g_qk"] = f(np.concatenate([np.tile(gq, (1, 6)), np.tile(gk, (1, 2))], axis=1))
    m["hy_cw"] = f(f(inputs["hy_conv_w"])[:NL].reshape(NL, 3, 6, 128).transpose(0, 3, 2, 1))
    m["hy_cb"] = f(f(inputs["hy_conv_b"])[:NL].reshape(NL, 6, 128).transpose(0, 2, 1))
    m["hy_b1c"] = f(f(inputs["hy_b1"])[:NL].reshape(NL, 64, 1))
    m["hy_b2c"] = f(f(inputs["hy_b2"])[:NL].reshape(NL, 64, 1))
    m["hy_freqc"] = f(f(inputs["hy_freq"])[:NL].transpose(0, 2, 1))
    m["hy_decay"] = f(f(inputs["hy_decay"])[:NL].reshape(NL, 1024))
    m["hy_d"] = f(f(inputs["hy_d"])[:NL].reshape(NL, 512))
    m["ffn_cw"] = f(f(inputs["ffn_conv_w"])[:NL].reshape(NL, 3, NF, 128).transpose(0, 3, 2, 1))
    m["ffn_cb"] = f(f(inputs["ffn_conv_b"])[:NL].reshape(NL, NF, 128).transpose(0, 2, 1))
    for k in ("ident", "ropeA_c", "ropeA_s", "ropeC_c", "ropeC_s", "maskW", "zT", "tneg", "fw", "inv", "ones_bf", "ones_f"):
        m[k] = c[k]
    return m


_PROG = {}


def kernel(**inputs):
    NB = 4
    x = np.ascontiguousarray(np.asarray(inputs["x"], dtype=np.float32))
    shared = make_inputs(inputs)
    if 'nc' not in _PROG:
        _PROG['nc'] = build_program(NB=NB, NL=2)
    nc = _PROG['nc']
    in_maps = []
    for c in range(NCORES):
        d = dict(shared)
        d["x"] = np.ascontiguousarray(x[c * NB:(c + 1) * NB])
        in_maps.append(d)
    res = run_bass_kernel_spmd(nc, in_maps, core_ids=list(range(NCORES)))
    return np.concatenate([np.asarray(r["out"], dtype=np.float32) for r in res.results], axis=0)
```

```python
import math
from contextlib import ExitStack

import numpy as np
import ml_dtypes

import concourse.bass as bass
import concourse.mybir as mybir
from concourse.bass_utils import run_bass_kernel_spmd

F32 = mybir.dt.float32
BF16 = mybir.dt.bfloat16
AF = mybir.ActivationFunctionType
ALU = mybir.AluOpType
AX = mybir.AxisListType

S_ = 2048
D_ = 1024
NT = 16
PW = 2560
DFF = 2816
NF = 22
EPS = 1e-6
NCORES = 8
ATT_NDUM = 0
A_RECIP_ACT = True

COMPUTE = ('pe', 'act', 'dve', 'pool')
ENGS = ('pe', 'act', 'dve', 'pool', 'sp')


class _Op:
    __slots__ = ('eng', 'fn', 'deps', 'need', 'val', 'dsem', 'is_dma', 'bar')

    def __init__(self, eng, fn):
        self.eng = eng
        self.fn = fn
        self.deps = []
        self.need = False
        self.val = None
        self.dsem = None
        self.is_dma = False
        self.bar = None


class Sched:
    def __init__(self, nc):
        self.nc = nc
        self.eops = {e: [] for e in ENGS}
        self.last_w = {}
        self.readers = {}
        self.dma_cnt = {}
        self.dma_last = {}
        self.n_ops = 0

    def _track(self, op, reads, writes):
        deps = {}
        for k in reads:
            w = self.last_w.get(k)
            if w is not None:
                deps[id(w)] = w
        for k in writes:
            w = self.last_w.get(k)
            if w is not None:
                deps[id(w)] = w
            for r in self.readers.get(k, {}).values():
                if isinstance(r, list):
                    for rr in r:
                        deps[id(rr)] = rr
                else:
                    deps[id(r)] = r
        for k in reads:
            d = self.readers.setdefault(k, {})
            if op.is_dma:
                d.setdefault('dma', []).append(op)
            else:
                d[op.eng] = op
        for k in writes:
            self.last_w[k] = op
            self.readers[k] = {}
        for d in deps.values():
            if d is op:
                continue
            if (not d.is_dma) and d.eng == 'pe' and op.eng == 'pe' and not op.is_dma:
                continue
            d.need = True
            op.deps.append((d, self.dma_cnt[d.dsem] if d.is_dma else None))

    def op(self, eng, fn, reads=(), writes=()):
        o = _Op(eng, fn)
        self._track(o, reads, writes)
        self.eops[eng].append(o)
        self.n_ops += 1
        return o

    def dma(self, out, in_, sem, reads=(), writes=(), eng='sp', **kw):
        fn = lambda e, o=out, i=in_: e.dma_start(out=o, in_=i, **kw)
        o = _Op(eng, fn)
        o.is_dma = True
        o.dsem = sem
        self.dma_cnt.setdefault(sem, 0)
        self._track(o, reads, writes)
        self.dma_cnt[sem] += 16
        o.val = self.dma_cnt[sem]
        o.need = True
        self.eops[eng].append(o)
        self.dma_last[sem] = o
        self.n_ops += 1
        return o

    def barrier(self):
        lasts = []
        for e in COMPUTE:
            for o in reversed(self.eops[e]):
                if not o.is_dma and o.bar is None:
                    o.need = True
                    lasts.append(o)
                    break
        dl = list(self.dma_last.values())
        for e in ENGS:
            b = _Op(e, None)
            b.bar = True
            b.deps = [(o, None) for o in lasts] + [(o, None) for o in dl]
            self.eops[e].append(b)
        self.last_w = {}
        self.readers = {}

    def emit(self):
        nc = self.nc
        for e in COMPUTE:
            c = 0
            for o in self.eops[e]:
                if o.is_dma or o.bar:
                    continue
                if o.need:
                    c += 1
                    o.val = c
        with ExitStack() as es:
            sems = {}
            for e in COMPUTE:
                sems[e] = es.enter_context(nc.semaphore("s_" + e))
            for k in self.dma_cnt:
                sems['d_' + k] = es.enter_context(nc.semaphore("d_" + k))
            block = es.enter_context(nc.Block())

            def semof(o):
                return sems['d_' + o.dsem] if o.is_dma else sems[o.eng]

            def run(engname, eh):
                waited = {}
                for o in self.eops[engname]:
                    need = {}
                    for d, ov in o.deps:
                        s = semof(d)
                        dv = ov if ov is not None else d.val
                        if dv > need.get(s.num, (0, None))[0]:
                            need[s.num] = (dv, s)
                    for key, (v, s) in need.items():
                        if waited.get(key, 0) >= v:
                            continue
                        eh.wait_ge(s, v)
                        waited[key] = v
                    if o.bar:
                        continue
                    ins = o.fn(eh)
                    if o.is_dma:
                        ins.then_inc(sems['d_' + o.dsem], 16)
                    elif o.need:
                        ins.then_inc(sems[o.eng], 1)

            @block.tensor
            def _(eh):
                run('pe', eh)

            @block.scalar
            def _(eh):
                run('act', eh)

            @block.vector
            def _(eh):
                run('dve', eh)

            @block.gpsimd
            def _(eh):
                run('pool', eh)

            @block.sync
            def _(eh):
                run('sp', eh)


def _tok_tiled(a):
    return np.ascontiguousarray(a.reshape(NT, 128, -1).transpose(1, 0, 2))


_CONST_CACHE = {}


def host_consts():
    if _CONST_CACHE:
        return _CONST_CACHE
    c = {}
    bf = ml_dtypes.bfloat16
    c['ident'] = np.eye(128, dtype=np.float32).astype(bf)
    pos = np.arange(S_, dtype=np.float32)
    fr = (10000.0 ** (-np.arange(0, 64, 2, dtype=np.float32) / 64)).astype(np.float32)
    ang = pos[:, None] * fr[None, :]
    co, si = np.cos(ang), np.sin(ang)
    c['ropeA_c'] = _tok_tiled(np.concatenate([co, co], 1).astype(np.float32))
    c['ropeA_s'] = _tok_tiled(np.concatenate([-si, si], 1).astype(np.float32))
    rows = (np.arange(S_) // 64).astype(np.float32)
    cols = (np.arange(S_) % 64).astype(np.float32)
    fr2 = (10000.0 ** (-np.arange(0, 32, 2, dtype=np.float32) / 32)).astype(np.float32)
    ar, ac = rows[:, None] * fr2[None, :], cols[:, None] * fr2[None, :]
    c['ropeC_c'] = _tok_tiled(np.concatenate([np.cos(ar), np.cos(ar), np.cos(ac), np.cos(ac)], 1).astype(np.float32))
    c['ropeC_s'] = _tok_tiled(np.concatenate([-np.sin(ar), np.sin(ar), -np.sin(ac), np.sin(ac)], 1).astype(np.float32))
    d = np.arange(128)[:, None] - np.arange(3968)[None, :] + 1920
    ad = np.abs(d)
    m = (ad <= 64).astype(np.float32) + ((d % 4 == 0) & (ad <= 256)) + ((d % 16 == 0) & (ad <= 1024))
    c['maskW'] = m.astype(bf)
    L = S_
    t = np.linspace(0.0, 1.0, L, dtype=np.float32)
    bands = np.linspace(1e-4, 15, 16, dtype=np.float32)
    angz = (2.0 * math.pi * bands[None, :] * np.arange(L, dtype=np.float32)[:, None] / L).astype(np.float32)
    z = np.concatenate([t[:, None], np.cos(angz), -np.sin(angz)], -1).astype(np.float32)
    c['zT'] = np.ascontiguousarray(z.T)
    c['tneg'] = np.ascontiguousarray((-t).reshape(NT, 128).T.astype(np.float32))
    N = 2 * L
    tt = np.arange(L, dtype=np.float64)[:, None]
    kk = (np.arange(L, dtype=np.float64)[None, :] + 0.5)
    w = 2.0 * math.pi * tt * kk / N
    Fw = np.concatenate([np.cos(w), -np.sin(w)], 1)
    c['fw'] = np.ascontiguousarray(Fw.reshape(NT, 128, 32, 128).transpose(2, 1, 0, 3)).astype(np.float32).astype(bf)
    c['inv'] = np.ascontiguousarray(((2.0 / N) * Fw).reshape(NT, 128, 32, 128).transpose(0, 3, 2, 1)).astype(np.float32).astype(bf)
    c['ones_bf'] = np.ones((128, 128), dtype=np.float32).astype(bf)
    c['ones_f'] = np.ones((128, 128), dtype=np.float32)
    _CONST_CACHE.update(c)
    return c


def build_program(NB=4, NL=2, stages="WPACFHOM", dbg=()):
    nc = bass.Bass("TRN2", target_bir_lowering=False)
    S = Sched(nc)
    _cnt = [0]

    def SB(name, shape, dt):
        _cnt[0] += 1
        return nc.sbuf_tensor("%s_u%d" % (name, _cnt[0]), shape, dt)

    def PS(name, shape, dt):
        _cnt[0] += 1
        return nc.psum_tensor("%s_u%d" % (name, _cnt[0]), shape, dt)

    def din(name, shape, dt=F32):
        return nc.dram_tensor(name, list(shape), dt, kind="ExternalInput").ap()

    def dscr(name, shape, dt=F32):
        kind = "ExternalOutput" if name in dbg else "Internal"
        return nc.dram_tensor(name, list(shape), dt, kind=kind).ap()

    x_in = din("x", [NB, S_, D_])
    w_in = din("w_in", [NL, D_, PW])
    w_out = din("w_out", [NL, D_, D_])
    w_gate = din("ffn_w_gate", [NL, D_, DFF])
    w_up = din("ffn_w_up", [NL, D_, DFF])
    w_down = din("ffn_w_down", [NL, DFF, D_])
    g_pre_col = din("g_mix_pre_col", [NL, 128, 8])
    g_fpre_col = din("g_ffn_pre_col", [NL, 128, 8])
    g_post = din("g_mix_post", [NL, D_])
    g_fpost = din("g_ffn_post", [NL, D_])
    g_qk = din("g_qk", [NL, 8 * 64])
    hy_cw = din("hy_cw", [NL, 128, 6, 3])
    hy_cb = din("hy_cb", [NL, 128, 6])
    hy_w1 = din("hy_w1", [NL, 33, 64])
    hy_b1 = din("hy_b1c", [NL, 64, 1])
    hy_fr = din("hy_freqc", [NL, 64, 2])
    hy_w2 = din("hy_w2", [NL, 64, 64])
    hy_b2 = din("hy_b2c", [NL, 64, 1])
    hy_w3 = din("hy_w3", [NL, 64, 1024])
    hy_dec = din("hy_decay", [NL, 1024])
    hy_d = din("hy_d", [NL, 512])
    f_cw = din("ffn_cw", [NL, 128, NF, 3])
    f_cb = din("ffn_cb", [NL, 128, NF])
    c_ident = din("ident", [128, 128], BF16)
    c_rAc = din("ropeA_c", [128, NT, 64])
    c_rAs = din("ropeA_s", [128, NT, 64])
    c_rCc = din("ropeC_c", [128, NT, 64])
    c_rCs = din("ropeC_s", [128, NT, 64])
    c_mask = din("maskW", [128, 3968], BF16)
    c_zT = din("zT", [33, S_])
    c_tneg = din("tneg", [128, NT])
    c_fw = din("fw", [32, 128, NT, 128], BF16)
    c_inv = din("inv", [NT, 128, 32, 128], BF16)
    c_ones_bf = din("ones_bf", [128, 128], BF16)
    c_ones_f = din("ones_f", [128, 128])

    out = nc.dram_tensor("out", [NB, S_, D_], F32, kind="ExternalOutput").ap()

    wbf_in = dscr("wbf_in", [NL, D_, PW], BF16)
    wbf_out = dscr("wbf_out", [NL, D_, D_], BF16)
    wbf_gate = dscr("wbf_gate", [NL, D_, DFF], BF16)
    wbf_up = dscr("wbf_up", [NL, D_, DFF], BF16)
    wbf_down = dscr("wbf_down", [NL, DFF, D_], BF16)
    qTA = dscr("qTA", [NB, 3, 128, S_], BF16)
    kTA = dscr("kTA", [NB, 3, 128, S_], BF16)
    vA = dscr("vA", [NB, 128, NT, 6 * 65], BF16)
    qTC = dscr("qTC", [NB, 3, 128, S_], BF16)
    kTC = dscr("kTC", [NB, 128, S_], BF16)
    vC = dscr("vC", [NB, 128, NT, 2 * 65], BF16)
    hyraw = dscr("hyraw", [NB, 6, 128, S_], BF16)
    oT = dscr("oT", [NB, D_, S_], BF16)
    x1tm = dscr("x1tm", [NT, 128, NB * 256], BF16)
    x2tm = dscr("x2tm", [NT, 128, NB * 256], BF16)
    kfd = dscr("kfd", [32, 128, 512], F32)
    xmid = dscr("xmid", [NB, S_, D_], F32)
    xl = [x_in] + [dscr("xl%d" % i, [NB, S_, D_], F32) for i in range(1, NL)] + [out]
    xl[NL] = out

    with ExitStack() as top:
        T = top.enter_context
        ident = T(SB("sb_ident", [128, 128], BF16))
        ones_bf = T(SB("sb_ones_bf", [128, 128], BF16))
        ones_f = T(SB("sb_ones_f", [128, 128], F32))
        S.dma(ident[:], c_ident[:, :], 'c0', writes=['ident'])
        S.dma(ones_bf[:], c_ones_bf[:, :], 'c0', writes=['ones_bf'])
        S.dma(ones_f[:], c_ones_f[:, :], 'c0', writes=['ones_f'])
        S.barrier()

        if 'W' in stages:
            jobs_w = []
            for l in range(NL):
                for i, (dst, src, rows) in enumerate(((wbf_in, w_in, D_), (wbf_out, w_out, D_), (wbf_gate, w_gate, D_),
                                                      (wbf_up, w_up, D_), (wbf_down, w_down, DFF))):
                    jobs_w.append((l, i, dst, src, rows))
            for wi, (l, i, dst, src, rows) in enumerate(jobs_w):
                nchunk = 8 if rows == D_ else 11
                rc = rows // nchunk
                for ch in range(nchunk):
                    S.dma(dst[l, ch * rc:(ch + 1) * rc, :], src[l, ch * rc:(ch + 1) * rc, :], 'wc%d' % (i % 2), eng='pool',
                          writes=[])
            S.barrier()

        def norm_transpose(ctx, xsrc_tile, gcol, hT_dst, key_hT, slot, pT, pfx, kpT=None):
            xt, sq, ssq, xn = ctx['xt'][slot], ctx['sq'], ctx['ssq'][slot], ctx['xn'][slot]
            kx, kn = pfx + 'xt%d' % slot, pfx + 'xn%d' % slot
            kpT = kpT or (pfx + 'pT')
            S.dma(xt[:], xsrc_tile, pfx + 'x%d' % slot, writes=[kx])
            S.op('pool', lambda e: e.memset(ssq[:], 0.0), writes=[pfx + 'ssq%d' % slot])
            S.op('act', lambda e: e.activation(sq[:], xt[:], AF.Square, accum_out=ssq[:]), reads=[kx], writes=[pfx + 'sq', pfx + 'ssq%d' % slot])
            S.op('dve', lambda e: e.tensor_scalar(ssq[:], ssq[:], 1.0 / D_, EPS, ALU.mult, ALU.add), reads=[pfx + 'ssq%d' % slot], writes=[pfx + 'ssq%d' % slot])
            S.op('act', lambda e: e.activation(ssq[:], ssq[:], AF.Sqrt), reads=[pfx + 'ssq%d' % slot], writes=[pfx + 'ssq%d' % slot])
            S.op('dve', lambda e: e.reciprocal(ssq[:], ssq[:]), reads=[pfx + 'ssq%d' % slot], writes=[pfx + 'ssq%d' % slot])
            S.op('act', lambda e: e.activation(xn[:], xt[:], AF.Copy, scale=ssq[:]), reads=[kx, pfx + 'ssq%d' % slot], writes=[kn])
            for c in range(8):
                S.op('pe', lambda e, c=c: e.transpose(pT[:, c, :], xn[:, c * 128:(c + 1) * 128], ident[:]), reads=[kn, 'ident'], writes=[kpT])
            S.op('dve', lambda e: e.tensor_tensor(hT_dst, pT[:, :, :], gcol[:, :].unsqueeze(2).broadcast_to([128, 8, 128]), ALU.mult),
                 reads=[kpT, 'gcol'], writes=[key_hT])

        def norm_partA(ctx, jobs, pfx):
            for (src, slot) in jobs:
                xt, ssq = ctx['xt'][slot], ctx['ssq'][slot]
                S.dma(xt[:], src, pfx + 'x%d' % slot, writes=[pfx + 'xt%d' % slot])
            for (src, slot) in jobs:
                xt, ssq, xn = ctx['xt'][slot], ctx['ssq'][slot], ctx['xn'][slot]
                S.op('act', lambda e, xt=xt, ssq=ssq, xn=xn: e.activation(xn[:], xt[:], AF.Square, accum_out=ssq[:]),
                     reads=[pfx + 'xt%d' % slot], writes=[pfx + 'xn%d' % slot, pfx + 'ssq%d' % slot])
            for (src, slot) in jobs:
                ssq = ctx['ssq'][slot]
                S.op('dve', lambda e, ssq=ssq: e.tensor_scalar(ssq[:], ssq[:], 1.0 / D_, EPS, ALU.mult, ALU.add), reads=[pfx + 'ssq%d' % slot], writes=[pfx + 'ssq%d' % slot])
            for (src, slot) in jobs:
                ssq = ctx['ssq'][slot]
                S.op('act', lambda e, ssq=ssq: e.activation(ssq[:], ssq[:], AF.Sqrt), reads=[pfx + 'ssq%d' % slot], writes=[pfx + 'ssq%d' % slot])
            for (src, slot) in jobs:
                ssq = ctx['ssq'][slot]
                S.op('dve', lambda e, ssq=ssq: e.reciprocal(ssq[:], ssq[:]), reads=[pfx + 'ssq%d' % slot], writes=[pfx + 'ssq%d' % slot])
            for (src, slot) in jobs:
                xt, ssq, xn = ctx['xt'][slot], ctx['ssq'][slot], ctx['xn'][slot]
                S.op('act', lambda e, xt=xt, ssq=ssq, xn=xn: e.activation(xn[:], xt[:], AF.Copy, scale=ssq[:]),
                     reads=[pfx + 'xt%d' % slot, pfx + 'ssq%d' % slot], writes=[pfx + 'xn%d' % slot])

        def norm_partB(ctx, slot, gcol, hT_dst, key_hT, pT, kpT, pfx):
            xn = ctx['xn'][slot]
            for c in range(8):
                S.op('pe', lambda e, c=c: e.transpose(pT[:, c, :], xn[:, c * 128:(c + 1) * 128], ident[:]), reads=[pfx + 'xn%d' % slot, 'ident'], writes=kpT)
            S.op('dve', lambda e: e.tensor_tensor(hT_dst, pT[:, :, :], gcol[:, :].unsqueeze(2).broadcast_to([128, 8, 128]), ALU.mult),
                 reads=list(kpT) + ['gcol'], writes=[key_hT])

        def post_load(ctx, xsrc_tile, slot, pfx):
            S.dma(ctx['xt'][slot][:], xsrc_tile, pfx + 'x%d' % slot, writes=[pfx + 'xt%d' % slot])

        def post_part1(ctx, ps, kps, xsrc_tile, slot, pfx):
            xt, ssq, yn = ctx['xt'][slot], ctx['ssq'][slot], ctx['yn'][slot]
            kx, ks, ky = pfx + 'xt%d' % slot, pfx + 'ssq%d' % slot, pfx + 'yn%d' % slot
            if xsrc_tile is not None:
                S.dma(xt[:], xsrc_tile, pfx + 'x%d' % slot, writes=[kx])
            S.op('act', lambda e: e.activation(yn[:], ps, AF.Square, accum_out=ssq[:]), reads=list(kps), writes=[ky, ks])
            S.op('dve', lambda e: e.tensor_scalar(ssq[:], ssq[:], 1.0 / D_, EPS, ALU.mult, ALU.add), reads=[ks], writes=[ks])
            S.op('act', lambda e: e.activation(ssq[:], ssq[:], AF.Sqrt), reads=[ks], writes=[ks])
            S.op('dve', lambda e: e.reciprocal(ssq[:], ssq[:]), reads=[ks], writes=[ks])

        def post_part2(ctx, ps, kps, gB, dst_tile, slot, pfx):
            xt, ssq, yn = ctx['xt'][slot], ctx['ssq'][slot], ctx['yn'][slot]
            kx, ks, ky = pfx + 'xt%d' % slot, pfx + 'ssq%d' % slot, pfx + 'yn%d' % slot
            S.op('dve', lambda e: e.scalar_tensor_tensor(yn[:], ps, ssq[:], gB[:], ALU.mult, ALU.mult), reads=list(kps) + [ks, 'gB'], writes=[ky])
            S.op('pool', lambda e: e.tensor_tensor(yn[:], yn[:], xt[:], ALU.add), reads=[ky, kx], writes=[ky])
            S.dma(dst_tile, yn[:], pfx + 'st%d' % slot, reads=[ky], writes=[])

        def post_norm_residual(ctx, ps, kps, gB, xsrc_tile, dst_tile, slot, pfx):
            xt, sq, ssq, yn = ctx['xt'][slot], ctx['sq'], ctx['ssq'][slot], ctx['yn'][slot]
            kx = pfx + 'rx%d' % slot
            ks = pfx + 'rs%d' % slot
            ky = pfx + 'ry%d' % slot
            S.dma(xt[:], xsrc_tile, pfx + 'rx%d' % slot, reads=[pfx + 'st%d' % slot], writes=[kx])
            S.op('pool', lambda e: e.memset(ssq[:], 0.0), writes=[ks])
            S.op('act', lambda e: e.activation(sq[:], ps, AF.Square, accum_out=ssq[:]), reads=[kps], writes=[pfx + 'sq', ks])
            S.op('dve', lambda e: e.tensor_scalar(ssq[:], ssq[:], 1.0 / D_, EPS, ALU.mult, ALU.add), reads=[ks], writes=[ks])
            S.op('act', lambda e: e.activation(ssq[:], ssq[:], AF.Sqrt), reads=[ks], writes=[ks])
            S.op('dve', lambda e: e.reciprocal(ssq[:], ssq[:]), reads=[ks], writes=[ks])
            S.op('dve', lambda e: e.scalar_tensor_tensor(yn[:], ps, ssq[:], gB[:], ALU.mult, ALU.mult), reads=[kps, ks, 'gB'], writes=[ky])
            S.op('pool', lambda e: e.tensor_tensor(yn[:], yn[:], xt[:], ALU.add), reads=[ky, kx], writes=[ky])
            S.dma(dst_tile, yn[:], pfx + 'st%d' % slot, reads=[ky], writes=[pfx + 'st%d' % slot])

        for l in range(NL):
            xcur = xl[l]
            xnext = xl[l + 1]
            def stage_P(l=l, xcur=xcur, xnext=xnext):
                with ExitStack() as st:
                    A = st.enter_context
                    wsb = A(SB("p_w", [128, 8, PW], BF16))
                    gcol = A(SB("p_gcol", [128, 8], F32))
                    rAc = A(SB("p_rAc", [128, NT, 64], F32))
                    rAs = A(SB("p_rAs", [128, NT, 64], F32))
                    rCc = A(SB("p_rCc", [128, NT, 64], F32))
                    rCs = A(SB("p_rCs", [128, NT, 64], F32))
                    g8 = A(SB("p_g8", [128, 8, 64], F32))
                    ctx = dict(xt=[A(SB("p_xt%d" % i, [128, D_], F32)) for i in range(8)],
                               ssq=[A(SB("p_ssq%d" % i, [128, 1], F32)) for i in range(8)],
                               xn=[A(SB("p_xn%d" % i, [128, D_], BF16)) for i in range(8)])
                    hT2 = [A(SB("p_hT%d" % i, [128, 8, 512], BF16)) for i in range(2)]
                    tsets = [[(A(SB("p_t1_%d" % i, [128, 8, 64], F32)), A(SB("p_t2_%d" % i, [128, 8, 64], F32)), A(SB("p_t3_%d" % i, [128, 8, 64], F32)))
                              for i in range(6)]][0]
                    ss8 = A(SB("p_ss8", [128, 8], F32))
                    qka2 = [A(SB("p_qka%d" % i, [128, 12, 64], BF16)) for i in range(2)]
                    qkc2 = [A(SB("p_qkc%d" % i, [128, 8, 64], BF16)) for i in range(2)]
                    vAs = [A(SB("p_vA%d" % i, [128, 6, 65], BF16)) for i in range(2)]
                    vCs = [A(SB("p_vC%d" % i, [128, 2, 65], BF16)) for i in range(2)]
                    qTs2 = [A(SB("p_qTs%d" % i, [128, 10, 512], BF16)) for i in range(2)]
                    hys = [A(SB("p_hys%d" % i, [128, 512], BF16)) for i in range(2)]
                    pT = A(PS("p_pT", [128, 8, 128], BF16))
                    pj = [A(PS("p_pj%d" % i, [128, 512], F32)) for i in range(3)]
                    pq = [A(PS("p_pq%d" % i, [128, 8, 128], BF16)) for i in range(2)]
                    ph = [A(PS("p_ph%d" % i, [128, 512], F32)) for i in range(2)]

                    S.dma(wsb[:], wbf_in[l].rearrange("(c p) n -> p c n", p=128), 'pw', writes=['wsb'])
                    S.dma(gcol[:], g_pre_col[l], 'pc', writes=['gcol'])
                    S.dma(rAc[:], c_rAc[:, :, :], 'pc', writes=['rAc'])
                    S.dma(rAs[:], c_rAs[:, :, :], 'pc', writes=['rAs'])
                    S.dma(rCc[:], c_rCc[:, :, :], 'pc', writes=['rCc'])
                    S.dma(rCs[:], c_rCs[:, :, :], 'pc', writes=['rCs'])
                    S.dma(g8[:].rearrange("p h e -> p (h e)"), g_qk[l:l + 1, :].broadcast_to([128, 512]), 'pc', writes=['g8'])
                    for i in range(2):
                        S.op('pool', lambda e, i=i: e.memset(vAs[i][:], 1.0), writes=['vAs%d' % i])
                        S.op('pool', lambda e, i=i: e.memset(vCs[i][:], 1.0), writes=['vCs%d' % i])

                    groups = [
                        [(0, 0, 384)],
                        [(0, 384, 384)],
                        [(0, 768, 384), (384, 2432, 128)],
                        [(0, 1920, 512)],
                    ]
                    pjn = 0
                    phn = 0
                    GL = [(b, tg) for b in range(NB) for tg in range(4)]
                    pend_tr = []

                    def grp_jobs(k):
                        b, tg = GL[k]
                        return [(xcur[b, (tg * 4 + ti) * 128:(tg * 4 + ti + 1) * 128, :], (k % 2) * 4 + ti) for ti in range(4)]

                    def partB(k, ti):
                        norm_partB(ctx, (k % 2) * 4 + ti, gcol, hT2[k % 2][:, :, ti * 128:(ti + 1) * 128], 'hT%d_%d' % (k % 2, ti), pT, ['p_pT'], 'p_')

                    norm_partA(ctx, grp_jobs(0), 'p_')
                    for ti in range(4):
                        partB(0, ti)
                    for k, (b, tg) in enumerate(GL):
                            hT = hT2[k % 2]
                            qTs = qTs2[k % 2]
                            kq_ = 'qTs%d' % (k % 2)
                            hk = ['hT%d_%d' % (k % 2, i_) for i_ in range(4)]
                            if k + 1 < len(GL):
                                norm_partA(ctx, grp_jobs(k + 1), 'p_')
                            for ti in range(4):
                                tt = tg * 4 + ti
                                slot = tt % 2
                                lhs = lambda c, ti=ti, hT=hT: hT[:, c, ti * 128:(ti + 1) * 128]
                                qka, qkc = qka2[ti % 2], qkc2[ti % 2]
                                kqa, kqc = 'qka%d' % (ti % 2), 'qkc%d' % (ti % 2)
                                pss = []
                                for gi, grp in enumerate(groups):
                                    tsi = (ti % 2) * 3 + (gi if gi < 2 else 2)
                                    t1, t2, t3 = tsets[tsi]
                                    k1_, k2_, k3_ = 't1_%d' % tsi, 't2_%d' % tsi, 't3_%d' % tsi
                                    ps = pj[pjn % 3]
                                    kp = 'pj%d' % (pjn % 3)
                                    pjn += 1
                                    for (po, wo, wd) in grp:
                                        for c in range(8):
                                            S.op('pe', lambda e, ps=ps, po=po, wo=wo, wd=wd, c=c, lhs=lhs: e.matmul(
                                                ps[:, po:po + wd], lhs(c), wsb[:, c, wo:wo + wd], start=(c == 0), stop=(c == 7)),
                                                reads=[hk[ti], 'wsb'], writes=[kp])
                                    if gi in (0, 1):
                                        x3 = ps[:, 0:384].rearrange("p (h e) -> p h e", e=64)
                                        cA = rAc[:, tt, :].unsqueeze(1).broadcast_to([128, 6, 64])
                                        sAlo = rAs[:, tt, 0:32].unsqueeze(1).broadcast_to([128, 6, 32])
                                        sAhi = rAs[:, tt, 32:64].unsqueeze(1).broadcast_to([128, 6, 32])
                                        S.op('dve', lambda e, x3=x3, cA=cA, t1=t1: e.tensor_tensor(t1[:, 0:6, :], x3, cA, ALU.mult), reads=[kp, 'rAc'], writes=[k1_])
                                        S.op('dve', lambda e, x3=x3, sAlo=sAlo, t2=t2: e.tensor_tensor(t2[:, 0:6, 0:32], x3[:, :, 32:64], sAlo, ALU.mult), reads=[kp, 'rAs'], writes=[k2_])
                                        S.op('dve', lambda e, x3=x3, sAhi=sAhi, t2=t2: e.tensor_tensor(t2[:, 0:6, 32:64], x3[:, :, 0:32], sAhi, ALU.mult), reads=[kp, 'rAs'], writes=[k2_])
                                        S.op('pool', lambda e, gi=gi, qka=qka, t1=t1, t2=t2: e.tensor_tensor(qka[:, gi * 6:(gi + 1) * 6, :], t1[:, 0:6, :], t2[:, 0:6, :], ALU.add),
                                             reads=[k1_, k2_], writes=[kqa])
                                    elif gi == 2:
                                        va, vc = vAs[slot], vCs[slot]
                                        S.op('act', lambda e, ps=ps, va=va: e.activation(va[:, :, 0:64], ps[:, 0:384].rearrange("p (h e) -> p h e", e=64), AF.Copy),
                                             reads=[kp, 'vst%d' % slot], writes=['vAs%d' % slot])
                                        S.op('act', lambda e, ps=ps, vc=vc: e.activation(vc[:, :, 0:64], ps[:, 384:512].rearrange("p (h e) -> p h e", e=64), AF.Copy),
                                             reads=[kp, 'vst%d' % slot], writes=['vCs%d' % slot])
                                        S.dma(vA[b, :, tt, :], va[:].rearrange("p h e -> p (h e)"), 'pv%d' % slot, reads=['vAs%d' % slot], writes=['vst%d' % slot])
                                        S.dma(vC[b, :, tt, :], vc[:].rearrange("p h e -> p (h e)"), 'pv%d' % slot, reads=['vCs%d' % slot], writes=['vst%d' % slot])
                                    else:
                                        x3 = ps[:, :].rearrange("p (h e) -> p h e", e=64)
                                        S.op('act', lambda e, x3=x3, t1=t1: e.activation(t1[:], x3, AF.Square), reads=[kp], writes=[k1_])
                                        S.op('dve', lambda e, t1=t1: e.reduce_sum(ss8[:], t1[:], axis=AX.X), reads=[k1_], writes=['ss8'])
                                        S.op('dve', lambda e: e.tensor_scalar(ss8[:], ss8[:], 1.0 / 64, EPS, ALU.mult, ALU.add), reads=['ss8'], writes=['ss8'])
                                        S.op('act', lambda e: e.activation(ss8[:], ss8[:], AF.Sqrt), reads=['ss8'], writes=['ss8'])
                                        S.op('dve', lambda e: e.reciprocal(ss8[:], ss8[:]), reads=['ss8'], writes=['ss8'])
                                        S.op('dve', lambda e, x3=x3, t3=t3: e.tensor_tensor(t3[:], x3, ss8[:, :].unsqueeze(2).broadcast_to([128, 8, 64]), ALU.mult),
                                             reads=[kp, 'ss8'], writes=[k3_])
                                        S.op('dve', lambda e, t3=t3: e.tensor_tensor(t3[:], t3[:], g8[:], ALU.mult), reads=[k3_, 'g8'], writes=[k3_])
                                        cC = rCc[:, tt, :].unsqueeze(1).broadcast_to([128, 8, 64])
                                        S.op('dve', lambda e, cC=cC, t1=t1, t3=t3: e.tensor_tensor(t1[:], t3[:], cC, ALU.mult), reads=[k3_, 'rCc'], writes=[k1_])
                                        t3v = t3[:].rearrange("p h (a b c) -> p h a b c", a=2, b=2)
                                        t2v = t2[:].rearrange("p h (a b c) -> p h a b c", a=2, b=2)
                                        sv = rCs[:, tt, :].rearrange("p (a b c) -> p a b c", a=2, b=2)
                                        for hb in range(2):
                                            sC = sv[:, :, hb, :].unsqueeze(1).broadcast_to([128, 8, 2, 16])
                                            S.op('dve', lambda e, hb=hb, sC=sC, t3v=t3v, t2v=t2v: e.tensor_tensor(t2v[:, :, :, hb, :], t3v[:, :, :, 1 - hb, :], sC, ALU.mult),
                                                 reads=[k3_, 'rCs'], writes=[k2_])
                                        S.op('pool', lambda e, qkc=qkc, t1=t1, t2=t2: e.tensor_tensor(qkc[:, 0:6, :].rearrange("p (j two) e -> p two j e", two=2),
                                                                               t1[:, 0:6, :].rearrange("p (two j) e -> p two j e", two=2),
                                                                               t2[:, 0:6, :].rearrange("p (two j) e -> p two j e", two=2), ALU.add),
                                             reads=[k1_, k2_], writes=[kqc])
                                        S.op('pool', lambda e, qkc=qkc, t1=t1, t2=t2: e.tensor_tensor(qkc[:, 6:8, :], t1[:, 6:8, :], t2[:, 6:8, :], ALU.add), reads=[k1_, k2_, kqc], writes=[kqc])
                                def do_tr(ti=ti, qTs=qTs, qka=qka, qkc=qkc, kqa=kqa, kqc=kqc, kq_=kq_):
                                    for j in range(6):
                                        S.op('pe', lambda e, j=j: e.transpose(pq[0][:, j, :], qka[:, 2 * j:2 * j + 2, :], ident[:]), reads=[kqa, 'ident'], writes=['pq0'])
                                    S.op('act', lambda e: e.activation(qTs[:, 0:6, ti * 128:(ti + 1) * 128], pq[0][:, 0:6, :], AF.Copy), reads=['pq0'], writes=[kq_])
                                    for j in range(3):
                                        S.op('pe', lambda e, j=j: e.transpose(pq[1][:, j, :], qkc[:, 2 * j:2 * j + 2, :], ident[:]), reads=[kqc, 'ident'], writes=['pq1'])
                                    S.op('pe', lambda e: e.transpose(pq[1][:, 3, :], qkc[:, 6:8, :], ident[:]), reads=[kqc, 'ident'], writes=['pq1'])
                                    S.op('dve', lambda e: e.tensor_copy(qTs[:, 6:10, ti * 128:(ti + 1) * 128], pq[1][:, 0:4, :]), reads=['pq1'], writes=[kq_])
                                if pend_tr:
                                    pend_tr.pop(0)()
                                pend_tr.append(do_tr)
                                if k + 1 < len(GL):
                                    partB(k + 1, ti)
                            for ct in range(6):
                                ps = ph[phn % 2]
                                kp = 'ph%d' % (phn % 2)
                                hs = hys[phn % 2]
                                kh = 'hys%d' % (phn % 2)
                                phn += 1
                                for c in range(8):
                                    S.op('pe', lambda e, ps=ps, c=c, ct=ct, hT=hT: e.matmul(ps[:, :], wsb[:, c, 1152 + ct * 128:1152 + (ct + 1) * 128], hT[:, c, :],
                                                                                     start=(c == 0), stop=(c == 7)),
                                         reads=hk + ['wsb'], writes=[kp])
                                S.op('act', lambda e, ps=ps, hs=hs: e.activation(hs[:], ps[:, :], AF.Copy), reads=[kp, kh + 'st'], writes=[kh])
                                S.dma(hyraw[b, ct, :, tg * 512:(tg + 1) * 512], hs[:], 'ph%d' % (phn % 2), reads=[kh], writes=[kh + 'st'])
                            while pend_tr:
                                pend_tr.pop(0)()
                            tsl = slice(tg * 512, (tg + 1) * 512)
                            S.dma(qTA[b, :, :, tsl].rearrange("j p t -> p j t"), qTs[:, 0:3, :], 'pq%d' % (k % 2), reads=[kq_], writes=[])
                            S.dma(kTA[b, :, :, tsl].rearrange("j p t -> p j t"), qTs[:, 3:6, :], 'pq%d' % (k % 2), reads=[kq_], writes=[])
                            S.dma(qTC[b, :, :, tsl].rearrange("j p t -> p j t"), qTs[:, 6:9, :], 'pq%d' % (k % 2), reads=[kq_], writes=[])
                            S.dma(kTC[b, :, tsl], qTs[:, 9, :], 'pq%d' % (k % 2), reads=[kq_], writes=[])
                    S.barrier()

            def attention(kind, l=l):
                with ExitStack() as st:
                    A = st.enter_context
                    pfx = 'a' + kind
                    LA = 4
                    NPS, NET = 5, 7
                    qT = [A(SB(pfx + "_qT%d" % i, [128, S_], BF16)) for i in range(2)]
                    kT = [[A(SB(pfx + "_kT%d_%d" % (i, h), [128, S_], BF16)) for h in range(2)] for i in range(2)]
                    for i in range(2):
                        for h in range(2):
                            S.op('pool', lambda e, i=i, h=h: e.memset(kT[i][h][:], 0.0), writes=['kT%d' % i])
                    nvh = 6 if kind == 'A' else 2
                    vs = [A(SB(pfx + "_v%d" % i, [128, NT, nvh * 65], BF16)) for i in range(2)]
                    et = [A(SB(pfx + "_et%d" % i, [128, 512], BF16)) for i in range(NET)]
                    em = [A(SB(pfx + "_em%d" % i, [128, 512], BF16)) for i in range(NET)]
                    mask = A(SB(pfx + "_mask", [128, 3968], BF16))
                    rec = A(SB(pfx + "_rec", [128, 512], F32))
                    osb = A(SB(pfx + "_osb", [64, 512], F32))
                    ob = [A(SB(pfx + "_ob%d" % i, [64, 512], BF16)) for i in range(2)]
                    ps = [A(PS(pfx + "_ps%d" % i, [128, 512], F32)) for i in range(NPS)]
                    po = [A(PS(pfx + "_po%d" % i, [128, 512], F32)) for i in range(2)]
                    pb = A(PS(pfx + "_pb", [128, 512], F32))
                    pdum = None
                    NDUM = ATT_NDUM
                    if kind == 'A':
                        S.dma(mask[:], c_mask[:, :], 'am', writes=['mask'])
                    items = []
                    grp = 0
                    for b in range(NB):
                        for j in range(3):
                            for half in range(2):
                                for g in range(4):
                                    tiles = []
                                    for i in range(NT):
                                        dmin = i * 128 - g * 512 - 511
                                        dmax = i * 128 + 127 - g * 512
                                        if kind == 'A' and (dmin > 1024 or dmax < -1024):
                                            continue
                                        tiles.append(i)
                                    for ii, i in enumerate(tiles):
                                        items.append(dict(b=b, j=j, half=half, g=g, i=i, ii=ii, nt=len(tiles), grp=grp))
                                    grp += 1
                    cur = dict(b=-1, bj=-1, nq=0, nv=0)
                    part2 = []

                    def issue_loads(it):
                        b, j = it['b'], it['j']
                        if b != cur['b']:
                            cur['b'] = b
                            sl = cur['nv'] % 2
                            cur['nv'] += 1
                            cur['vt'], cur['kv'] = vs[sl], 'v%d' % sl
                            S.dma(vs[sl][:], (vA if kind == 'A' else vC)[b], 'av%d' % sl, writes=['v%d' % sl])
                            if kind == 'C':
                                cur['kt'], cur['kk'] = kT[b % 2], 'kT%d' % (b % 2)
                                for h in range(2):
                                    S.dma(kT[b % 2][h][64 * h:64 * h + 64, :], kTC[b, 64 * h:64 * h + 64, :], 'ak%d' % (b % 2), writes=['kT%d' % (b % 2)])
                        if (b, j) != cur['bj']:
                            cur['bj'] = (b, j)
                            sl = cur['nq'] % 2
                            cur['nq'] += 1
                            cur['qt'], cur['kq'] = qT[sl], 'qT%d' % sl
                            S.dma(qT[sl][:], (qTA if kind == 'A' else qTC)[b, j], 'aq%d' % sl, writes=['qT%d' % sl])
                            if kind == 'A':
                                cur['kt'], cur['kk'] = kT[sl], 'kT%d' % sl
                                for h in range(2):
                                    S.dma(kT[sl][h][64 * h:64 * h + 64, :], kTA[b, j, 64 * h:64 * h + 64, :], 'ak%d' % sl, writes=['kT%d' % sl])
                        for k_ in ('vt', 'kv', 'kt', 'kk', 'qt', 'kq'):
                            it[k_] = cur[k_]

                    def emit_qk(n, it):
                        base = 64 * it['half']
                        i, g = it['i'], it['g']
                        pst, kps = ps[n % NPS], 'ps%d' % (n % NPS)
                        ett, ke = et[n % NET], 'et%d' % (n % NET)
                        emt, kem = em[n % NET], 'em%d' % (n % NET)
                        kt, qt = it['kt'][it['half']], it['qt']
                        S.op('pe', lambda e: e.matmul(pst[:, :], kt[:, i * 128:(i + 1) * 128], qt[:, g * 512:(g + 1) * 512], start=True, stop=True),
                             reads=[it['kk'], it['kq']], writes=[kps])
                        S.op('act', lambda e: e.activation(ett[:], pst[:, :], AF.Exp, scale=0.125), reads=[kps], writes=[ke])
                        for _ in range(NDUM):
                            S.op('pe', lambda e: e.matmul(pdum[:, :], kt[:, i * 128:(i + 1) * 128], qt[:, g * 512:(g + 1) * 512], start=True, stop=True),
                                 reads=[it['kk'], it['kq']], writes=['pdum'])
                        if kind == 'A':
                            x0 = g * 512 - i * 128 + 1920
                            eng = 'dve'
                            S.op(eng, lambda e: e.tensor_tensor(emt[:], ett[:], mask[:, x0:x0 + 512], ALU.mult), reads=[ke, 'mask'], writes=[kem])
                            it['rhs'], it['kr'] = emt, kem
                        else:
                            it['rhs'], it['kr'] = ett, ke

                    def emit_pv(m, it):
                        b, j, half, g, i, ii, nt = it['b'], it['j'], it['half'], it['g'], it['i'], it['ii'], it['nt']
                        if kind == 'A':
                            head = 2 * j + half
                            vh, chunk = head, head
                        else:
                            head = j + 3 * half
                            vh, chunk = half, 10 + head
                        pot, kpo = po[it['grp'] % 2], 'po%d' % (it['grp'] % 2)
                        vt, rhs = it['vt'], it['rhs']
                        S.op('pe', lambda e: e.matmul(pot[0:65, :], vt[:, i, vh * 65:(vh + 1) * 65], rhs[:], start=(ii == 0), stop=(ii == nt - 1)),
                             reads=[it['kv'], it['kr']], writes=[kpo])
                        if ii == nt - 1:
                            if kind == 'A' and A_RECIP_ACT:
                                S.op('act', lambda e: e.activation(rec[64:65, :], pot[64:65, :], AF.Ln), reads=[kpo], writes=['rec'])
                                S.op('act', lambda e: e.activation(rec[64:65, :], rec[64:65, :], AF.Exp, scale=-1.0), reads=['rec'], writes=['rec'])
                            else:
                                S.op('dve', lambda e: e.reciprocal(rec[64:65, :], pot[64:65, :]), reads=[kpo], writes=['rec'])
                            S.op('dve', lambda e: e.tensor_copy(osb[:], pot[0:64, :]), reads=[kpo], writes=['osb'])
                            obt, kob = ob[it['grp'] % 2], 'ob%d' % (it['grp'] % 2)

                            def fin():
                                S.op('pe', lambda e: e.matmul(pb[0:64, :], ones_f[64:65, 0:64], rec[64:65, :], start=True, stop=True), reads=['rec', 'ones_f'], writes=['pb'])
                                S.op('dve', lambda e: e.tensor_tensor(obt[:], osb[:], pb[0:64, :], ALU.mult), reads=['osb', 'pb'], writes=[kob])
                                S.dma(oT[b, chunk * 64:(chunk + 1) * 64, g * 512:(g + 1) * 512], obt[:], 'ao%d' % (it['grp'] % 2), reads=[kob], writes=[])
                            part2.append((m + 9, fin))

                    N_ = len(items)
                    PFD = 24
                    nload = 0
                    for n in range(N_ + LA):
                        while nload < N_ and nload <= n + PFD:
                            issue_loads(items[nload])
                            nload += 1
                        if n < N_:
                            emit_qk(n, items[n])
                        m = n - LA
                        if m >= 0:
                            emit_pv(m, items[m])
                            while part2 and part2[0][0] <= m:
                                part2.pop(0)[1]()
                    while part2:
                        part2.pop(0)[1]()
                    S.barrier()

            def stage_F(l=l, xcur=xcur, xnext=xnext):
                with ExitStack() as st:
                    A = st.enter_context
                    zT = A(SB("f_zT", [33, S_], F32))
                    w1 = A(SB("f_w1", [33, 64], F32))
                    w2 = A(SB("f_w2", [64, 64], F32))
                    w3 = A(SB("f_w3", [64, 1024], F32))
                    b1 = A(SB("f_b1", [64, 1], F32))
                    b2 = A(SB("f_b2", [64, 1], F32))
                    fr = A(SB("f_fr", [64, 2], F32))
                    sc = A(SB("f_sc", [64, 8], F32))
                    h1 = A(SB("f_h1", [64, S_], F32))
                    h2 = A(SB("f_h2", [64, S_], F32))
                    sa = A(SB("f_sa", [64, 512], F32))
                    sb_ = A(SB("f_sb", [64, 512], F32))
                    sc_ = A(SB("f_sc2", [64, 512], F32))
                    decB = A(SB("f_decB", [128, 1024], F32))
                    tneg = A(SB("f_tneg", [128, NT], F32))
                    wins = [A(SB("f_win%d" % i, [128, 1024], F32)) for i in range(2)]
                    filt = A(SB("f_filt", [128, NT, 1024], F32))
                    absfs = [A(SB("f_abs%d" % i, [128, 1024], F32)) for i in range(2)]
                    rn = A(SB("f_rn", [128, 512], F32))
                    tmp = A(SB("f_tmp", [128, 512], F32))
                    tmp2 = A(SB("f_tmp2", [128, 512], F32))
                    Pm = A(SB("f_P", [128, NT, 512], BF16))
                    Qm = A(SB("f_Q", [128, NT, 512], BF16))
                    fwb = [A(SB("f_fw%d" % i, [128, NT, 128], BF16)) for i in range(3)]
                    ko = [A(SB("f_ko%d" % i, [128, 512], F32)) for i in range(2)]
                    pm = [A(PS("f_pm%d" % i, [128, 512], F32)) for i in range(4)]
                    pn = [A(PS("f_pn%d" % i, [128, 512], F32)) for i in range(2)]

                    S.dma(zT[:], c_zT[:, :], 'fc', writes=['zT'])
                    for rt_ in range(2):
                        S.dma(fwb[rt_][:], c_fw[rt_], 'ff%d' % rt_, writes=['fwb%d' % rt_])
                    S.dma(w1[:], hy_w1[l], 'fc', writes=['w1'])
                    S.dma(w2[:], hy_w2[l], 'fc', writes=['w2'])
                    S.dma(w3[:], hy_w3[l], 'fc', writes=['w3'])
                    S.dma(b1[:], hy_b1[l], 'fc', writes=['b1'])
                    S.dma(b2[:], hy_b2[l], 'fc', writes=['b2'])
                    S.dma(fr[:], hy_fr[l], 'fc', writes=['fr'])
                    S.dma(decB[:], hy_dec[l:l + 1, :].broadcast_to([128, 1024]), 'fc', writes=['decB'])
                    S.dma(tneg[:], c_tneg[:, :], 'fc', writes=['tneg'])
                    for li, bb in ((0, b1), (1, b2)):
                        o = 4 * li
                        S.op('dve', lambda e, li=li, bb=bb, o=o: e.tensor_tensor(sc[:, o + 3:o + 4], fr[:, li:li + 1], bb[:, 0:1], ALU.mult), reads=['fr', 'b1', 'b2'], writes=['sc'])
                        S.op('dve', lambda e, li=li, o=o: e.tensor_copy(sc[:, o + 2:o + 3], fr[:, li:li + 1]), reads=['fr', 'sc'], writes=['sc'])
                        S.op('dve', lambda e, o=o: e.tensor_scalar(sc[:, o:o + 2], sc[:, o + 2:o + 4], 0.25, None, ALU.mult), reads=['sc'], writes=['sc'])

                    def sin_layer(li, wmat, kw, src, ksrc, dst, kdst, K):
                        o = 4 * li
                        for n in range(4):
                            p = pm[n % 4]
                            kp = 'pm%d' % (n % 4)
                            sl = slice(n * 512, (n + 1) * 512)
                            S.op('pe', lambda e, p=p, sl=sl: e.matmul(p[0:64, :], wmat[0:K, :], src[0:K, sl], start=True, stop=True), reads=[kw, ksrc], writes=[kp])
                            S.op('act', lambda e, p=p: e.activation(sa[:], p[0:64, :], AF.Sin, scale=sc[:, o:o + 1], bias=sc[:, o + 1:o + 2]), reads=[kp, 'sc'], writes=['sa'])
                            S.op('act', lambda e, p=p: e.activation(sb_[:], p[0:64, :], AF.Abs, scale=sc[:, o + 2:o + 3], bias=sc[:, o + 3:o + 4]), reads=[kp, 'sc'], writes=['sb'])
                            S.op('dve', lambda e: e.tensor_scalar(sb_[:], sb_[:], -0.25, float(math.pi / 2), ALU.mult, ALU.add), reads=['sb'], writes=['sb'])
                            S.op('act', lambda e: e.activation(sb_[:], sb_[:], AF.Sin), reads=['sb'], writes=['sb'])
                            S.op('dve', lambda e: e.tensor_tensor(sc_[:], sa[:], sa[:], ALU.mult), reads=['sa'], writes=['sc2'])
                            S.op('dve', lambda e: e.tensor_scalar(sc_[:], sc_[:], -8.0, 4.0, ALU.mult, ALU.add), reads=['sc2'], writes=['sc2'])
                            S.op('dve', lambda e: e.tensor_tensor(sa[:], sa[:], sb_[:], ALU.mult), reads=['sa', 'sb'], writes=['sa'])
                            S.op('dve', lambda e, sl=sl: e.tensor_tensor(dst[:, sl], sa[:], sc_[:], ALU.mult), reads=['sa', 'sc2'], writes=[kdst])

                    sin_layer(0, w1, 'w1', zT, 'zT', h1, 'h1', 33)
                    sin_layer(1, w2, 'w2', h1, 'h1', h2, 'h2', 64)
                    def f_win(tt):
                        S.op('act', lambda e: e.activation(wins[tt % 2][:], decB[:], AF.Exp, scale=tneg[:, tt:tt + 1]), reads=['decB', 'tneg'], writes=['win%d' % (tt % 2)])

                    def f_h3(tt):
                        for n in range(2):
                            p = pm[(tt * 2 + n) % 4]
                            kp = 'pm%d' % ((tt * 2 + n) % 4)
                            sl = slice(n * 512, (n + 1) * 512)
                            S.op('pe', lambda e, p=p, sl=sl: e.matmul(p[:, :], h2[:, tt * 128:(tt + 1) * 128], w3[:, sl], start=True, stop=True), reads=['h2', 'w3'], writes=[kp])

                    f_win(0)
                    f_h3(0)
                    for tt in range(NT):
                        if tt + 1 < NT:
                            f_win(tt + 1)
                            f_h3(tt + 1)
                        win = wins[tt % 2]
                        absf = absfs[tt % 2]
                        for n in range(2):
                            p = pm[(tt * 2 + n) % 4]
                            kp = 'pm%d' % ((tt * 2 + n) % 4)
                            sl = slice(n * 512, (n + 1) * 512)
                            S.op('dve', lambda e, p=p, tt=tt, sl=sl, win=win: e.tensor_tensor(filt[:, tt, sl], p[:, :], win[:, sl], ALU.mult), reads=[kp, 'win%d' % (tt % 2)], writes=['filt%d' % tt])
                        if tt == 0:
                            fv0 = filt[0:1, 0, :].rearrange("p (o f c) -> p o f c", o=2, f=2)
                            S.op('dve', lambda e, fv0=fv0: e.memset(fv0[:, :, 1, :], 0.0), reads=['filt0'], writes=['filt0'])
                        S.op('act', lambda e, tt=tt, absf=absf: e.activation(absf[:], filt[:, tt, :], AF.Abs), reads=['filt%d' % tt], writes=['absf%d' % (tt % 2)])
                        for n in range(2):
                            S.op('pe', lambda e, n=n, tt=tt, absf=absf: e.matmul(pn[n][:, :], ones_f[:, :], absf[:, n * 512:(n + 1) * 512], start=(tt == 0), stop=(tt == NT - 1)),
                                 reads=['absf%d' % (tt % 2), 'ones_f'], writes=['pn%d' % n])
                    for o in range(2):
                        S.op('act', lambda e, o=o: e.activation(tmp[:, 0:256], pn[o][:, 0:256], AF.Copy), reads=['pn%d' % o], writes=['tmp'])
                        S.op('dve', lambda e, o=o: e.tensor_tensor(rn[:, o * 256:(o + 1) * 256], tmp[:, 0:256], pn[o][:, 256:512], ALU.add), reads=['tmp', 'pn%d' % o], writes=['rn'])
                    S.op('dve', lambda e: e.reciprocal(rn[:], rn[:]), reads=['rn'], writes=['rn'])
                    rn3 = rn[:].rearrange("p (o c) -> p o c", o=2)
                    for tt in range(NT):
                        fv = filt[:, tt, :].rearrange("p (o f c) -> p o f c", o=2, f=2)
                        ta_ = tmp[:].rearrange("p (o c) -> p o c", o=2)
                        tb_ = tmp2[:].rearrange("p (o c) -> p o c", o=2)
                        S.op('pool', lambda e, fv=fv, ta_=ta_: e.tensor_tensor(ta_, fv[:, :, 0, :], fv[:, :, 1, :], ALU.add), reads=['filt%d' % tt], writes=['tmp'])
                        S.op('pool', lambda e, fv=fv, tb_=tb_: e.tensor_tensor(tb_, fv[:, :, 0, :], fv[:, :, 1, :], ALU.subtract), reads=['filt%d' % tt], writes=['tmp2'])
                        S.op('dve', lambda e, tt=tt, ta_=ta_: e.tensor_tensor(Pm[:, tt, :].rearrange("p (o c) -> p o c", o=2), ta_, rn3, ALU.mult), reads=['tmp', 'rn'], writes=['Pm'])
                        S.op('dve', lambda e, tt=tt, tb_=tb_: e.tensor_tensor(Qm[:, tt, :].rearrange("p (o c) -> p o c", o=2), tb_, rn3, ALU.mult), reads=['tmp2', 'rn'], writes=['Qm'])
                    for rt in range(32):
                        if rt + 2 < 32:
                            S.dma(fwb[(rt + 2) % 3][:], c_fw[rt + 2], 'ff%d' % ((rt + 2) % 3), writes=['fwb%d' % ((rt + 2) % 3)])
                        fb = fwb[rt % 3]
                        kfb = 'fwb%d' % (rt % 3)
                        p = pm[rt % 4]
                        kp = 'pm%d' % (rt % 4)
                        src, ks = (Pm, 'Pm') if rt < 16 else (Qm, 'Qm')
                        for tt in range(NT):
                            S.op('pe', lambda e, p=p, fb=fb, tt=tt, src=src: e.matmul(p[:, :], fb[:, tt, :], src[:, tt, :], start=(tt == 0), stop=(tt == NT - 1)),
                                 reads=[kfb, ks], writes=[kp])
                        kot = ko[rt % 2]
                        kko = 'ko%d' % (rt % 2)
                        S.op('act', lambda e, p=p, kot=kot: e.activation(kot[:], p[:, :], AF.Copy), reads=[kp], writes=[kko])
                        S.dma(kfd[rt], kot[:], 'fk%d' % (rt % 2), reads=[kko], writes=[])
                    S.barrier()

            def stage_H(l=l, xcur=xcur, xnext=xnext):
                with ExitStack() as st:
                    A = st.enter_context
                    NC_ = NB * 256
                    V = A(SB("h_V", [128, NT, NC_], BF16))
                    Y = A(SB("h_Y", [128, 32, NC_], BF16))
                    cw = A(SB("h_cw", [128, 6, 3], F32))
                    cb = A(SB("h_cb", [128, 6], F32))
                    dB = A(SB("h_dB", [128, 512], F32))
                    raw = [A(SB("h_raw%d" % i, [128, S_], BF16)) for i in range(3)]
                    u0s = [A(SB("h_u0_%d" % i, [128, S_], F32)) for i in range(2)]
                    ubs = [A(SB("h_ub_%d" % i, [128, S_], BF16)) for i in range(2)]
                    xs = [A(SB("h_xs%d" % i, [128, NT, 128], BF16)) for i in range(2)]
                    fwb = [A(SB("h_fw%d" % i, [128, NT, 128], BF16)) for i in range(3)]
                    ivb = [A(SB("h_iv%d" % i, [128, 32, 128], BF16)) for i in range(2)]
                    kre = [A(SB("h_kre%d" % i, [128, 256], F32)) for i in range(2)]
                    kim = [A(SB("h_kim%d" % i, [128, 256], F32)) for i in range(2)]
                    ure = A(SB("h_ure", [128, NC_], F32))
                    uim = A(SB("h_uim", [128, NC_], F32))
                    ta = A(SB("h_ta", [128, NC_], F32))
                    tb = A(SB("h_tb", [128, NC_], F32))
                    xg = [A(SB("h_xg%d" % i, [128, NC_], BF16)) for i in range(2)]
                    obts = [A(SB("h_obt%d" % i, [128, NC_], BF16)) for i in range(2)]
                    obT = [A(SB("h_obT%d" % i, [128, NB * 2, 128], BF16)) for i in range(2)]
                    pp = A(PS("h_pp", [128, 8, 512], F32))

                    S.dma(cw[:], hy_cw[l], 'hc', writes=['cw'])
                    S.dma(cb[:], hy_cb[l], 'hc', writes=['cb'])
                    S.dma(dB[:], hy_d[l:l + 1, :].broadcast_to([128, 512]), 'hc', writes=['dB'])
                    nr = 0
                    nx = 0
                    chains = [(b, ct) for b in range(NB) for ct in range(6)]

                    def load_raw(i):
                        S.dma(raw[i % 3][:], hyraw[chains[i][0], chains[i][1]], 'hr%d' % (i % 3), writes=['raw%d' % (i % 3)])

                    load_raw(0)
                    load_raw(1)
                    def S12(ci):
                        b, ct = chains[ci]
                        if ci + 2 < len(chains):
                            load_raw(ci + 2)
                        r = raw[ci % 3]
                        kr = 'raw%d' % (ci % 3)
                        u0, ub = u0s[ci % 2], ubs[ci % 2]
                        ku0, kub = 'u0_%d' % (ci % 2), 'ub_%d' % (ci % 2)
                        S.op('act', lambda e: e.activation(u0[:], r[:], AF.Identity, scale=cw[:, ct, 1:2], bias=cb[:, ct:ct + 1]), reads=[kr, 'cw', 'cb'], writes=[ku0])
                        S.op('dve', lambda e: e.scalar_tensor_tensor(u0[:, 1:S_], r[:, 0:S_ - 1], cw[:, ct, 0:1], u0[:, 1:S_], ALU.mult, ALU.add),
                             reads=[kr, ku0, 'cw'], writes=[ku0])
                        S.op('dve', lambda e: e.scalar_tensor_tensor(ub[:, 0:S_ - 1], r[:, 1:S_], cw[:, ct, 2:3], u0[:, 0:S_ - 1], ALU.mult, ALU.add),
                             reads=[kr, ku0, 'cw'], writes=[kub])

                    def S3(ci):
                        b, ct = chains[ci]
                        u0, ub = u0s[ci % 2], ubs[ci % 2]
                        ku0, kub = 'u0_%d' % (ci % 2), 'ub_%d' % (ci % 2)
                        S.op('act', lambda e: e.activation(ub[:, S_ - 1:S_], u0[:, S_ - 1:S_], AF.Copy), reads=[ku0, kub], writes=[kub])
                        bank = ci % 2
                        pt = pp[:, 4 * bank:4 * bank + 2, :].rearrange("p a n -> p (a n)").bitcast(BF16).rearrange("p (t n) -> p t n", n=128)[:, 0:NT, :]
                        kpt = 'ppt%d' % bank
                        for tt in range(NT):
                            S.op('pe', lambda e, tt=tt: e.transpose(pt[:, tt, :], ub[:, tt * 128:(tt + 1) * 128], ident[:]), reads=[kub, 'ident'], writes=[kpt])
                        col = b * 256 + (ct % 2) * 128
                        if ct < 2:
                            S.op('act', lambda e: e.activation(V[:, :, col:col + 128], pt, AF.Copy), reads=[kpt], writes=['V'])
                        else:
                            nx = xcnt[0]
                            xcnt[0] += 1
                            x_ = xs[nx % 2]
                            kx = 'xs%d' % (nx % 2)
                            S.op('act', lambda e: e.activation(x_[:], pt, AF.Copy), reads=[kpt], writes=[kx])
                            dstt = (x1tm if ct < 4 else x2tm)
                            S.dma(dstt[:, :, col:col + 128].rearrange("t p c -> p t c"), x_[:], 'hx%d' % (nx % 2), reads=[kx], writes=[])

                    xcnt = [0]
                    S12(0)
                    for ci in range(len(chains)):
                        if ci + 1 < len(chains):
                            S12(ci + 1)
                        S3(ci)
                    S.barrier()
                    nfw = 0
                    niv = 0
                    pend_h = []
                    nk = 0
                    ngx = 0
                    nob = 0
                    GW = min(512, NC_)
                    ngrp = NC_ // GW
                    for order in range(2):
                        for a in range(16):
                            k1, k2 = kre[nk % 2], kim[nk % 2]
                            kk = 'kf%d' % (nk % 2)
                            S.dma(k1[:], kfd[a, :, order * 256:(order + 1) * 256], 'hk%d' % (nk % 2), writes=[kk])
                            S.dma(k2[:], kfd[16 + a, :, order * 256:(order + 1) * 256], 'hk%d' % (nk % 2), writes=[kk])
                            nk += 1
                            for part in range(2):
                                rt = a + 16 * part
                                fb = fwb[nfw % 3]
                                kfb = 'fwb%d' % (nfw % 3)
                                S.dma(fb[:], c_fw[rt], 'hf%d' % (nfw % 3), writes=[kfb])
                                nfw += 1
                                for tt in range(NT):
                                    for n in range(ngrp):
                                        bk = (a % 2) * 4 + part * 2 + n
                                        S.op('pe', lambda e, bk=bk, fb=fb, tt=tt, n=n: e.matmul(pp[:, bk, 0:GW], fb[:, tt, :], V[:, tt, n * GW:(n + 1) * GW],
                                                                                            start=(tt == 0), stop=(tt == NT - 1)),
                                             reads=[kfb, 'V'], writes=['pp%d' % bk])
                            bre = [(a % 2) * 4 + n for n in range(ngrp)]
                            bim = [(a % 2) * 4 + 2 + n for n in range(ngrp)]
                            for n in range(ngrp):
                                sl = slice(n * GW, (n + 1) * GW)
                                S.op('act', lambda e, n=n, sl=sl, bre=bre: e.activation(ure[:, sl], pp[:, bre[n], 0:GW], AF.Copy), reads=['pp%d' % bre[n]], writes=['ure'])
                                S.op('act', lambda e, n=n, sl=sl, bim=bim: e.activation(uim[:, sl], pp[:, bim[n], 0:GW], AF.Copy), reads=['pp%d' % bim[n]], writes=['uim'])
                            nb_ = NC_ // 256
                            k1b = k1[:].unsqueeze(1).broadcast_to([128, nb_, 256])
                            k2b = k2[:].unsqueeze(1).broadcast_to([128, nb_, 256])
                            v3 = lambda t_: t_.rearrange("p (b c) -> p b c", c=256)
                            S.op('pool', lambda e, k1b=k1b: e.tensor_tensor(v3(ta[:]), v3(ure[:]), k1b, ALU.mult), reads=['ure', kk], writes=['ta'])
                            S.op('dve', lambda e, k2b=k2b: e.tensor_tensor(v3(tb[:]), v3(uim[:]), k2b, ALU.mult), reads=['uim', kk], writes=['tb'])
                            S.op('pool', lambda e, a=a: e.tensor_tensor(Y[:, a, :], ta[:], tb[:], ALU.subtract), reads=['ta', 'tb'], writes=['Y'])
                            S.op('dve', lambda e, k2b=k2b: e.tensor_tensor(v3(tb[:]), v3(ure[:]), k2b, ALU.mult), reads=['ure', kk, 'tb'], writes=['tb'])
                            S.op('pool', lambda e, k1b=k1b: e.tensor_tensor(v3(ta[:]), v3(uim[:]), k1b, ALU.mult), reads=['uim', kk, 'ta'], writes=['ta'])
                            S.op('dve', lambda e, a=a: e.tensor_tensor(Y[:, 16 + a, :], ta[:], tb[:], ALU.add), reads=['ta', 'tb'], writes=['Y'])
                        for tt in range(NT):
                            ib = ivb[niv % 2]
                            kib = 'ivb%d' % (niv % 2)
                            S.dma(ib[:], c_inv[tt], 'hi%d' % (niv % 2), writes=[kib])
                            niv += 1
                            xg_ = xg[ngx % 2]
                            kxg = 'xg%d' % (ngx % 2)
                            S.dma(xg_[:], (x1tm if order == 0 else x2tm)[tt], 'hg%d' % (ngx % 2), writes=[kxg])
                            ngx += 1
                            bks = [(tt % 2) * ngrp + n for n in range(ngrp)]
                            for rt in range(32):
                                for n in range(ngrp):
                                    S.op('pe', lambda e, ib=ib, rt=rt, n=n, bk=bks[n]: e.matmul(pp[:, bk, 0:GW], ib[:, rt, :], Y[:, rt, n * GW:(n + 1) * GW],
                                                                                               start=(rt == 0), stop=(rt == 31)),
                                         reads=[kib, 'Y'], writes=['pp%d' % bks[n]])
                            while pend_h:
                                pend_h.pop(0)()
                            nb_ = NC_ // 256
                            dv = dB[:, order * 256:(order + 1) * 256].unsqueeze(1).broadcast_to([128, nb_, 256])
                            v3 = lambda t_: t_.rearrange("p (b c) -> p b c", c=256)
                            S.op('pool', lambda e, tt=tt, dv=dv: e.tensor_tensor(v3(ta[:]), v3(V[:, tt, :]), dv, ALU.mult), reads=['V', 'dB', 'ta'], writes=['ta'])
                            for n in range(ngrp):
                                sl = slice(n * GW, (n + 1) * GW)
                                S.op('dve', lambda e, sl=sl, bk=bks[n]: e.tensor_tensor(ta[:, sl], ta[:, sl], pp[:, bk, 0:GW], ALU.add), reads=['ta', 'pp%d' % bks[n]], writes=['ta'])
                            if order == 0:
                                S.op('pool', lambda e, tt=tt, xg_=xg_: e.tensor_tensor(V[:, tt, :], ta[:], xg_[:], ALU.mult), reads=['ta', kxg, 'V'], writes=['V'])
                            else:
                                obt = obts[nob % 2]
                                kobt = 'obt%d' % (nob % 2)
                                S.op('pool', lambda e, xg_=xg_, obt=obt: e.tensor_tensor(obt[:], ta[:], xg_[:], ALU.mult), reads=['ta', kxg], writes=[kobt])

                                def fin_tt(tt=tt, obt=obt, kobt=kobt, nob=nob):
                                    pt = pp[:, 6:8, :].rearrange("p a n -> p (a n)").bitcast(BF16).rearrange("p (t n) -> p t n", n=128)
                                    for q in range(NB * 2):
                                        S.op('pe', lambda e, q=q: e.transpose(pt[:, q, :], obt[:, q * 128:(q + 1) * 128], ident[:]), reads=[kobt, 'ident'], writes=['pp6'])
                                    oo = obT[nob % 2]
                                    koo = 'obT%d' % (nob % 2)
                                    S.op('act', lambda e: e.activation(oo[:], pt[:, 0:NB * 2, :], AF.Copy), reads=['pp6'], writes=[koo])
                                    for b in range(NB):
                                        S.dma(oT[b, 384:640, tt * 128:(tt + 1) * 128].rearrange("(h p) t -> p h t", p=128), oo[:, 2 * b:2 * b + 2, :], 'ho%d' % (nob % 2),
                                              reads=[koo], writes=[])
                                pend_h.append(fin_tt)
                                nob += 1
                    while pend_h:
                        pend_h.pop(0)()
                    S.barrier()

            def stage_O(l=l, xcur=xcur, xnext=xnext):
                with ExitStack() as st:
                    A = st.enter_context
                    wo = A(SB("o_w", [128, 8, D_], BF16))
                    gB = A(SB("o_gB", [128, D_], F32))
                    oTs = [A(SB("o_oT%d" % i, [128, 8, S_], BF16)) for i in range(2)]
                    NSL = 4
                    ctx = dict(xt=[A(SB("o_xt%d" % i, [128, D_], F32)) for i in range(NSL)],
                               ssq=[A(SB("o_ssq%d" % i, [128, 1], F32)) for i in range(NSL)],
                               yn=[A(SB("o_yn%d" % i, [128, D_], F32)) for i in range(NSL)])
                    pm = [A(PS("o_pm%d" % i, [128, D_], F32)) for i in range(NSL)]
                    S.dma(wo[:], wbf_out[l].rearrange("(c p) n -> p c n", p=128), 'ow', writes=['wo'])
                    S.dma(gB[:], g_post[l:l + 1, :].broadcast_to([128, D_]), 'oc', writes=['gB'])
                    n = 0
                    pend = []
                    TL = [(b, tt) for b in range(NB) for tt in range(NT)]
                    PF = 2

                    def load_oT(b):
                        for c in range(8):
                            S.dma(oTs[b % 2][:, c, :], oT[b, c * 128:(c + 1) * 128, :], 'oo%d' % (b % 2), writes=['oTs%d' % (b % 2)])

                    load_oT(0)
                    for i_ in range(min(PF, len(TL))):
                        post_load(ctx, xcur[TL[i_][0], TL[i_][1] * 128:(TL[i_][1] + 1) * 128, :], i_ % NSL, 'o_')
                    for n, (b, tt) in enumerate(TL):
                        ot = oTs[b % 2]
                        ko = 'oTs%d' % (b % 2)
                        if tt == 4 and b + 1 < NB:
                            load_oT(b + 1)
                        sl = n % NSL
                        p = pm[sl]
                        kp = 'opm%d' % sl
                        for nn in range(2):
                            for c in range(8):
                                S.op('pe', lambda e, p=p, nn=nn, c=c, ot=ot, tt=tt: e.matmul(p[:, nn * 512:(nn + 1) * 512], ot[:, c, tt * 128:(tt + 1) * 128],
                                                                                              wo[:, c, nn * 512:(nn + 1) * 512], start=(c == 0), stop=(c == 7)),
                                     reads=[ko, 'wo'], writes=[kp])
                        while pend:
                            pend.pop(0)()
                        if n + PF < len(TL):
                            b2, t2 = TL[n + PF]
                            post_load(ctx, xcur[b2, t2 * 128:(t2 + 1) * 128, :], (n + PF) % NSL, 'o_')
                        post_part1(ctx, p[:, :], [kp], None, sl, 'o_')
                        pend.append(lambda p=p, kp=kp, sl=sl, b=b, tt=tt: post_part2(ctx, p[:, :], [kp], gB, xmid[b, tt * 128:(tt + 1) * 128, :], sl, 'o_'))
                    while pend:
                        pend.pop(0)()
                    S.barrier()

            def stage_M(l=l, xcur=xcur, xnext=xnext):
                with ExitStack() as st:
                    A = st.enter_context
                    big = A(SB("m_big", [128, NF * D_], BF16))
                    hT = big[:, 0:8 * S_].rearrange("p (c t) -> p c t", c=8)
                    wd = big[:, :].rearrange("p (f n) -> p f n", f=NF)
                    actT = A(SB("m_actT", [128, NF, S_], BF16))
                    gcol = A(SB("m_gcol", [128, 8], F32))
                    gB = A(SB("m_gB", [128, D_], F32))
                    cw = A(SB("m_cw", [128, NF, 3], F32))
                    cb = A(SB("m_cb", [128, NF], F32))
                    wg = [A(SB("m_wg%d" % i, [128, 8, 128], BF16)) for i in range(2)]
                    wu = [A(SB("m_wu%d" % i, [128, 8, 128], BF16)) for i in range(2)]
                    u0 = A(SB("m_u0", [128, S_], F32))
                    gl = A(SB("m_gl", [128, S_], F32))
                    ctx = dict(xt=[A(SB("m_xt%d" % i, [128, D_], F32)) for i in range(4)],
                               ssq=[A(SB("m_ssq%d" % i, [128, 1], F32)) for i in range(4)],
                               xn=[A(SB("m_xn%d" % i, [128, D_], BF16)) for i in range(4)],
                               yn=[A(SB("m_yn%d" % i, [128, D_], F32)) for i in range(4)])
                    pp = A(PS("m_pp", [128, 8 * 512], F32))
                    pTb = [pp[:, o_:o_ + 512].bitcast(BF16).rearrange("p (c n) -> p c n", n=128) for o_ in (0, 2048)]
                    kpTb = [['mpg', 'mpd0'], ['mpu', 'mpd2']]
                    S.dma(gcol[:], g_fpre_col[l], 'mc', writes=['gcol'])
                    S.dma(gB[:], g_fpost[l:l + 1, :].broadcast_to([128, D_]), 'mc', writes=['gB'])
                    S.dma(cw[:], f_cw[l], 'mc', writes=['cw'])
                    S.dma(cb[:], f_cb[l], 'mc', writes=['cb'])
                    nw = 0
                    nd = 0
                    pend = []
                    for b in range(NB):
                        jobs = lambda kb: [(xmid[b, (2 * kb + i_) * 128:(2 * kb + i_ + 1) * 128, :], (kb % 2) * 2 + i_) for i_ in range(2)]
                        norm_partA(ctx, jobs(0), 'm_')
                        for kb in range(8):
                            if kb + 1 < 8:
                                norm_partA(ctx, jobs(kb + 1), 'm_')
                            for i_ in range(2):
                                tt = 2 * kb + i_
                                norm_partB(ctx, (kb % 2) * 2 + i_, gcol, hT[:, :, tt * 128:(tt + 1) * 128], 'big', pTb[tt % 2], kpTb[tt % 2], 'm_')
                        for f in range(NF):
                            g_, u_ = wg[nw % 2], wu[nw % 2]
                            kw = 'wgu%d' % (nw % 2)
                            S.dma(g_[:], wbf_gate[l, :, f * 128:(f + 1) * 128].rearrange("(c p) n -> p c n", p=128), 'mw%d' % (nw % 2), writes=[kw])
                            S.dma(u_[:], wbf_up[l, :, f * 128:(f + 1) * 128].rearrange("(c p) n -> p c n", p=128), 'mw%d' % (nw % 2), writes=[kw])
                            nw += 1
                            for (wt, off, kps) in ((g_, 0, ['mpg', 'mpd0', 'mpd1']), (u_, 2048, ['mpu', 'mpd2', 'mpd3'])):
                                for n in range(4):
                                    for c in range(8):
                                        S.op('pe', lambda e, wt=wt, off=off, n=n, c=c: e.matmul(pp[:, off + n * 512:off + (n + 1) * 512], wt[:, c, :], hT[:, c, n * 512:(n + 1) * 512],
                                                                                                 start=(c == 0), stop=(c == 7)),
                                             reads=[kw, 'big'], writes=kps)
                            gp = pp[:, 0:S_]
                            up = pp[:, S_:2 * S_]
                            S.op('act', lambda e, f=f, gp=gp: e.activation(u0[:], gp, AF.Identity, scale=cw[:, f, 1:2], bias=cb[:, f:f + 1]), reads=['mpg', 'cw', 'cb'], writes=['u0'])
                            S.op('dve', lambda e, f=f, gp=gp: e.scalar_tensor_tensor(u0[:, 1:S_], gp[:, 0:S_ - 1], cw[:, f, 0:1], u0[:, 1:S_], ALU.mult, ALU.add),
                                 reads=['mpg', 'u0', 'cw'], writes=['u0'])
                            S.op('dve', lambda e, f=f, gp=gp: e.scalar_tensor_tensor(u0[:, 0:S_ - 1], gp[:, 1:S_], cw[:, f, 2:3], u0[:, 0:S_ - 1], ALU.mult, ALU.add),
                                 reads=['mpg', 'u0', 'cw'], writes=['u0'])
                            S.op('act', lambda e: e.activation(gl[:], u0[:], AF.Gelu_apprx_tanh), reads=['u0'], writes=['gl'])
                            S.op('dve', lambda e, f=f, up=up: e.tensor_tensor(actT[:, f, :], gl[:], up, ALU.mult), reads=['gl', 'mpu'], writes=['actT'])
                        for f0 in range(0, NF, 2):
                            S.dma(wd[:, f0:f0 + 2, :], wbf_down[l, f0 * 128:(f0 + 2) * 128, :].rearrange("(f p) n -> p f n", p=128), 'md', writes=['big'])
                        for tt in range(NT):
                            sl = nd % 4
                            off = sl * 1024
                            kp = ['mpd%d' % sl, 'mpg' if sl < 2 else 'mpu']
                            for nn in range(2):
                                for f in range(NF):
                                    S.op('pe', lambda e, off=off, nn=nn, f=f, tt=tt: e.matmul(pp[:, off + nn * 512:off + (nn + 1) * 512], actT[:, f, tt * 128:(tt + 1) * 128],
                                                                                               wd[:, f, nn * 512:(nn + 1) * 512], start=(f == 0), stop=(f == NF - 1)),
                                         reads=['actT', 'big'], writes=kp)
                            while pend:
                                pend.pop(0)()
                            if tt == 0:
                                for i_ in range(2):
                                    post_load(ctx, xmid[b, i_ * 128:(i_ + 1) * 128, :], (nd + i_) % 4, 'm_')
                            if tt + 2 < NT:
                                post_load(ctx, xmid[b, (tt + 2) * 128:(tt + 3) * 128, :], (nd + 2) % 4, 'm_')
                            post_part1(ctx, pp[:, off:off + 1024], ['mpd%d' % sl], None, sl, 'm_')
                            pend.append(lambda off=off, sl=sl, b=b, tt=tt: post_part2(ctx, pp[:, off:off + 1024], ['mpd%d' % sl], gB, xnext[b, tt * 128:(tt + 1) * 128, :], sl, 'm_'))
                            nd += 1
                        while pend:
                            pend.pop(0)()
                    S.barrier()
            if 'P' in stages:
                stage_P()
            if 'A' in stages:
                attention('A')
            if 'C' in stages:
                attention('C')
            if 'F' in stages:
                stage_F()
            if 'H' in stages:
                stage_H()
            if 'O' in stages:
                stage_O()
            if 'M' in stages:
                stage_M()
    S.emit()
    return nc


def make_inputs(inputs, NL=2):
    f = lambda a: np.ascontiguousarray(np.asarray(a, dtype=np.float32))
    c = host_consts()
    m = {}
    for k in ("w_in", "w_out", "ffn_w_gate", "ffn_w_up", "ffn_w_down", "hy_w1", "hy_w2", "hy_w3"):
        m[k] = f(inputs[k])[:NL]
    m["g_mix_pre_col"] = f(f(inputs["g_mix_pre"])[:NL].reshape(NL, 8, 128).transpose(0, 2, 1))
    m["g_ffn_pre_col"] = f(f(inputs["g_ffn_pre"])[:NL].reshape(NL, 8, 128).transpose(0, 2, 1))
    m["g_mix_post"] = f(inputs["g_mix_post"])[:NL]
    m["g_ffn_post"] = f(inputs["g_ffn_post"])[:NL]
    gq, gk = f(inputs["g_q"])[:NL], f(inputs["g_k"])[:NL]
    m["g_qk"] = f(np.concatenate([np.tile(gq, (1, 6)), np.tile(gk, (1, 2))], axis=1))
    m["hy_cw"] = f(f(inputs["hy_conv_w"])[:NL].reshape(NL, 3, 6, 128).transpose(0, 3, 2, 1))
    m["hy_cb"] = f(f(inputs["hy_conv_b"])[:NL].reshape(NL, 6, 128).transpose(0, 2, 1))
    m["hy_b1c"] = f(f(inputs["hy_b1"])[:NL].reshape(NL, 64, 1))
    m["hy_b2c"] = f(f(inputs["hy_b2"])[:NL].reshape(NL, 64, 1))
    m["hy_freqc"] = f(f(inputs["hy_freq"])[:NL].transpose(0, 2, 1))
    m["hy_decay"] = f(f(inputs["hy_decay"])[:NL].reshape(NL, 1024))
    m["hy_d"] = f(f(inputs["hy_d"])[:NL].reshape(NL, 512))
    m["ffn_cw"] = f(f(inputs["ffn_conv_w"])[:NL].reshape(NL, 3, NF, 128).transpose(0, 3, 2, 1))
    m["ffn_cb"] = f(f(inputs["ffn_conv_b"])[:NL].reshape(NL, NF, 128).transpose(0, 2, 1))
    for k in ("ident", "ropeA_c", "ropeA_s", "ropeC_c", "ropeC_s", "maskW", "zT", "tneg", "fw", "inv", "ones_bf", "ones_f"):
        m[k] = c[k]
    return m


_PROG = {}


def kernel(**inputs):
    NB = 4
    x = np.ascontiguousarray(np.asarray(inputs["x"], dtype=np.float32))
    shared = make_inputs(inputs)
    if 'nc' not in _PROG:
        _PROG['nc'] = build_program(NB=NB, NL=2)
    nc = _PROG['nc']
    in_maps = []
    for c in range(NCORES):
        d = dict(shared)
        d["x"] = np.ascontiguousarray(x[c * NB:(c + 1) * NB])
        in_maps.append(d)
    res = run_bass_kernel_spmd(nc, in_maps, core_ids=list(range(NCORES)))
    return np.concatenate([np.asarray(r["out"], dtype=np.float32) for r in res.results], axis=0)
```
